# Optimizing a Trainium2 kernel written in Bass

```python
import math
import jax, jax.numpy as jnp
from jax import lax
import numpy as np

D_MODEL = 1024
BATCH = 2
SEQ = 8192
DEPTH = 1

PLE_DIM = 256
D_FF = 2816
MLA_HEADS = 8
MLA_NOPE = 64
MLA_ROPE = 32
MLA_QK = MLA_NOPE + MLA_ROPE
MLA_V = 64
MLA_Q_RANK = 256
MLA_KV_RANK = 128
ROPE_THETA = 10000.0
DIFF_HEADS = 4
DIFF_QK = 64
DIFF_V = 2 * DIFF_QK
MIX_WIDTH = MLA_HEADS * MLA_V + DIFF_HEADS * DIFF_V
IN_SIZES = (MLA_Q_RANK, MLA_KV_RANK, MLA_ROPE,
            DIFF_HEADS * 2 * DIFF_QK, DIFF_HEADS * 2 * DIFF_QK, DIFF_HEADS * DIFF_V)
N_IN = sum(IN_SIZES)
IN_OFFSETS = [int(v) for v in np.cumsum(IN_SIZES)[:-1]]
Q_BLOCK = 128
EPS = 1e-6

kernel_name = "hybrid_mla_diffattn_macaron_encoder"


def rms_norm(x, g):
    xf = x.astype(jnp.float32)
    y = xf * lax.rsqrt(jnp.mean(xf * xf, axis=-1, keepdims=True) + EPS)
    return (y * g.astype(jnp.float32)).astype(x.dtype)


def swiglu(u, w_gate, w_up, w_down):
    return (jax.nn.silu(u @ w_gate) * (u @ w_up)) @ w_down


def rope_tables(S, dtype):
    pos = jnp.arange(S, dtype=jnp.float32)
    inv = ROPE_THETA ** (-jnp.arange(0, MLA_ROPE, 2, dtype=jnp.float32) / MLA_ROPE)
    ang = pos[:, None] * inv[None, :]
    ang = jnp.concatenate([ang, ang], axis=-1)
    return jnp.cos(ang).astype(dtype), jnp.sin(ang).astype(dtype)


def apply_rope(x, cos, sin):
    half = x.shape[-1] // 2
    rot = jnp.concatenate([-x[..., half:], x[..., :half]], axis=-1)
    return x * cos + rot * sin


def sweep_query_blocks(block_fn, qs):
    B, H, S, _ = qs[0].shape
    nb = S // Q_BLOCK
    blocks = tuple(q.reshape(B, H, nb, Q_BLOCK, q.shape[-1]).transpose(2, 0, 1, 3, 4) for q in qs)
    starts = jnp.arange(nb, dtype=jnp.int32) * Q_BLOCK
    out = lax.map(lambda a: block_fn(a[0], *a[1]), (starts, blocks))
    dv = out.shape[-1]
    return out.transpose(1, 0, 3, 2, 4).reshape(B, S, H * dv)


def mla_mixer(q_lat, kv_lat, k_rope, g_q_lat, w_q_up, g_kv_lat, w_kv_up, g_q, g_k, cos, sin):
    B, S, _ = q_lat.shape
    q = (rms_norm(q_lat, g_q_lat) @ w_q_up).reshape(B, S, MLA_HEADS, MLA_QK)
    kv = (rms_norm(kv_lat, g_kv_lat) @ w_kv_up).reshape(B, S, MLA_HEADS, MLA_NOPE + MLA_V)
    k_nope, v = kv[..., :MLA_NOPE], kv[..., MLA_NOPE:]
    q_nope = rms_norm(q[..., :MLA_NOPE], g_q[:MLA_NOPE])
    q_rot = apply_rope(rms_norm(q[..., MLA_NOPE:], g_q[MLA_NOPE:]), cos[:, None, :], sin[:, None, :])
    k_nope = rms_norm(k_nope, g_k[:MLA_NOPE])
    k_rot = apply_rope(rms_norm(k_rope, g_k[MLA_NOPE:]), cos, sin)
    q = jnp.concatenate([q_nope, q_rot], axis=-1).transpose(0, 2, 1, 3)
    k = jnp.concatenate([k_nope, jnp.broadcast_to(k_rot[:, :, None, :], (B, S, MLA_HEADS, MLA_ROPE))],
                        axis=-1).transpose(0, 2, 1, 3)
    v = v.transpose(0, 2, 1, 3)
    scale = MLA_QK ** -0.5

    def block(start, qb):
        s = jnp.einsum('bhqd,bhkd->bhqk', qb, k).astype(jnp.float32) * scale
        a = jax.nn.softmax(s, axis=-1)
        return jnp.einsum('bhqk,bhkd->bhqd', a.astype(v.dtype), v)

    return sweep_query_blocks(block, (q,))


def diff_mixer(qd, kd, vd, g_q, g_k, lq1, lk1, lq2, lk2, g_sub, slopes, lambda_init):
    B, S, _ = qd.shape
    q = rms_norm(qd.reshape(B, S, DIFF_HEADS, 2, DIFF_QK), g_q)
    k = rms_norm(kd.reshape(B, S, DIFF_HEADS, 2, DIFF_QK), g_k)
    q1 = q[:, :, :, 0].transpose(0, 2, 1, 3)
    q2 = q[:, :, :, 1].transpose(0, 2, 1, 3)
    k1 = k[:, :, :, 0].transpose(0, 2, 1, 3)
    k2 = k[:, :, :, 1].transpose(0, 2, 1, 3)
    v = vd.reshape(B, S, DIFF_HEADS, DIFF_V).transpose(0, 2, 1, 3)
    lam = (jnp.exp(jnp.sum(lq1.astype(jnp.float32) * lk1.astype(jnp.float32)))
           - jnp.exp(jnp.sum(lq2.astype(jnp.float32) * lk2.astype(jnp.float32))) + lambda_init)
    scale = DIFF_QK ** -0.5
    pos_k = jnp.arange(S, dtype=jnp.int32)

    def block(start, q1b, q2b):
        pos_q = start + jnp.arange(Q_BLOCK, dtype=jnp.int32)
        dist = jnp.abs(pos_q[:, None] - pos_k[None, :]).astype(jnp.float32)
        bias = -slopes[:, None, None] * dist
        a1 = jax.nn.softmax(jnp.einsum('bhqd,bhkd->bhqk', q1b, k1).astype(jnp.float32) * scale + bias, axis=-1)
        a2 = jax.nn.softmax(jnp.einsum('bhqd,bhkd->bhqk', q2b, k2).astype(jnp.float32) * scale + bias, axis=-1)
        a = a1 - lam * a2
        return jnp.einsum('bhqk,bhkd->bhqd', a.astype(v.dtype), v)

    o = sweep_query_blocks(block, (q1, q2)).reshape(B, S, DIFF_HEADS, DIFF_V)
    o = rms_norm(o, g_sub) * (1.0 - lambda_init)
    return o.reshape(B, S, DIFF_HEADS * DIFF_V)


def setup_inputs(seed: int = 0) -> dict:
    key = jax.random.key(seed)
    ks = jax.random.split(key, 32)
    f32 = jnp.float32

    def w(k, shape, fan_in):
        return jax.random.normal(k, shape, f32) * (fan_in ** -0.5)

    def gain(k, n):
        return 1.0 + 0.02 * jax.random.normal(k, (DEPTH, n), f32)

    return {
        "x": jax.random.normal(ks[0], (BATCH, SEQ, D_MODEL), f32),
        "p": jax.random.normal(ks[1], (DEPTH, BATCH, SEQ, PLE_DIM), f32),
        "g_ffn1": gain(ks[2], D_MODEL),
        "w_ffn1_gate": w(ks[3], (DEPTH, D_MODEL, D_FF), D_MODEL),
        "w_ffn1_up": w(ks[4], (DEPTH, D_MODEL, D_FF), D_MODEL),
        "w_ffn1_down": w(ks[5], (DEPTH, D_FF, D_MODEL), D_FF),
        "g_mix": gain(ks[6], D_MODEL),
        "w_in": w(ks[7], (DEPTH, D_MODEL, N_IN), D_MODEL),
        "g_q_lat": gain(ks[8], MLA_Q_RANK),
        "w_q_up": w(ks[9], (DEPTH, MLA_Q_RANK, MLA_HEADS * MLA_QK), MLA_Q_RANK),
        "g_kv_lat": gain(ks[10], MLA_KV_RANK),
        "w_kv_up": w(ks[11], (DEPTH, MLA_KV_RANK, MLA_HEADS * (MLA_NOPE + MLA_V)), MLA_KV_RANK),
        "g_mla_q": gain(ks[12], MLA_QK),
        "g_mla_k": gain(ks[13], MLA_QK),
        "g_diff_q": gain(ks[14], DIFF_QK),
        "g_diff_k": gain(ks[15], DIFF_QK),
        "lambda_q1": 0.1 * jax.random.normal(ks[16], (DEPTH, DIFF_QK), f32),
        "lambda_k1": 0.1 * jax.random.normal(ks[17], (DEPTH, DIFF_QK), f32),
        "lambda_q2": 0.1 * jax.random.normal(ks[18], (DEPTH, DIFF_QK), f32),
        "lambda_k2": 0.1 * jax.random.normal(ks[19], (DEPTH, DIFF_QK), f32),
        "g_diff_sub": gain(ks[20], DIFF_V),
        "w_out": w(ks[21], (DEPTH, MIX_WIDTH, D_MODEL), MIX_WIDTH),
        "g_ffn2": gain(ks[22], D_MODEL),
        "w_ffn2_gate": w(ks[23], (DEPTH, D_MODEL, D_FF), D_MODEL),
        "w_ffn2_up": w(ks[24], (DEPTH, D_MODEL, D_FF), D_MODEL),
        "w_ffn2_down": w(ks[25], (DEPTH, D_FF, D_MODEL), D_FF),
        "g_ple_in": gain(ks[26], D_MODEL),
        "w_ple_gate": w(ks[27], (DEPTH, D_MODEL, D_MODEL), D_MODEL),
        "b_ple_gate": 0.02 * jax.random.normal(ks[28], (DEPTH, D_MODEL), f32),
        "w_ple_proj": w(ks[29], (DEPTH, PLE_DIM, D_MODEL), PLE_DIM),
        "g_ple_out": gain(ks[30], D_MODEL),
    }


def reference(x, p, g_ffn1, w_ffn1_gate, w_ffn1_up, w_ffn1_down, g_mix, w_in, g_q_lat, w_q_up,
              g_kv_lat, w_kv_up, g_mla_q, g_mla_k, g_diff_q, g_diff_k, lambda_q1, lambda_k1,
              lambda_q2, lambda_k2, g_diff_sub, w_out, g_ffn2, w_ffn2_gate, w_ffn2_up, w_ffn2_down,
              g_ple_in, w_ple_gate, b_ple_gate, w_ple_proj, g_ple_out):
    S = x.shape[1]
    cos, sin = rope_tables(S, x.dtype)
    slopes = 2.0 ** (-8.0 * jnp.arange(1, DIFF_HEADS + 1, dtype=jnp.float32) / DIFF_HEADS)
    h = x
    for i in range(DEPTH):
        lambda_init = 0.8 - 0.6 * math.exp(-0.3 * i)
        h = h + 0.5 * swiglu(rms_norm(h, g_ffn1[i]), w_ffn1_gate[i], w_ffn1_up[i], w_ffn1_down[i])
        u = rms_norm(h, g_mix[i]) @ w_in[i]
        q_lat, kv_lat, k_rope, qd, kd, vd = jnp.split(u, IN_OFFSETS, axis=-1)
        o_mla = mla_mixer(q_lat, kv_lat, k_rope, g_q_lat[i], w_q_up[i], g_kv_lat[i], w_kv_up[i],
                          g_mla_q[i], g_mla_k[i], cos, sin)
        o_diff = diff_mixer(qd, kd, vd, g_diff_q[i], g_diff_k[i], lambda_q1[i], lambda_k1[i],
                            lambda_q2[i], lambda_k2[i], g_diff_sub[i], slopes, lambda_init)
        h = h + jnp.concatenate([o_mla, o_diff], axis=-1) @ w_out[i]
        h = h + 0.5 * swiglu(rms_norm(h, g_ffn2[i]), w_ffn2_gate[i], w_ffn2_up[i], w_ffn2_down[i])
        gate = jax.nn.sigmoid(rms_norm(h, g_ple_in[i]) @ w_ple_gate[i] + b_ple_gate[i])
        h = h + gate * rms_norm(p[i] @ w_ple_proj[i], g_ple_out[i])
    return h
```

```python
import math
from contextlib import ExitStack

import numpy as np
import concourse.bass as bass
import concourse.mybir as mybir
from concourse.bass_utils import run_bass_kernel_spmd

F32 = mybir.dt.float32
BF16 = mybir.dt.bfloat16
AF = mybir.ActivationFunctionType
ALU = mybir.AluOpType

NCORES = 8
D = 1024
DFF = 2816
NF = DFF // 128
S = 8192
T = 2048
TT = 512
NTT = T // TT
NKB = S // 128
EPS = 1e-6
N_IN = 1952
LAMBDA_INIT = 0.8 - 0.6 * math.exp(0.0)
SLOPES = [2.0 ** (-8.0 * (i + 1) / 4) for i in range(4)]
MLA_SCALE = 96 ** -0.5

NFT = 12 + 12 * NTT * 16
SKIP_TH = 150.0


def _tile_needed(h, qt, j):
    sg, b = j // 16, j % 16
    for r in range(4):
        rho = (r + sg) % 4
        lo = T * (r - rho) + TT * qt - 128 * b - 127
        hi = T * (r - rho) + TT * qt - 128 * b + 511
        dmin = 0 if lo <= 0 <= hi else min(abs(lo), abs(hi))
        if SLOPES[h] * dmin <= SKIP_TH:
            return True
    return False


OFF_QLAT, OFF_KVLAT, OFF_KROPE, OFF_QD, OFF_KD, OFF_VD = 0, 256, 384, 416, 928, 1440

C_GFFN1, C_GMIX, C_GFFN2, C_GPLEIN, C_GPLEOUT, C_BPLE = 0, 8, 16, 24, 32, 40
C_GQLAT = 48
C_GKVLAT = 50
C_GMQ = 51
C_GMQROT = 52
C_GMKN = 53
C_GMKR = 54
C_GMKRROT = 55
C_GDQ = 56
C_GDK = 57
C_GSUB = 58
C_LQ1, C_LK1, C_LQ2, C_LK2 = 59, 60, 61, 62
NCONST = 64


_PID = {}


def _pid(e):
    if id(e) not in _PID:
        _PID[id(e)] = e.partition_id()
    return _PID[id(e)]


def I(method, *a, **kw):
    return lambda e: getattr(e, method)(*a, **kw)


class Sched:
    ENGS = ("sp", "act", "pool", "dve", "pe")

    def __init__(self):
        self.ops = {e: [] for e in self.ENGS}
        self.last_w = {}
        self.readers = {}
        self.sem_cnt = {}
        self.sem_amt = {}
        self.barrier_refs = []

    def add(self, eng, fn, reads=(), writes=(), dma=None, cc=None, extra=()):
        writes = list(writes) + [k for k in reads if isinstance(k, tuple) and k[0] == "ps" and k not in writes]
        deps = set(self.barrier_refs) | set(extra)
        for k in reads:
            w = self.last_w.get(k)
            if w is not None:
                deps.add(w)
        for k in writes:
            w = self.last_w.get(k)
            if w is not None:
                deps.add(w)
            for r in self.readers.get(k, ()):
                deps.add(r)
        deps = set(("sem", d[1], self.sem_cnt[d[1]]) if d[0] == "sem" else d for d in deps)
        idx = len(self.ops[eng])
        key = dma if dma is not None else cc
        if key is not None:
            self.sem_amt[key] = 16 if dma is not None else 1
            self.sem_cnt[key] = self.sem_cnt.get(key, 0) + 1
            ref = ("sem", key, self.sem_cnt[key])
        else:
            ref = ("op", eng, idx)
        self.ops[eng].append(dict(fn=fn, deps=deps, key=key, ref=ref))
        for k in reads:
            self.readers.setdefault(k, []).append(ref)
        for k in writes:
            self.last_w[k] = ref
            self.readers[k] = []
        return ref

    def barrier(self):
        refs = []
        for e in self.ENGS:
            if self.ops[e]:
                o = self.ops[e][-1]
                refs.append(("op", e, len(self.ops[e]) - 1) if o["key"] is None else o["ref"])
        for key, cnt in self.sem_cnt.items():
            refs.append(("sem", key, cnt))
        self.barrier_refs = refs
        self.last_w = {}
        self.readers = {}

    def emit(self, nc):
        need = {e: set() for e in self.ENGS}
        for e in self.ENGS:
            for o in self.ops[e]:
                nd = set()
                for d in o["deps"]:
                    if d[0] == "op":
                        if d[1] == "pe" and e == "pe":
                            continue
                        if d[1] == "sp":
                            continue
                        need[d[1]].add(d[2])
                    nd.add(d)
                o["deps"] = nd
        cnt = {}
        for e in self.ENGS:
            c = 0
            for i, o in enumerate(self.ops[e]):
                if o["key"] is None and i in need[e]:
                    c += 1
                    o["inc"] = True
                else:
                    o["inc"] = False
                cnt[(e, i)] = c
        with ExitStack() as st:
            esem = {e: st.enter_context(nc.semaphore("s_" + e)) for e in ("act", "pool", "dve", "pe")}
            ksem = {}
            for i, k in enumerate(self.sem_cnt):
                ksem[k] = st.enter_context(nc.semaphore("k%d" % i))
            block = st.enter_context(nc.Block())

            def run(ename, eng):
                waited = {}
                for o in self.ops[ename]:
                    for d in o["deps"]:
                        if d[0] == "op":
                            if d[1] == "pe" and ename == "pe":
                                continue
                            if d[1] == "sp":
                                continue
                            sem, val, wk = esem[d[1]], cnt[(d[1], d[2])], ("e", d[1])
                        else:
                            sem, val, wk = ksem[d[1]], d[2] * self.sem_amt[d[1]], ("k", d[1])
                        if waited.get(wk, 0) >= val:
                            continue
                        waited[wk] = val
                        eng.wait_ge(sem, val)
                    inst = o["fn"](eng)
                    if o["key"] is not None:
                        if self.sem_amt[o["key"]] == 16:
                            inst.then_inc(ksem[o["key"]], 16)
                        else:
                            inst.then_inc(ksem[o["key"]])
                    elif o["inc"]:
                        inst.then_inc(esem[ename], 1)
                for o in self.ops[ename]:
                    pass
                for k, c in self.sem_cnt.items():
                    if k in self.issuer and self.issuer[k] == ename:
                        eng.wait_ge(ksem[k], c * self.sem_amt[k])

            self.issuer = {}
            for e in self.ENGS:
                for o in self.ops[e]:
                    if o["key"] is not None:
                        self.issuer[o["key"]] = e

            @block.sync
            def _(eng):
                run("sp", eng)

            @block.scalar
            def _(eng):
                run("act", eng)

            @block.gpsimd
            def _(eng):
                run("pool", eng)

            @block.vector
            def _(eng):
                run("dve", eng)

            @block.tensor
            def _(eng):
                run("pe", eng)


def build_program(upto=99, debug=False, phases=("f1", "mix", "f2", "ple")):
    nc = bass.Bass("TRN2", target_bir_lowering=False)
    sc = Sched()

    def din(name, shape, dt=F32):
        return nc.dram_tensor(name, list(shape), dt, kind="ExternalInput").ap()

    xT = din("xT", [D, T])
    pT = din("pT", [256, T])
    w1g, w1u, w1d = din("w1g", [D, DFF]), din("w1u", [D, DFF]), din("w1d", [DFF, D])
    w2g, w2u, w2d = din("w2g", [D, DFF]), din("w2u", [D, DFF]), din("w2d", [DFF, D])
    w_in = din("w_in", [D, N_IN])
    w_qup = din("w_qup", [256, 768])
    w_kvup = din("w_kvup", [128, 1024])
    w_out = din("w_out", [D, D])
    w_pg = din("w_pg", [D, D])
    w_pp = din("w_pp", [256, D])
    consts_d = din("consts", [128, NCONST])
    cs96_d = din("cs96", [96, 2, T])
    cs32_d = din("cs32", [32, 2, T])
    tlin_d = din("tlin", [128, TT])
    tabs_d = din("tabs", [128, 896])
    ftab_d = din("ftab", [128, NFT])
    emats_d = din("emats", [128, 4, 128])
    outT = nc.dram_tensor("outT", [D, T], F32, kind="ExternalOutput").ap()

    q_mla_d = nc.dram_tensor("q_mla_d", [8 * 96, T], BF16).ap()
    q_diff_d = nc.dram_tensor("q_diff_d", [4 * 128, T], BF16).ap()
    kT_mla_l = [nc.dram_tensor("kT_mla_l%d" % c, [768, TT], BF16) for c in range(4)]
    kT_mla_g = [nc.dram_tensor("kT_mla_g%d" % c, [4 * 768, TT], BF16) for c in range(4)]
    kT_diff_l = [nc.dram_tensor("kT_diff_l%d" % c, [256, T], BF16) for c in range(2)]
    kT_diff_g = [nc.dram_tensor("kT_diff_g%d" % c, [4 * 256, T], BF16) for c in range(2)]
    v_mla_l = [nc.dram_tensor("v_mla_l%d" % c, [TT, 512], BF16) for c in range(4)]
    v_mla_g = [nc.dram_tensor("v_mla_g%d" % c, [4 * TT, 512], BF16) for c in range(4)]
    v_diff_l = [nc.dram_tensor("v_diff_l%d" % c, [1024, 512], BF16) for c in range(2)]
    v_diff_g = [nc.dram_tensor("v_diff_g%d" % c, [4 * 1024, 512], BF16) for c in range(2)]
    kT_diff_r = nc.dram_tensor("kT_diff_r", [4 * 4 * 128, T], BF16)
    v_diff_r = nc.dram_tensor("v_diff_r", [S, 512], BF16)

    hT = nc.alloc_sbuf_tensor("hT", [128, 8, T], F32)
    consts = nc.alloc_sbuf_tensor("consts_sb", [128, NCONST], F32)
    emats = nc.alloc_sbuf_tensor("emats_sb", [128, 4, 128], F32)
    ones_bf = nc.alloc_sbuf_tensor("ones_bf", [128, 128], BF16)
    eps_t = nc.alloc_sbuf_tensor("eps_t", [128, 1], F32)
    misc = nc.alloc_sbuf_tensor("misc", [128, 8], F32)
    PS = nc.alloc_psum_tensor("ps", [128, 8 * 512], F32)

    def bank(b, p0=0, p1=128, n=1):
        return PS[p0:p1, b * 512:(b + n) * 512]

    ONES32 = emats[:, 0, :]
    EMLA = emats[:, 1, :]
    EDIFF = emats[:, 2, :]
    SHIFT = emats[:, 3, :]

    def col(c, p0=0, p1=128):
        return consts[p0:p1, c:c + 1]

    sc.add("sp", I("dma_start", out=consts[:], in_=consts_d), writes=["consts"], dma="c0")
    sc.add("sp", I("dma_start", out=emats[:], in_=emats_d), writes=["emats"], dma="c0")
    for t in range(NTT):
        for k in range(8):
            sc.add("sp", I("dma_start", out=hT[:, k, t * TT:(t + 1) * TT], in_=xT[k * 128:(k + 1) * 128, t * TT:(t + 1) * TT]),
                   writes=[("hT", k, t)], dma="x%d" % t)
    sc.add("dve", I("memset", ones_bf[:], 1.0), writes=["ones_bf"])
    sc.add("dve", I("memset", eps_t[:], EPS), writes=["eps"])

    def rmsnorm_tile(tok, gcol, xn_tile, sq, sd, rstd, stat_bank, keypfx, scr=None):
        ti = tok // TT
        scr = keypfx if scr is None else scr
        for k in range(8):
            sc.add("act", I("activation", out=sq[:, k, :], in_=hT[:, k, tok:tok + TT], func=AF.Square),
                   reads=[("hT", k, ti)], writes=[(scr, "sq", k)])
        for k in range(8):
            sc.add("pe", I("matmul", bank(stat_bank), lhsT=ones_bf[:], rhs=sq[:, k, :],
                                                 start=(k == 0), stop=(k == 7)),
                   reads=[(scr, "sq", k), "ones_bf"], writes=[("ps", stat_bank)])
        sc.add("act", I("activation", out=sd[:], in_=bank(stat_bank), func=AF.Ln, bias=eps_t[:], scale=1.0 / D),
               reads=[("ps", stat_bank), "eps"], writes=[(scr, "sd")])
        sc.add("act", I("activation", out=rstd[:], in_=sd[:], func=AF.Exp, scale=-0.5), reads=[(scr, "sd")], writes=[(scr, "rstd")])
        for k in range(8):
            sc.add("dve", I("scalar_tensor_tensor",
                out=xn_tile[:, k, :], in0=hT[:, k, tok:tok + TT], scalar=col(gcol + k), in1=rstd[:],
                op0=ALU.mult, op1=ALU.mult),
                reads=[("hT", k, ti), (scr, "rstd"), "consts"], writes=[(keypfx, "xn", k)])

    def ffn(name, gcol, wg, wu, wd, mid_hook=None):
        wg_v = wg.rearrange("(k p) f -> p k f", p=128)
        wu_v = wu.rearrange("(k p) f -> p k f", p=128)
        wd_v = wd.rearrange("(f p) o -> p f o", p=128)
        with ExitStack() as st:
            al = lambda n, s, d: st.enter_context(nc.sbuf_tensor(name + n, s, d))
            xn = al("xn", [128, 2, 8, TT], BF16)
            hid = al("hid", [128, NF, 2 * TT], BF16)
            sq = al("sq", [128, 8, TT], BF16)
            sd = al("sd", [128, TT], F32)
            rstd = al("rstd", [128, TT], F32)
            wgu = [al("wgu%d" % i, [128, 2, 8, 128], BF16) for i in range(3)]
            wdt = [al("wdt%d" % i, [128, NF, 128], BF16) for i in range(2)]
            sg = [al("sg%d" % i, [128, TT], F32) for i in range(2)]
            widx = [0]
            didx = [0]
            for half in range(2):
                if half == 1 and mid_hook is not None:
                    mid_hook()
                for sub in range(2):
                    tok = half * 1024 + sub * TT
                    rmsnorm_tile(tok, gcol, xn[:, sub, :, :], sq, sd, rstd, 6 + sub, (name, "n", sub), scr=(name, "scr"))
                for f in range(NF):
                    slot = widx[0] % 3
                    widx[0] += 1
                    wt = wgu[slot]
                    sc.add("pool", I("dma_start", out=wt[:, 0, :, :], in_=wg_v[:, :, f * 128:(f + 1) * 128]),
                           writes=[(name, "wgu", slot, 0)], dma=(name, "wgu", slot))
                    sc.add("pool", I("dma_start", out=wt[:, 1, :, :], in_=wu_v[:, :, f * 128:(f + 1) * 128]),
                           writes=[(name, "wgu", slot, 1)], dma=(name, "wgu", slot))
                    for sub in range(2):
                        gb, ub = sub, 2 + sub
                        for k in range(8):
                            sc.add("pe", I("matmul",
                                bank(gb), lhsT=wt[:, 0, k, :], rhs=xn[:, sub, k, :], start=(k == 0), stop=(k == 7)),
                                reads=[(name, "wgu", slot, 0), ((name, "n", sub), "xn", k)], writes=[("ps", gb)])
                        for k in range(8):
                            sc.add("pe", I("matmul",
                                bank(ub), lhsT=wt[:, 1, k, :], rhs=xn[:, sub, k, :], start=(k == 0), stop=(k == 7)),
                                reads=[(name, "wgu", slot, 1), ((name, "n", sub), "xn", k)], writes=[("ps", ub)])
                        sc.add("act", I("activation", out=sg[sub][:], in_=bank(gb), func=AF.Silu),
                               reads=[("ps", gb)], writes=[(name, "sg", sub)])
                        sc.add("dve", I("tensor_tensor",
                            out=hid[:, f, sub * TT:(sub + 1) * TT], in0=sg[sub][:], in1=bank(ub), op=ALU.mult),
                            reads=[(name, "sg", sub), ("ps", ub)], writes=[(name, "hid", f, sub)])
                for o in range(8):
                    slot = didx[0] % 2
                    didx[0] += 1
                    wt = wdt[slot]
                    sc.add("pool", I("dma_start", out=wt[:], in_=wd_v[:, :, o * 128:(o + 1) * 128]),
                           writes=[(name, "wdt", slot)], dma=(name, "wdt", slot))
                    for sub in range(2):
                        ob = 4 + sub
                        ti = half * 2 + sub
                        tok = ti * TT
                        for f in range(NF):
                            sc.add("pe", I("matmul",
                                bank(ob), lhsT=wt[:, f, :], rhs=hid[:, f, sub * TT:(sub + 1) * TT],
                                start=(f == 0), stop=(f == NF - 1)),
                                reads=[(name, "wdt", slot), (name, "hid", f, sub)], writes=[("ps", ob)])
                        sc.add("dve", I("scalar_tensor_tensor",
                            out=hT[:, o, tok:tok + TT], in0=bank(ob), scalar=0.5, in1=hT[:, o, tok:tok + TT],
                            op0=ALU.mult, op1=ALU.add),
                            reads=[("ps", ob), ("hT", o, ti)], writes=[("hT", o, ti)])
        sc.barrier()

    wst = ExitStack()
    win_pre = None
    if upto >= 2 and "mix" in phases and "f1" in phases:
        win_pre = wst.enter_context(nc.sbuf_tensor("m_win", [128, 8, N_IN], BF16))

    def win_hook():
        win_v = w_in.rearrange("(k p) n -> p k n", p=128)
        for k in range(8):
            sc.add("pool", I("dma_start", out=win_pre[:, k, :], in_=win_v[:, k, :]), writes=[("win", k)], dma=("win", k))

    if "f1" in phases:
        ffn("f1", C_GFFN1, w1g, w1u, w1d, mid_hook=win_hook if win_pre is not None else None)
    if upto >= 2 and "mix" in phases:
        mixer_phase(nc, sc, locals())
    with ExitStack() as pst:
        pw = ple_prefetch(nc, sc, locals(), pst) if (upto >= 4 and "ple" in phases) else None
        if upto >= 3 and "f2" in phases:
            ffn("f2", C_GFFN2, w2g, w2u, w2d)
        if pw is not None:
            ple_phase(nc, sc, locals(), pw)

    for k in range(8):
        sc.add("sp", I("dma_start", out=outT[k * 128:(k + 1) * 128, :], in_=hT[:, k, :]),
               reads=[("hT", k, t) for t in range(NTT)], dma="out")
    sc.emit(nc)
    return nc


MIX_STOP = 99


def mixer_phase(nc, sc, L):
    hT, consts, emats, ones_bf, eps_t, misc, PS = (L[k] for k in ("hT", "consts", "emats", "ones_bf", "eps_t", "misc", "PS"))
    bank, col, rmsnorm_tile = L["bank"], L["col"], L["rmsnorm_tile"]
    ONES32, EMLA, EDIFF, SHIFT = L["ONES32"], L["EMLA"], L["EDIFF"], L["SHIFT"]
    w_in, w_qup, w_kvup, w_out = L["w_in"], L["w_qup"], L["w_kvup"], L["w_out"]
    cs96_d, cs32_d, tlin_d, tabs_d, ftab_d = L["cs96_d"], L["cs32_d"], L["tlin_d"], L["tabs_d"], L["ftab_d"]
    q_mla_d, q_diff_d = L["q_mla_d"], L["q_diff_d"]
    kT_mla_l, kT_diff_l, v_mla_l, v_diff_l = L["kT_mla_l"], L["kT_diff_l"], L["v_mla_l"], L["v_diff_l"]
    kT_mla_g, kT_diff_g, v_mla_g, v_diff_g = L["kT_mla_g"], L["kT_diff_g"], L["v_mla_g"], L["v_diff_g"]
    kT_diff_r, v_diff_r = L["kT_diff_r"], L["v_diff_r"]

    groups = [[0, 1, 2, 3], [4, 5, 6, 7]]
    MLA_PAIRS = [[("v_mla", c, v_mla_l[c], v_mla_g[c]), ("kT_mla", c, kT_mla_l[c], kT_mla_g[c])] for c in range(4)]
    DIFF_PAIRS = [("kT_diff", c, kT_diff_l[c], kT_diff_g[c]) for c in range(2)] + [("v_diff", c, v_diff_l[c], v_diff_g[c]) for c in range(2)]
    ncc = [0]

    def issue_gathers(pairs, explicit_deps, extra=()):
        for (nm, c, a, g) in pairs:
            rd = []
            if explicit_deps and nm == "v_mla":
                rd = [("v_mla_l", c, tb) for tb in range(4)]
            if explicit_deps and nm == "kT_mla":
                rd = [("kT_mla_l", h, part, c) for h in range(8) for part in (0, 1)]
            sc.add("pool", I("collective_compute", "AllGather", ALU.bypass, replica_groups=groups,
                             ins=[a.ap().opt()], outs=[g.ap().opt()]),
                   reads=rd, writes=[("gath", nm, c)], cc="cc%d" % ncc[0], extra=extra)
            ncc[0] += 1

    with ExitStack() as st:
        al = lambda n, s, d: st.enter_context(nc.sbuf_tensor("m_" + n, s, d))
        xn = al("xn", [128, 8, TT], BF16)
        sq8 = al("sq8", [128, 8, TT], BF16)
        sd0 = al("sd0", [128, TT], F32)
        rstd0 = al("rstd0", [128, TT], F32)
        win_pre = L.get("win_pre")
        win = win_pre if win_pre is not None else al("win", [128, 8, N_IN], BF16)
        win_rot = al("winrot", [128, 8, 32], BF16)
        wq = al("wq", [128, 2, 768], BF16)
        wq_rot = al("wqrot", [128, 2, 8, 96], BF16)
        wkv = al("wkv", [128, 1024], BF16)
        wkv_v = al("wkvv", [128, 8, 64], BF16)
        cg = al("cg", [96, 2, T], F32)
        ck = al("ck", [32, 2, T], F32)
        qlat_n = al("qlatn", [128, 2, TT], BF16)
        kvlat_n = al("kvlatn", [128, TT], BF16)
        NS = 4
        sqf = [al("sqf%d" % i, [128, TT], BF16) for i in range(NS)]
        embf = al("embf", [128, 3, 128], BF16)
        sc.add("dve", I("tensor_copy", out=embf[:], in_=emats[:, 0:3, :]), reads=["emats"], writes=["embf"])
        ONESB, EMLAB, EDIFFB = embf[:, 0, :], embf[:, 1, :], embf[:, 2, :]
        NOB = 8
        rsf = [al("rsf%d" % i, [128, TT], F32) for i in range(NS)]
        t1 = [al("t1_%d" % i, [128, TT], F32) for i in range(NS)]
        t2 = [al("t2_%d" % i, [128, TT], F32) for i in range(NS)]
        ob = [al("ob%d" % i, [128, TT], BF16) for i in range(8)]
        vst = [al("vst%d" % i, [128, 512], BF16) for i in range(NS)]

        win_v = w_in.rearrange("(k p) n -> p k n", p=128)
        for k in range(8 if win_pre is None else 0):
            sc.add("pool", I("dma_start", out=win[:, k, :], in_=win_v[:, k, :]), writes=[("win", k)], dma=("win", k))
        for k in range(8):
            sc.add("pool", I("dma_start", out=win_rot[:, k, 0:16], in_=win_v[:, k, OFF_KROPE + 16:OFF_KROPE + 32]),
                   writes=[("winrot", k, 0)], dma="winrot")
            sc.add("pool", I("dma_start", out=win_rot[:, k, 16:32], in_=win_v[:, k, OFF_KROPE:OFF_KROPE + 16]),
                   writes=[("winrot", k, 1)], dma="winrot")
        wq_v = w_qup.rearrange("(k p) n -> p k n", p=128)
        wq_v4 = w_qup.rearrange("(k p) (h d) -> p k h d", p=128, d=96)
        sc.add("dve", I("memset", wq_rot[:], 0.0), writes=["wqrot"])
        for k in range(2):
            sc.add("pool", I("dma_start", out=wq[:, k, :], in_=wq_v[:, k, :]), writes=[("wq", k)], dma="wq")
            sc.add("pool", I("dma_start", out=wq_rot[:, k, :, 64:80], in_=wq_v4[:, k, :, 80:96]),
                   reads=["wqrot"], writes=[("wqrot", k, 0)], dma="wqrot")
            sc.add("pool", I("dma_start", out=wq_rot[:, k, :, 80:96], in_=wq_v4[:, k, :, 64:80]),
                   reads=["wqrot"], writes=[("wqrot", k, 1)], dma="wqrot")
        sc.add("pool", I("dma_start", out=wkv[:], in_=w_kvup), writes=["wkv"], dma="wkv")
        sc.add("pool", I("dma_start", out=wkv_v[:], in_=w_kvup.rearrange("p (h d) -> p h d", d=128)[:, :, 64:128]),
               writes=["wkvv"], dma="wkv")
        sc.add("sp", I("dma_start", out=cg[:], in_=cs96_d), writes=["cg_raw"], dma="cg")
        sc.add("sp", I("dma_start", out=ck[:], in_=cs32_d), writes=["ck_raw"], dma="cg")
        sc.add("dve", I("tensor_scalar", out=cg[:, 0, :], in0=cg[:, 0, :], scalar1=col(C_GMQ, 0, 96), scalar2=None, op0=ALU.mult),
               reads=["cg_raw", "consts"], writes=["cg0"])
        sc.add("dve", I("tensor_scalar", out=cg[:, 1, :], in0=cg[:, 1, :], scalar1=col(C_GMQROT, 0, 96), scalar2=None, op0=ALU.mult),
               reads=["cg_raw", "consts"], writes=["cg1"])
        sc.add("dve", I("tensor_scalar", out=ck[:, 0, :], in0=ck[:, 0, :], scalar1=col(C_GMKR, 0, 32), scalar2=None, op0=ALU.mult),
               reads=["ck_raw", "consts"], writes=["ck0"])
        sc.add("dve", I("tensor_scalar", out=ck[:, 1, :], in0=ck[:, 1, :], scalar1=col(C_GMKRROT, 0, 32), scalar2=None, op0=ALU.mult),
               reads=["ck_raw", "consts"], writes=["ck1"])
        sc.add("dve", I("tensor_tensor", out=misc[0:64, 0:1], in0=col(C_LQ1, 0, 64), in1=col(C_LK1, 0, 64), op=ALU.mult),
               reads=["consts"], writes=["lam_p1"])
        sc.add("dve", I("tensor_tensor", out=misc[0:64, 1:2], in0=col(C_LQ2, 0, 64), in1=col(C_LK2, 0, 64), op=ALU.mult),
               reads=["consts"], writes=["lam_p2"])
        sc.add("pe", I("matmul", PS[:, 0:2], lhsT=emats[0:64, 0, :], rhs=misc[0:64, 0:2], start=True, stop=True),
               reads=["lam_p1", "lam_p2", "emats"], writes=[("ps", 0)])
        sc.add("act", I("activation", out=misc[:, 2:4], in_=PS[:, 0:2], func=AF.Exp), reads=[("ps", 0)], writes=["lam_e"])
        sc.add("dve", I("tensor_tensor", out=misc[:, 4:5], in0=misc[:, 3:4], in1=misc[:, 2:3], op=ALU.subtract),
               reads=["lam_e"], writes=["lam_d"])
        sc.add("dve", I("tensor_scalar", out=misc[:, 5:6], in0=misc[:, 4:5], scalar1=-LAMBDA_INIT, scalar2=None, op0=ALU.add),
               reads=["lam_d"], writes=["neglam"])
        sc.add("dve", I("tensor_scalar", out=misc[:, 6:7], in0=col(C_GSUB), scalar1=1.0 - LAMBDA_INIT, scalar2=None, op0=ALU.mult),
               reads=["consts"], writes=["gsub"])

        rr = [0]
        pb = [0]

        obr = [0]

        def nob():
            obr[0] += 1
            return obr[0] % 8

        def nslot():
            rr[0] += 1
            return rr[0] % NS

        def nbank():
            pb[0] += 1
            return pb[0] % 6

        def norm_stats(src_bank, P, emat_ap, scale, s):
            sb = 6 + (s % 2)
            sc.add("act", I("activation", out=sqf[s][0:P, :], in_=bank(src_bank, 0, P), func=AF.Square),
                   reads=[("ps", src_bank)], writes=[("sqf", s)])
            sc.add("pe", I("matmul", bank(sb, 0, P), lhsT=emat_ap, rhs=sqf[s][0:P, :], start=True, stop=True),
                   reads=[("sqf", s), "embf"], writes=[("ps", sb)])
            sc.add("act", I("activation", out=rsf[s][0:P, :], in_=bank(sb, 0, P), func=AF.Ln, bias=eps_t[0:P, :], scale=scale),
                   reads=[("ps", sb), "eps"], writes=[("rsf", s)])
            sc.add("act", I("activation", out=rsf[s][0:P, :], in_=rsf[s][0:P, :], func=AF.Exp, scale=-0.5), reads=[("rsf", s)], writes=[("rsf", s)])

        def sq_part(src_bank, P, s):
            sc.add("act", I("activation", out=sqf[s][0:P, :], in_=bank(src_bank, 0, P), func=AF.Square),
                   reads=[("ps", src_bank)], writes=[("sqf", s)])

        def stat_part(P, emat_ap, scale, s):
            sb = 6 + (s % 2)
            sc.add("pe", I("matmul", bank(sb, 0, P), lhsT=emat_ap, rhs=sqf[s][0:P, :], start=True, stop=True),
                   reads=[("sqf", s), "embf"], writes=[("ps", sb)])
            sc.add("act", I("activation", out=rsf[s][0:P, :], in_=bank(sb, 0, P), func=AF.Ln, bias=eps_t[0:P, :], scale=scale),
                   reads=[("ps", sb), "eps"], writes=[("rsf", s)])
            sc.add("act", I("activation", out=rsf[s][0:P, :], in_=rsf[s][0:P, :], func=AF.Exp, scale=-0.5), reads=[("rsf", s)], writes=[("rsf", s)])

        def run_chains(chains):
            prev = None
            for ch in chains:
                ch[0]()
                if prev is not None:
                    prev[1]()
                prev = ch
            if prev is not None:
                prev[1]()

        xk = lambda k: (("mx",), "xn", k)

        def proj_in(b, c0, M):
            for k in range(8):
                sc.add("pe", I("matmul", bank(b, 0, M), lhsT=win[:, k, c0:c0 + M], rhs=xn[:, k, :], start=(k == 0), stop=(k == 7)),
                       reads=[("win", k), xk(k)], writes=[("ps", b)])

        def mk_A(ti):
            st_ = {}

            def A():
                st_["b"] = (nbank(), nbank())
                st_["s"] = (nslot(), nslot())
                for c in range(2):
                    proj_in(st_["b"][c], OFF_QLAT + c * 128, 128)
                    sq_part(st_["b"][c], 128, st_["s"][c])

            def B():
                (b0, b1), (s0, s1) = st_["b"], st_["s"]
                sc.add("pe", I("matmul", bank(6), lhsT=ONESB, rhs=sqf[s0][:], start=True, stop=False), reads=[("sqf", s0), "embf"], writes=[("ps", 6)])
                sc.add("pe", I("matmul", bank(6), lhsT=ONESB, rhs=sqf[s1][:], start=False, stop=True), reads=[("sqf", s1), "embf"], writes=[("ps", 6)])
                sc.add("act", I("activation", out=rsf[s0][:], in_=bank(6), func=AF.Ln, bias=eps_t[:], scale=1.0 / 256),
                       reads=[("ps", 6), "eps"], writes=[("rsf", s0)])
                sc.add("act", I("activation", out=rsf[s0][:], in_=rsf[s0][:], func=AF.Exp, scale=-0.5), reads=[("rsf", s0)], writes=[("rsf", s0)])
                for c, b in ((0, b0), (1, b1)):
                    sc.add("dve", I("scalar_tensor_tensor", out=qlat_n[:, c, :], in0=bank(b), scalar=col(C_GQLAT + c), in1=rsf[s0][:],
                                    op0=ALU.mult, op1=ALU.mult),
                           reads=[("ps", b), ("rsf", s0), "consts"], writes=[("qlatn", c)])
            return A, B

        def mk_B(ti):
            st_ = {}

            def A():
                st_["b"], st_["s"] = nbank(), nslot()
                proj_in(st_["b"], OFF_KVLAT, 128)
                sq_part(st_["b"], 128, st_["s"])

            def B():
                b, s = st_["b"], st_["s"]
                stat_part(128, ONESB, 1.0 / 128, s)
                sc.add("dve", I("scalar_tensor_tensor", out=kvlat_n[:], in0=bank(b), scalar=col(C_GKVLAT), in1=rsf[s][:], op0=ALU.mult, op1=ALU.mult),
                       reads=[("ps", b), ("rsf", s), "consts"], writes=["kvlatn"])
            return A, B

        def rope_tail(P, bq, br, s, tab, tsl, k0, k1, eng="pool"):
            sc.add("dve", I("tensor_tensor", out=t1[s][0:P, :], in0=bank(bq, 0, P), in1=tab[:, 0, tsl], op=ALU.mult),
                   reads=[("ps", bq), k0], writes=[("t1", s)])
            sc.add("dve", I("tensor_tensor", out=t2[s][0:P, :], in0=bank(br, 0, P), in1=tab[:, 1, tsl], op=ALU.mult),
                   reads=[("ps", br), k1], writes=[("t2", s)])
            sc.add(eng, I("tensor_tensor", out=t1[s][0:P, :], in0=t1[s][0:P, :], in1=t2[s][0:P, :], op=ALU.add),
                   reads=[("t1", s), ("t2", s)], writes=[("t1", s)])
            o_ = nob()
            sc.add(eng, I("tensor_tensor", out=ob[o_][0:P, :], in0=t1[s][0:P, :], in1=rsf[s][0:P, :], op=ALU.mult),
                   reads=[("t1", s), ("rsf", s)], writes=[("ob", o_)])
            return o_

        def mk_C(ti, h):
            st_ = {}
            tsl = slice(ti * TT, (ti + 1) * TT)

            def A():
                bq, br, s = nbank(), nbank(), nslot()
                st_.update(bq=bq, br=br, s=s)
                for c in range(2):
                    sc.add("pe", I("matmul", bank(bq, 0, 96), lhsT=wq[:, c, h * 96:(h + 1) * 96], rhs=qlat_n[:, c, :], start=(c == 0), stop=(c == 1)),
                           reads=[("wq", c), ("qlatn", c)], writes=[("ps", bq)])
                for c in range(2):
                    sc.add("pe", I("matmul", bank(br, 0, 96), lhsT=wq_rot[:, c, h, :], rhs=qlat_n[:, c, :], start=(c == 0), stop=(c == 1)),
                           reads=[("wqrot", c, 0), ("wqrot", c, 1), ("qlatn", c)], writes=[("ps", br)])
                sq_part(bq, 96, s)

            def B():
                bq, br, s = st_["bq"], st_["br"], st_["s"]
                stat_part(96, EMLAB[0:96, 0:96], 1.0, s)
                o_ = rope_tail(96, bq, br, s, cg, tsl, "cg0", "cg1")
                sc.add("sp", I("dma_start", out=q_mla_d[h * 96:(h + 1) * 96, tsl], in_=ob[o_][0:96, :]),
                       reads=[("ob", o_)], writes=[("q_mla_d", h, ti)], dma=("st", o_))
            return A, B

        def mk_D(ti, h):
            st_ = {}
            tsl = slice(ti * TT, (ti + 1) * TT)

            def A():
                bk, s = nbank(), nslot()
                st_.update(bk=bk, s=s)
                sc.add("pe", I("matmul", bank(bk, 0, 64), lhsT=wkv[:, h * 128:h * 128 + 64], rhs=kvlat_n[:], start=True, stop=True),
                       reads=["wkv", "kvlatn"], writes=[("ps", bk)])
                sq_part(bk, 64, s)

            def B():
                bk, s = st_["bk"], st_["s"]
                stat_part(64, ONESB[0:64, 0:64], 1.0 / 64, s)
                sc.add("act", I("activation", out=t2[s][0:64, :], in_=bank(bk, 0, 64), func=AF.Copy, scale=col(C_GMKN, 0, 64)),
                       reads=[("ps", bk), "consts"], writes=[("t2", s)])
                o_ = nob()
                sc.add("dve", I("tensor_tensor", out=ob[o_][0:64, :], in0=t2[s][0:64, :], in1=rsf[s][0:64, :], op=ALU.mult),
                       reads=[("t2", s), ("rsf", s)], writes=[("ob", o_)])
                sc.add("sp", I("dma_start", out=kT_mla_l[ti].ap()[h * 96:h * 96 + 64, :], in_=ob[o_][0:64, :]),
                       reads=[("ob", o_)], writes=[("kT_mla_l", h, 0, ti)], dma=("st", o_))
            return A, B

        def mk_KR(ti):
            st_ = {}
            tsl = slice(ti * TT, (ti + 1) * TT)

            def A():
                bq, br, s = nbank(), nbank(), nslot()
                st_.update(bq=bq, br=br, s=s)
                for k in range(8):
                    sc.add("pe", I("matmul", bank(bq, 0, 32), lhsT=win[:, k, OFF_KROPE:OFF_KROPE + 32], rhs=xn[:, k, :], start=(k == 0), stop=(k == 7)),
                           reads=[("win", k), xk(k)], writes=[("ps", bq)])
                for k in range(8):
                    sc.add("pe", I("matmul", bank(br, 0, 32), lhsT=win_rot[:, k, :], rhs=xn[:, k, :], start=(k == 0), stop=(k == 7)),
                           reads=[("winrot", k, 0), ("winrot", k, 1), xk(k)], writes=[("ps", br)])
                sq_part(bq, 32, s)

            def B():
                bq, br, s = st_["bq"], st_["br"], st_["s"]
                stat_part(32, ONESB[0:32, 0:32], 1.0 / 32, s)
                o_ = rope_tail(32, bq, br, s, ck, tsl, "ck0", "ck1", eng="dve")
                for h in range(8):
                    sc.add("sp", I("dma_start", out=kT_mla_l[ti].ap()[h * 96 + 64:h * 96 + 96, :], in_=ob[o_][0:32, :]),
                           reads=[("ob", o_)], writes=[("kT_mla_l", h, 1, ti)], dma=("st", o_))
            return A, B

        def mk_FG(ti, off, gc, dname, h):
            st_ = {}
            tsl = slice(ti * TT, (ti + 1) * TT)
            dst_ap = (q_diff_d[h * 128:(h + 1) * 128, tsl] if dname == "q_diff_d"
                      else kT_diff_l[h // 2].ap()[(h % 2) * 128:(h % 2 + 1) * 128, tsl])

            def A():
                b, s = nbank(), nslot()
                st_.update(b=b, s=s)
                proj_in(b, off + h * 128, 128)
                sq_part(b, 128, s)

            def B():
                b, s = st_["b"], st_["s"]
                stat_part(128, EDIFFB, 1.0, s)
                sc.add("act", I("activation", out=t2[s][:], in_=bank(b), func=AF.Copy, scale=col(gc)),
                       reads=[("ps", b), "consts"], writes=[("t2", s)])
                o_ = nob()
                sc.add("pool", I("tensor_tensor", out=ob[o_][:], in0=t2[s][:], in1=rsf[s][:], op=ALU.mult),
                       reads=[("t2", s), ("rsf", s)], writes=[("ob", o_)])
                sc.add("sp", I("dma_start", out=dst_ap, in_=ob[o_][:]), reads=[("ob", o_)], writes=[(dname, h, ti)], dma=("st", o_))
            return A, B

        def mk_V(ti, tb, kind):
            st_ = {}
            tok = ti * TT
            row = (tok + tb * 128) % 1024
            dst = (v_mla_l[ti].ap()[tb * 128:(tb + 1) * 128, :] if kind == "mla"
                   else v_diff_l[(tok + tb * 128) // 1024].ap()[row:row + 128, :])

            def A():
                bv = nbank()
                st_.update(bv=bv)
                if kind == "mla":
                    sc.add("pe", I("matmul", bank(bv), lhsT=kvlat_n[:, tb * 128:(tb + 1) * 128], rhs=wkv_v[:].rearrange("p h d -> p (h d)"),
                                   start=True, stop=True),
                           reads=["kvlatn", "wkvv"], writes=[("ps", bv)])
                else:
                    for k in range(8):
                        sc.add("pe", I("matmul", bank(bv), lhsT=xn[:, k, tb * 128:(tb + 1) * 128], rhs=win[:, k, OFF_VD:OFF_VD + 512],
                                       start=(k == 0), stop=(k == 7)),
                               reads=[("win", k), xk(k)], writes=[("ps", bv)])

            def B():
                bv, s = st_["bv"], nslot()
                sc.add("act", I("activation", out=vst[s][:], in_=bank(bv), func=AF.Copy), reads=[("ps", bv)], writes=[("vst", s)])
                sc.add("sp", I("dma_start", out=dst, in_=vst[s][:]), reads=[("vst", s)],
                       writes=[("v_mla_l" if kind == "mla" else "v_diff_l", ti, tb)], dma=("stv", s))
            return A, B

        for want in ("BDE", "ACFGH"):
            for ti in range(NTT):
                rmsnorm_tile(ti * TT, C_GMIX, xn, sq8, sd0, rstd0, 7, ("mx",))
                if want == "BDE":
                    chains = [mk_B(ti), mk_KR(ti)] + [mk_D(ti, h) for h in range(8)] + [mk_V(ti, tb, "mla") for tb in range(4)]
                else:
                    chains = ([mk_A(ti)] + [mk_FG(ti, OFF_QD, C_GDQ, "q_diff_d", h) for h in range(4)]
                              + [mk_FG(ti, OFF_KD, C_GDK, "kT_diff_l", h) for h in range(4)]
                              + [mk_C(ti, h) for h in range(8)] + [mk_V(ti, tb, "diff") for tb in range(4)])
                run_chains(chains)
                if want == "BDE":
                    issue_gathers(MLA_PAIRS[ti], True)
    sc.barrier()
    L["wst"].close()
    if MIX_STOP <= 1:
        return

    if MIX_STOP <= 1.5:
        return

    def krot(e, sg, c):
        rho = (_pid(e) % 4 + sg) % 4
        k4 = kT_diff_g[c].ap().rearrange("(rk p) t -> rk p t", rk=4)
        return e.dma_start(out=kT_diff_r.ap()[sg * 512 + c * 256:sg * 512 + (c + 1) * 256, :], in_=k4[bass.ds(rho, 1), :, :])

    def vrot(e, sg, c):
        rho = (_pid(e) % 4 + sg) % 4
        v4 = v_diff_g[c].ap().rearrange("(rk t) c -> rk t c", rk=4)
        return e.dma_start(out=v_diff_r.ap()[sg * T + c * 1024:sg * T + (c + 1) * 1024, :], in_=v4[bass.ds(rho, 1), :, :])

    def record_rot():
        for sg in range(4):
            for c in range(2):
                sc.add("sp", lambda e, sg=sg, c=c: krot(e, sg, c), reads=[("gath", "kT_diff", c)], writes=["kT_diff_r"], dma="rot")
                sc.add("sp", lambda e, sg=sg, c=c: vrot(e, sg, c), reads=[("gath", "v_diff", c)], writes=["v_diff_r"], dma="rot")

    if MIX_STOP <= 2:
        record_rot()
        sc.barrier()
        return

    with ExitStack() as st0:
      mixT = st0.enter_context(nc.sbuf_tensor("a_mixT", [128, 8, T], BF16))
      with ExitStack() as st:
        al = lambda n, s, d: st.enter_context(nc.sbuf_tensor("a_" + n, s, d))
        kbuf = [al("kbuf%d" % i, [128, S], BF16) for i in range(2)]
        vbuf = [al("vbuf%d" % i, [128, NKB, 128], BF16) for i in range(2)]
        qbuf = [al("qbuf0", [128, T], BF16)] * 2
        NPT = 3
        pT_ = [al("pT%d" % i, [128, 2 * TT], BF16) for i in range(NPT)]
        tmp = [al("tmp%d" % i, [128, 2 * TT], F32) for i in range(2)]
        tlin = al("tlin", [128, TT], F32)
        tabs = al("tabs", [128, 896], F32)
        ftab = al("ftab", [128, NFT], F32)
        osb = [al("osb%d" % i, [128, TT], F32) for i in range(2)]
        rec = [al("rec%d" % i, [128, TT], F32) for i in range(2)]
        ea = [al("ea%d" % i, [128, TT], F32) for i in range(2)]
        ostage = [al("ost0", [64, TT], BF16)] * 2

        sc.add("sp", I("dma_start", out=tlin[:], in_=tlin_d), writes=["tlin"], dma="atab")
        sc.add("sp", I("dma_start", out=tabs[:], in_=tabs_d), writes=["tabs"], dma="atab")
        sc.add("sp", I("dma_start", out=ftab[:], in_=ftab_d), writes=["ftab"], dma="atab")
        for i in range(2):
            sc.add("dve", I("memset", vbuf[i][:, :, 64:128], 1.0), writes=[("vbuf", i, "ones")])

        def load_mla(h, i):
            for c in range(NTT):
                for r in range(4):
                    j0 = c * 16 + r * 4
                    sc.add("sp", I("dma_start", out=kbuf[i][0:96, j0 * 128:(j0 + 4) * 128],
                                   in_=kT_mla_g[c].ap()[r * 768 + h * 96:r * 768 + (h + 1) * 96, :]),
                           reads=[("gath", "kT_mla", c)], writes=[("kbuf", i, c)], dma=("kb", i, c))
                    vsrc = v_mla_g[c].ap()[r * TT:(r + 1) * TT, h * 64:(h + 1) * 64].rearrange("(kb p) d -> p kb d", p=128)
                    sc.add("sp", I("dma_start", out=vbuf[i][:, j0:j0 + 4, 0:64], in_=vsrc),
                           reads=[("vbuf", i, "ones"), ("gath", "v_mla", c)], writes=[("vbuf", i, c)], dma=("vb", i, c))

        def load_mla_q(h):
            sc.add("sp", I("dma_start", out=qbuf[0][0:96, :], in_=q_mla_d[h * 96:(h + 1) * 96, :]),
                   writes=[("qbuf", 0)], dma=("qb", 0))

        steps = [(h, qt, g) for h in range(8) for qt in range(NTT) for g in range(NKB // 2)]

        def mla_qk(si):
            h, qt, g = steps[si]
            i = h % 2
            sb = (si % 3) * 2
            for j in range(2):
                kb = g * 2 + j
                sc.add("pe", I("matmul", bank(sb + j), lhsT=kbuf[i][0:96, kb * 128:(kb + 1) * 128],
                                                           rhs=qbuf[i][0:96, qt * TT:(qt + 1) * TT], start=True, stop=True),
                       reads=[("kbuf", i, kb // 16), ("qbuf", 0)], writes=[("ps", sb + j)])

        def mla_exp_av(si):
            h, qt, g = steps[si]
            i = h % 2
            sb = (si % 3) * 2
            ps_ = si % NPT
            it = h * NTT + qt
            obk = 6 + (it % 2)
            sc.add("act", I("activation", out=pT_[ps_][:], in_=bank(sb, n=2), func=AF.Exp, scale=MLA_SCALE),
                   reads=[("ps", sb), ("ps", sb + 1)], writes=[("pT", ps_)])
            for j in range(2):
                kb = g * 2 + j
                sc.add("pe", I("matmul", bank(obk), lhsT=vbuf[i][:, kb, :], rhs=pT_[ps_][:, j * TT:(j + 1) * TT],
                                                           start=(kb == 0), stop=(kb == NKB - 1)),
                       reads=[("vbuf", i, kb // 16), ("pT", ps_)], writes=[("ps", obk)])
            if g == NKB // 2 - 1:
                e2 = it % 2
                db = obk
                sc.add("act", I("activation", out=osb[e2][:], in_=bank(obk), func=AF.Copy),
                       reads=[("ps", obk)], writes=[("osb", e2)])
                sc.add("pe", I("matmul", bank(db, 0, 64), lhsT=SHIFT[:, 0:64], rhs=osb[e2][:], start=True, stop=True),
                       reads=[("osb", e2), "emats"], writes=[("ps", db)])
                sc.add("dve", I("reciprocal", out=rec[e2][0:64, :], in_=bank(db, 0, 64)), reads=[("ps", db)], writes=[("rec", e2)])
                sc.add("dve", I("tensor_tensor", out=ostage[e2][:], in0=osb[e2][0:64, :], in1=rec[e2][0:64, :], op=ALU.mult),
                       reads=[("osb", e2), ("rec", e2)], writes=[("ost", 0)])
                p0 = (h % 2) * 64
                sc.add("sp", I("dma_start", out=mixT[p0:p0 + 64, h // 2, qt * TT:(qt + 1) * TT], in_=ostage[e2][:]),
                       reads=[("ost", 0)], writes=[("mixT", h // 2, qt, h % 2)], dma=("ostd", 0))

        PF = 2
        load_mla_q(0)
        load_mla(0, 0)
        for si in range(len(steps) if MIX_STOP >= 3 else 0):
            h, qt, g = steps[si]
            if qt == 0 and g == 0 and h + 1 < 8:
                load_mla(h + 1, (h + 1) % 2)
            if qt == 0 and g == 0 and 1 <= h <= 4:
                nb = (h + 1) % 2
                marker = sc.add("dve", I("memset", misc[:, 7:8], 0.0),
                                reads=[("kbuf", nb, c) for c in range(NTT)] + [("vbuf", nb, c) for c in range(NTT)], writes=["marker"])
                issue_gathers([DIFF_PAIRS[h - 1]], False, extra=[marker])
            if qt == 0 and g == 0 and h == 6:
                record_rot()
            if si == 0:
                for p in range(min(PF, len(steps))):
                    mla_qk(p)
            if si + PF < len(steps):
                if steps[si + PF][0] != steps[si + PF - 1][0]:
                    load_mla_q(steps[si + PF][0])
                mla_qk(si + PF)
            mla_exp_av(si)

        def load_diff(h, i):
            for sg in range(4):
                sc.add("sp", I("dma_start", out=kbuf[i][:, sg * T:(sg + 1) * T],
                               in_=kT_diff_r.ap()[sg * 512 + h * 128:sg * 512 + (h + 1) * 128, :]),
                       reads=["kT_diff_r"], writes=[("kbuf", i)] + [("kbuf", i, c) for c in range(NTT)], dma=("kbd", i))
            vsrc = v_diff_r.ap().rearrange("(kb p) (h d) -> p kb h d", p=128, d=128)
            for q4 in range(4):
                sc.add("sp", I("dma_start", out=vbuf[i][:, q4 * 16:(q4 + 1) * 16, :], in_=vsrc[:, q4 * 16:(q4 + 1) * 16, h, :]),
                       reads=["v_diff_r"], writes=[("vbuf", i)] + [("vbuf", i, c) for c in range(NTT)], dma=("vbd", i))

        def load_diff_q(h):
            sc.add("sp", I("dma_start", out=qbuf[0][:], in_=q_diff_d[h * 128:(h + 1) * 128, :]),
                   writes=[("qbuf", 0)], dma=("qb", 0))

        dsteps = []
        for h in range(4):
            for qt in range(NTT):
                js = [j for j in range(NKB) if _tile_needed(h, qt, j)]
                for n, j in enumerate(js):
                    dsteps.append((h, qt, j, n == 0, n == len(js) - 1))

        def diff_qk(si):
            h, qt, j, first, last = dsteps[si]
            i = h % 2
            for m in range(2):
                sb = (2 * si + m) % 4
                sc.add("pe", I("matmul", bank(sb), lhsT=kbuf[i][m * 64:(m + 1) * 64, j * 128:(j + 1) * 128],
                                                     rhs=qbuf[i][m * 64:(m + 1) * 64, qt * TT:(qt + 1) * TT], start=True, stop=True),
                       reads=[("kbuf", i), ("qbuf", 0)], writes=[("ps", sb)])

        def diff_rest(si):
            h, qt, j, first, last = dsteps[si]
            i = h % 2
            sl = SLOPES[h]
            sg, b = j // 16, j % 16
            if sg == 0 and 4 * qt <= b < 4 * qt + 4:
                off = 384 - 128 * (b - 4 * qt)
                in0, scal, abias, rd = tabs[:, off:off + TT], 8.0 * sl, 0.0, ["tabs"]
            elif sg == 0:
                sgn = 1.0 if b < 4 * qt else -1.0
                in0, scal, abias, rd = tlin[:], -8.0 * sl * sgn, -sl * sgn * float(TT * qt - 128 * b), ["tlin"]
            else:
                c_sig = h * 3 + (sg - 1)
                c_cb = 12 + ((h * 3 + (sg - 1)) * NTT + qt) * 16 + b
                in0, scal, abias, rd = tlin[:], ftab[:, c_sig:c_sig + 1], ftab[:, c_cb:c_cb + 1], ["tlin", "ftab"]
            for m in range(2):
                sb = (2 * si + m) % 4
                sl4 = (2 * si + m) % 4
                tv = tmp[sl4 // 2][:, (sl4 % 2) * TT:(sl4 % 2 + 1) * TT]
                pv = pT_[sl4 // 2][:, (sl4 % 2) * TT:(sl4 % 2 + 1) * TT]
                sc.add("dve", I("scalar_tensor_tensor", out=tv, in0=in0, scalar=scal, in1=bank(sb),
                                op0=ALU.mult, op1=ALU.add),
                       reads=rd + [("ps", sb)], writes=[("tmp", sl4)])
                sc.add("act", I("activation", out=pv, in_=tv, func=AF.Exp, bias=abias, scale=0.125),
                       reads=[("tmp", sl4), "ftab"], writes=[("pTd", sl4)])
            if si + PF < len(dsteps):
                if dsteps[si + PF][0] != dsteps[si + PF - 1][0]:
                    load_diff_q(dsteps[si + PF][0])
                diff_qk(si + PF)
            for m in range(2):
                sl4 = (2 * si + m) % 4
                pv = pT_[sl4 // 2][:, (sl4 % 2) * TT:(sl4 % 2 + 1) * TT]
                sc.add("pe", I("matmul", bank(4 + 2 * m), lhsT=vbuf[i][:, j, :], rhs=pv, start=first, stop=last),
                       reads=[("vbuf", i), ("pTd", sl4)], writes=[("ps", 4 + 2 * m)])
                sc.add("pe", I("matmul", bank(5 + 2 * m), lhsT=ones_bf[:], rhs=pv, start=first, stop=last),
                       reads=["ones_bf", ("pTd", sl4)], writes=[("ps", 5 + 2 * m)])
            if last:
                for m in range(2):
                    sc.add("act", I("activation", out=rec[m][:], in_=bank(5 + 2 * m), func=AF.Ln), reads=[("ps", 5 + 2 * m)], writes=[("rec", m)])
                    sc.add("act", I("activation", out=rec[m][:], in_=rec[m][:], func=AF.Exp, scale=-1.0), reads=[("rec", m)], writes=[("rec", m)])
                    sc.add("dve", I("tensor_tensor", out=ea[m][:], in0=bank(4 + 2 * m), in1=rec[m][:], op=ALU.mult),
                           reads=[("ps", 4 + 2 * m), ("rec", m)], writes=[("ea", m)])
                sc.add("dve", I("scalar_tensor_tensor", out=osb[0][:], in0=ea[1][:], scalar=misc[:, 5:6], in1=ea[0][:],
                                                               op0=ALU.mult, op1=ALU.add),
                       reads=[("ea", 0), ("ea", 1), "neglam"], writes=[("osb", 0)])
                sc.add("act", I("activation", out=osb[1][:], in_=osb[0][:], func=AF.Square), reads=[("osb", 0)], writes=[("osb", 1)])
                sc.add("pe", I("matmul", bank(5), lhsT=ONES32, rhs=osb[1][:], start=True, stop=True),
                       reads=[("osb", 1), "emats"], writes=[("ps", 5)])
                sc.add("act", I("activation", out=rec[0][:], in_=bank(5), func=AF.Ln, bias=eps_t[:], scale=1.0 / 128),
                       reads=[("ps", 5), "eps"], writes=[("rec", 0)])
                sc.add("act", I("activation", out=rec[1][:], in_=rec[0][:], func=AF.Exp, scale=-0.5), reads=[("rec", 0)], writes=[("rec", 1)])
                sc.add("dve", I("scalar_tensor_tensor", out=mixT[:, 4 + h, qt * TT:(qt + 1) * TT], in0=osb[0][:], scalar=misc[:, 6:7],
                                                               in1=rec[1][:], op0=ALU.mult, op1=ALU.mult),
                       reads=[("osb", 0), ("rec", 1), "gsub"], writes=[("mixT", 4 + h, qt, 0), ("mixT", 4 + h, qt, 1)])

        load_diff(0, 0)
        load_diff_q(0)
        for si in range(len(dsteps) if MIX_STOP >= 4 else 0):
            h, qt, j, first, last = dsteps[si]
            if qt == 0 and first and h + 1 < 4:
                load_diff(h + 1, (h + 1) % 2)
            if si == 0:
                for p in range(min(PF, len(dsteps))):
                    diff_qk(p)
            diff_rest(si)

      sc.barrier()
      with ExitStack() as st:
        wout = st.enter_context(nc.sbuf_tensor("a_wout", [128, 8, D], BF16))
        wout_v = w_out.rearrange("(k p) n -> p k n", p=128)
        for k in range(8):
            sc.add("pool", I("dma_start", out=wout[:, k, :], in_=wout_v[:, k, :]), writes=[("wout", k)], dma=("wout", k))
        for ti in range(NTT):
            for o in range(8):
                b = (ti * 8 + o) % 4
                for c in range(8):
                    sc.add("pe", I("matmul", bank(b), lhsT=wout[:, c, o * 128:(o + 1) * 128],
                                                                rhs=mixT[:, c, ti * TT:(ti + 1) * TT], start=(c == 0), stop=(c == 7)),
                           reads=[("wout", c), ("mixT", c, ti, 0), ("mixT", c, ti, 1)], writes=[("ps", b)])
                sc.add("dve", I("tensor_tensor", out=hT[:, o, ti * TT:(ti + 1) * TT], in0=bank(b),
                                                               in1=hT[:, o, ti * TT:(ti + 1) * TT], op=ALU.add),
                       reads=[("ps", b), ("hT", o, ti)], writes=[("hT", o, ti)])
    sc.barrier()


def ple_prefetch(nc, sc, L, st):
    w_pg, w_pp, pT = L["w_pg"], L["w_pp"], L["pT"]
    wpg = st.enter_context(nc.sbuf_tensor("p_wpg", [128, 8, D], BF16))
    wpp = st.enter_context(nc.sbuf_tensor("p_wpp", [128, 2, D], BF16))
    pbf = st.enter_context(nc.sbuf_tensor("p_pbf", [128, 2, T], BF16))
    wpg_v = w_pg.rearrange("(k p) n -> p k n", p=128)
    for k in range(8):
        sc.add("pool", I("dma_start", out=wpg[:, k, :], in_=wpg_v[:, k, :]), writes=[("wpg", k)], dma="wpg")
    wpp_v = w_pp.rearrange("(k p) n -> p k n", p=128)
    pT_v = pT.rearrange("(k p) t -> p k t", p=128)
    for k in range(2):
        sc.add("pool", I("dma_start", out=wpp[:, k, :], in_=wpp_v[:, k, :]), writes=[("wpp", k)], dma="wpp")
        for q4 in range(4):
            sc.add("pool", I("dma_start", out=pbf[:, k, q4 * TT:(q4 + 1) * TT], in_=pT_v[:, k, q4 * TT:(q4 + 1) * TT]),
                   writes=[("pbf", k)], dma="pbf")
    return wpg, wpp, pbf


def ple_phase(nc, sc, L, pw):
    hT, consts, emats, ones_bf, eps_t, PS = (L[k] for k in ("hT", "consts", "emats", "ones_bf", "eps_t", "PS"))
    bank, col, rmsnorm_tile = L["bank"], L["col"], L["rmsnorm_tile"]
    ONES32 = L["ONES32"]
    wpg, wpp, pbf = pw
    with ExitStack() as st:
        al = lambda n, s, d: st.enter_context(nc.sbuf_tensor("p_" + n, s, d))
        xn = [al("xn%d" % i, [128, 8, TT], BF16) for i in range(2)]
        sq8 = al("sq8", [128, 8, TT], BF16)
        sd0 = al("sd0", [128, TT], F32)
        rstd0 = al("rstd0", [128, TT], F32)
        gate = [al("gate0", [128, 8, TT], F32)] * 2
        esb = [al("esb%d" % i, [128, 8, TT], F32) for i in range(2)]
        esq = [al("esq0", [128, 8, TT], F32)] * 2
        sd1 = [al("sd1_%d" % i, [128, TT], F32) for i in range(2)]
        rs1 = [al("rs1_%d" % i, [128, TT], F32) for i in range(2)]
        for ti in range(NTT):
            tok = ti * TT
            z = ti % 2
            for o in range(8):
                b = 3 + (o % 3)
                for k in range(2):
                    sc.add("pe", I("matmul", bank(b), lhsT=wpp[:, k, o * 128:(o + 1) * 128], rhs=pbf[:, k, tok:tok + TT],
                                   start=(k == 0), stop=(k == 1)),
                           reads=[("wpp", k), ("pbf", k)], writes=[("ps", b)])
                sc.add("dve", I("tensor_scalar", out=esb[z][:, o, :], in0=bank(b), scalar1=col(C_GPLEOUT + o), scalar2=None, op0=ALU.mult),
                       reads=[("ps", b), "consts"], writes=[("esb", z, o)])
                sc.add("act", I("activation", out=esq[z][:, o, :], in_=bank(b), func=AF.Square), reads=[("ps", b)], writes=[("esq", 0, o)])
            for o in range(8):
                sc.add("pe", I("matmul", bank(6), lhsT=ONES32, rhs=esq[z][:, o, :], start=(o == 0), stop=(o == 7)),
                       reads=[("esq", 0, o), "emats"], writes=[("ps", 6)])
            sc.add("act", I("activation", out=sd1[z][:], in_=bank(6), func=AF.Ln, bias=eps_t[:], scale=1.0 / D),
                   reads=[("ps", 6), "eps"], writes=[("sd1", z)])
            sc.add("act", I("activation", out=rs1[z][:], in_=sd1[z][:], func=AF.Exp, scale=-0.5), reads=[("sd1", z)], writes=[("rs1", z)])
            rmsnorm_tile(tok, C_GPLEIN, xn[z], sq8, sd0, rstd0, 7, ("pl", z), scr=("pl", "scr"))
            for o in range(8):
                b = o % 3
                for k in range(8):
                    sc.add("pe", I("matmul", bank(b), lhsT=wpg[:, k, o * 128:(o + 1) * 128], rhs=xn[z][:, k, :],
                                   start=(k == 0), stop=(k == 7)),
                           reads=[("wpg", k), (("pl", z), "xn", k)], writes=[("ps", b)])
                sc.add("act", I("activation", out=gate[z][:, o, :], in_=bank(b), func=AF.Sigmoid, bias=col(C_BPLE + o)),
                       reads=[("ps", b), "consts"], writes=[("gate", 0, o)])
                sc.add("pool", I("tensor_tensor", out=esb[z][:, o, :], in0=esb[z][:, o, :], in1=rs1[z][:], op=ALU.mult),
                       reads=[("esb", z, o), ("rs1", z)], writes=[("esb", z, o)])
                sc.add("dve", I("tensor_tensor", out=esb[z][:, o, :], in0=esb[z][:, o, :], in1=gate[z][:, o, :], op=ALU.mult),
                       reads=[("esb", z, o), ("gate", 0, o)], writes=[("esb", z, o)])
                sc.add("dve", I("tensor_tensor", out=hT[:, o, tok:tok + TT], in0=esb[z][:, o, :], in1=hT[:, o, tok:tok + TT], op=ALU.add),
                       reads=[("esb", z, o), ("hT", o, ti)], writes=[("hT", o, ti)])
    sc.barrier()


def _const_tables(core):
    r = core % 4
    pos = (np.arange(T, dtype=np.float32) + np.float32(r * T)).astype(np.float32)
    inv = (np.float32(10000.0) ** (-np.arange(0, 32, 2, dtype=np.float32) / np.float32(32))).astype(np.float32)
    ang = pos[:, None] * inv[None, :]
    ang = np.concatenate([ang, ang], axis=-1)
    cos = np.cos(ang).astype(np.float32).T
    sin = np.sin(ang).astype(np.float32).T
    sin_signed = sin.copy()
    sin_signed[0:16] *= -1.0
    cs32 = np.stack([cos, sin_signed], axis=1)
    cs96 = np.zeros((96, 2, T), np.float32)
    cs96[0:64, 0, :] = 1.0
    cs96[64:96] = cs32
    ip = np.arange(TT, dtype=np.float32)[None, :]
    jp = np.arange(128, dtype=np.float32)[:, None]
    tlin = (ip - jp).astype(np.float32)
    xx = np.arange(896, dtype=np.float32)[None, :]
    tabs = (-np.abs(xx - 384.0 - jp)).astype(np.float32)
    ftab = np.zeros((128, NFT), np.float32)
    for h in range(4):
        for sg in (1, 2, 3):
            rho = (r + sg) % 4
            sgn = 1.0 if r > rho else -1.0
            ftab[:, h * 3 + (sg - 1)] = -8.0 * SLOPES[h] * sgn
            for qt in range(NTT):
                for b in range(16):
                    x0 = float(T * (r - rho) + TT * qt - 128 * b)
                    ftab[:, 12 + ((h * 3 + (sg - 1)) * NTT + qt) * 16 + b] = -SLOPES[h] * sgn * x0
    return cs96, cs32, tlin, tabs, ftab


def _emats():
    e = np.zeros((128, 4, 128), np.float32)
    e[:, 0, :] = 1.0
    e[0:64, 1, 0:64] = 1.0 / 64
    e[64:96, 1, 64:96] = 1.0 / 32
    e[0:64, 2, 0:64] = 1.0 / 64
    e[64:128, 2, 64:128] = 1.0 / 64
    for i in range(64):
        e[64 + i, 3, i] = 1.0
    return e


def _consts(inp):
    c = np.zeros((128, NCONST), np.float32)

    def chunks(v):
        return np.asarray(v, np.float32).reshape(8, 128).T

    c[:, C_GFFN1:C_GFFN1 + 8] = chunks(inp["g_ffn1"][0])
    c[:, C_GMIX:C_GMIX + 8] = chunks(inp["g_mix"][0])
    c[:, C_GFFN2:C_GFFN2 + 8] = chunks(inp["g_ffn2"][0])
    c[:, C_GPLEIN:C_GPLEIN + 8] = chunks(inp["g_ple_in"][0])
    c[:, C_GPLEOUT:C_GPLEOUT + 8] = chunks(inp["g_ple_out"][0])
    c[:, C_BPLE:C_BPLE + 8] = chunks(inp["b_ple_gate"][0])
    c[:, C_GQLAT:C_GQLAT + 2] = np.asarray(inp["g_q_lat"][0], np.float32).reshape(2, 128).T
    c[:, C_GKVLAT] = np.asarray(inp["g_kv_lat"][0], np.float32)
    gq = np.asarray(inp["g_mla_q"][0], np.float32)
    gk = np.asarray(inp["g_mla_k"][0], np.float32)
    perm = np.concatenate([np.arange(16, 32), np.arange(0, 16)])
    c[0:96, C_GMQ] = gq
    c[64:96, C_GMQROT] = gq[64 + perm]
    c[0:64, C_GMKN] = gk[0:64]
    c[0:32, C_GMKR] = gk[64:96]
    c[0:32, C_GMKRROT] = gk[64 + perm]
    c[:, C_GDQ] = np.tile(np.asarray(inp["g_diff_q"][0], np.float32), 2)
    c[:, C_GDK] = np.tile(np.asarray(inp["g_diff_k"][0], np.float32), 2)
    c[:, C_GSUB] = np.asarray(inp["g_diff_sub"][0], np.float32)
    c[0:64, C_LQ1] = np.asarray(inp["lambda_q1"][0], np.float32)
    c[0:64, C_LK1] = np.asarray(inp["lambda_k1"][0], np.float32)
    c[0:64, C_LQ2] = np.asarray(inp["lambda_q2"][0], np.float32)
    c[0:64, C_LK2] = np.asarray(inp["lambda_k2"][0], np.float32)
    return c


_PROG = {}


def _get_prog(upto=99):
    if upto not in _PROG:
        _PROG[upto] = build_program(upto=upto)
    return _PROG[upto]


def make_in_maps(inp):
    x = np.asarray(inp["x"], np.float32)
    p = np.asarray(inp["p"], np.float32)[0]
    shared = {
        "w1g": np.ascontiguousarray(inp["w_ffn1_gate"][0], np.float32),
        "w1u": np.ascontiguousarray(inp["w_ffn1_up"][0], np.float32),
        "w1d": np.ascontiguousarray(inp["w_ffn1_down"][0], np.float32),
        "w2g": np.ascontiguousarray(inp["w_ffn2_gate"][0], np.float32),
        "w2u": np.ascontiguousarray(inp["w_ffn2_up"][0], np.float32),
        "w2d": np.ascontiguousarray(inp["w_ffn2_down"][0], np.float32),
        "w_in": np.ascontiguousarray(inp["w_in"][0], np.float32),
        "w_qup": np.ascontiguousarray(inp["w_q_up"][0], np.float32),
        "w_kvup": np.ascontiguousarray(inp["w_kv_up"][0], np.float32),
        "w_out": np.ascontiguousarray(inp["w_out"][0], np.float32),
        "w_pg": np.ascontiguousarray(inp["w_ple_gate"][0], np.float32),
        "w_pp": np.ascontiguousarray(inp["w_ple_proj"][0], np.float32),
        "consts": _consts(inp),
        "emats": _emats(),
    }
    maps = []
    for c in range(NCORES):
        b, r = c // 4, c % 4
        cs96, cs32, tlin, tabs, ftab = _const_tables(c)
        m = dict(shared)
        m["xT"] = np.ascontiguousarray(x[b, r * T:(r + 1) * T, :].T)
        m["pT"] = np.ascontiguousarray(p[b, r * T:(r + 1) * T, :].T)
        m["cs96"], m["cs32"], m["tlin"], m["tabs"], m["ftab"] = cs96, cs32, tlin, tabs, ftab
        maps.append(m)
    return maps


def kernel(**inputs):
    nc = _get_prog()
    maps = make_in_maps(inputs)
    res = run_bass_kernel_spmd(nc, maps, core_ids=list(range(NCORES)))
    out = np.empty((2, S, D), np.float32)
    for c in range(NCORES):
        b, r = c // 4, c % 4
        out[b, r * T:(r + 1) * T, :] = np.asarray(res.results[c]["outT"], np.float32).T
    return out
```

```python
import math
from contextlib import ExitStack

import numpy as np
import concourse.bass as bass
import concourse.mybir as mybir
from concourse.bass_utils import run_bass_kernel_spmd

F32 = mybir.dt.float32
BF16 = mybir.dt.bfloat16
AF = mybir.ActivationFunctionType
ALU = mybir.AluOpType

NCORES = 8
D = 1024
DFF = 2816
NF = DFF // 128
S = 8192
T = 2048
TT = 512
NTT = T // TT
NKB = S // 128
EPS = 1e-6
N_IN = 1952
LAMBDA_INIT = 0.8 - 0.6 * math.exp(0.0)
SLOPES = [2.0 ** (-8.0 * (i + 1) / 4) for i in range(4)]
MLA_SCALE = 96 ** -0.5

NFT = 12 + 12 * NTT * 16
SKIP_TH = 150.0


def _tile_needed(h, qt, j):
    sg, b = j // 16, j % 16
    for r in range(4):
        rho = (r + sg) % 4
        lo = T * (r - rho) + TT * qt - 128 * b - 127
        hi = T * (r - rho) + TT * qt - 128 * b + 511
        dmin = 0 if lo <= 0 <= hi else min(abs(lo), abs(hi))
        if SLOPES[h] * dmin <= SKIP_TH:
            return True
    return False


OFF_QLAT, OFF_KVLAT, OFF_KROPE, OFF_QD, OFF_KD, OFF_VD = 0, 256, 384, 416, 928, 1440

C_GFFN1, C_GMIX, C_GFFN2, C_GPLEIN, C_GPLEOUT, C_BPLE = 0, 8, 16, 24, 32, 40
C_GQLAT = 48
C_GKVLAT = 50
C_GMQ = 51
C_GMQROT = 52
C_GMKN = 53
C_GMKR = 54
C_GMKRROT = 55
C_GDQ = 56
C_GDK = 57
C_GSUB = 58
C_LQ1, C_LK1, C_LQ2, C_LK2 = 59, 60, 61, 62
NCONST = 64


_PID = {}


def _pid(e):
    if id(e) not in _PID:
        _PID[id(e)] = e.partition_id()
    return _PID[id(e)]


def I(method, *a, **kw):
    return lambda e: getattr(e, method)(*a, **kw)


class Sched:
    ENGS = ("sp", "act", "pool", "dve", "pe")

    def __init__(self):
        self.ops = {e: [] for e in self.ENGS}
        self.last_w = {}
        self.readers = {}
        self.sem_cnt = {}
        self.sem_amt = {}
        self.barrier_refs = []

    def add(self, eng, fn, reads=(), writes=(), dma=None, cc=None, extra=()):
        writes = list(writes) + [k for k in reads if isinstance(k, tuple) and k[0] == "ps" and k not in writes]
        deps = set(self.barrier_refs) | set(extra)
        for k in reads:
            w = self.last_w.get(k)
            if w is not None:
                deps.add(w)
        for k in writes:
            w = self.last_w.get(k)
            if w is not None:
                deps.add(w)
            for r in self.readers.get(k, ()):
                deps.add(r)
        deps = set(("sem", d[1], self.sem_cnt[d[1]]) if d[0] == "sem" else d for d in deps)
        idx = len(self.ops[eng])
        key = dma if dma is not None else cc
        if key is not None:
            self.sem_amt[key] = 16 if dma is not None else 1
            self.sem_cnt[key] = self.sem_cnt.get(key, 0) + 1
            ref = ("sem", key, self.sem_cnt[key])
        else:
            ref = ("op", eng, idx)
        self.ops[eng].append(dict(fn=fn, deps=deps, key=key, ref=ref))
        for k in reads:
            self.readers.setdefault(k, []).append(ref)
        for k in writes:
            self.last_w[k] = ref
            self.readers[k] = []
        return ref

    def barrier(self):
        refs = []
        for e in self.ENGS:
            if self.ops[e]:
                o = self.ops[e][-1]
                refs.append(("op", e, len(self.ops[e]) - 1) if o["key"] is None else o["ref"])
        for key, cnt in self.sem_cnt.items():
            refs.append(("sem", key, cnt))
        self.barrier_refs = refs
        self.last_w = {}
        self.readers = {}

    def emit(self, nc):
        need = {e: set() for e in self.ENGS}
        for e in self.ENGS:
            for o in self.ops[e]:
                nd = set()
                for d in o["deps"]:
                    if d[0] == "op":
                        if d[1] == "pe" and e == "pe":
                            continue
                        if d[1] == "sp":
                            continue
                        need[d[1]].add(d[2])
                    nd.add(d)
                o["deps"] = nd
        cnt = {}
        for e in self.ENGS:
            c = 0
            for i, o in enumerate(self.ops[e]):
                if o["key"] is None and i in need[e]:
                    c += 1
                    o["inc"] = True
                else:
                    o["inc"] = False
                cnt[(e, i)] = c
        with ExitStack() as st:
            esem = {e: st.enter_context(nc.semaphore("s_" + e)) for e in ("act", "pool", "dve", "pe")}
            ksem = {}
            for i, k in enumerate(self.sem_cnt):
                ksem[k] = st.enter_context(nc.semaphore("k%d" % i))
            block = st.enter_context(nc.Block())

            def run(ename, eng):
                waited = {}
                for o in self.ops[ename]:
                    for d in o["deps"]:
                        if d[0] == "op":
                            if d[1] == "pe" and ename == "pe":
                                continue
                            if d[1] == "sp":
                                continue
                            sem, val, wk = esem[d[1]], cnt[(d[1], d[2])], ("e", d[1])
                        else:
                            sem, val, wk = ksem[d[1]], d[2] * self.sem_amt[d[1]], ("k", d[1])
                        if waited.get(wk, 0) >= val:
                            continue
                        waited[wk] = val
                        eng.wait_ge(sem, val)
                    inst = o["fn"](eng)
                    if o["key"] is not None:
                        if self.sem_amt[o["key"]] == 16:
                            inst.then_inc(ksem[o["key"]], 16)
                        else:
                            inst.then_inc(ksem[o["key"]])
                    elif o["inc"]:
                        inst.then_inc(esem[ename], 1)
                for o in self.ops[ename]:
                    pass
                for k, c in self.sem_cnt.items():
                    if k in self.issuer and self.issuer[k] == ename:
                        eng.wait_ge(ksem[k], c * self.sem_amt[k])

            self.issuer = {}
            for e in self.ENGS:
                for o in self.ops[e]:
                    if o["key"] is not None:
                        self.issuer[o["key"]] = e

            @block.sync
            def _(eng):
                run("sp", eng)

            @block.scalar
            def _(eng):
                run("act", eng)

            @block.gpsimd
            def _(eng):
                run("pool", eng)

            @block.vector
            def _(eng):
                run("dve", eng)

            @block.tensor
            def _(eng):
                run("pe", eng)


def build_program(upto=99, debug=False, phases=("f1", "mix", "f2", "ple")):
    nc = bass.Bass("TRN2", target_bir_lowering=False)
    sc = Sched()

    def din(name, shape, dt=F32):
        return nc.dram_tensor(name, list(shape), dt, kind="ExternalInput").ap()

    xT = din("xT", [D, T])
    pT = din("pT", [256, T])
    w1g, w1u, w1d = din("w1g", [D, DFF]), din("w1u", [D, DFF]), din("w1d", [DFF, D])
    w2g, w2u, w2d = din("w2g", [D, DFF]), din("w2u", [D, DFF]), din("w2d", [DFF, D])
    w_in = din("w_in", [D, N_IN])
    w_qup = din("w_qup", [256, 768])
    w_kvup = din("w_kvup", [128, 1024])
    w_out = din("w_out", [D, D])
    w_pg = din("w_pg", [D, D])
    w_pp = din("w_pp", [256, D])
    consts_d = din("consts", [128, NCONST])
    cs96_d = din("cs96", [96, 2, T])
    cs32_d = din("cs32", [32, 2, T])
    tlin_d = din("tlin", [128, TT])
    tabs_d = din("tabs", [128, 896])
    ftab_d = din("ftab", [128, NFT])
    emats_d = din("emats", [128, 4, 128])
    outT = nc.dram_tensor("outT", [D, T], F32, kind="ExternalOutput").ap()

    q_mla_d = nc.dram_tensor("q_mla_d", [8 * 96, T], BF16).ap()
    q_diff_d = nc.dram_tensor("q_diff_d", [4 * 128, T], BF16).ap()
    kT_mla_l = [nc.dram_tensor("kT_mla_l%d" % c, [768, TT], BF16) for c in range(4)]
    kT_mla_g = [nc.dram_tensor("kT_mla_g%d" % c, [4 * 768, TT], BF16) for c in range(4)]
    kT_diff_l = [nc.dram_tensor("kT_diff_l%d" % c, [256, T], BF16) for c in range(2)]
    kT_diff_g = [nc.dram_tensor("kT_diff_g%d" % c, [4 * 256, T], BF16) for c in range(2)]
    v_mla_l = [nc.dram_tensor("v_mla_l%d" % c, [TT, 512], BF16) for c in range(4)]
    v_mla_g = [nc.dram_tensor("v_mla_g%d" % c, [4 * TT, 512], BF16) for c in range(4)]
    v_diff_l = [nc.dram_tensor("v_diff_l%d" % c, [1024, 512], BF16) for c in range(2)]
    v_diff_g = [nc.dram_tensor("v_diff_g%d" % c, [4 * 1024, 512], BF16) for c in range(2)]
    kT_diff_r = nc.dram_tensor("kT_diff_r", [4 * 4 * 128, T], BF16)
    v_diff_r = nc.dram_tensor("v_diff_r", [S, 512], BF16)

    hT = nc.alloc_sbuf_tensor("hT", [128, 8, T], F32)
    consts = nc.alloc_sbuf_tensor("consts_sb", [128, NCONST], F32)
    emats = nc.alloc_sbuf_tensor("emats_sb", [128, 4, 128], F32)
    ones_bf = nc.alloc_sbuf_tensor("ones_bf", [128, 128], BF16)
    eps_t = nc.alloc_sbuf_tensor("eps_t", [128, 1], F32)
    misc = nc.alloc_sbuf_tensor("misc", [128, 8], F32)
    PS = nc.alloc_psum_tensor("ps", [128, 8 * 512], F32)

    def bank(b, p0=0, p1=128, n=1):
        return PS[p0:p1, b * 512:(b + n) * 512]

    ONES32 = emats[:, 0, :]
    EMLA = emats[:, 1, :]
    EDIFF = emats[:, 2, :]
    SHIFT = emats[:, 3, :]

    def col(c, p0=0, p1=128):
        return consts[p0:p1, c:c + 1]

    sc.add("sp", I("dma_start", out=consts[:], in_=consts_d), writes=["consts"], dma="c0")
    sc.add("sp", I("dma_start", out=emats[:], in_=emats_d), writes=["emats"], dma="c0")
    for t in range(NTT):
        for k in range(8):
            sc.add("sp", I("dma_start", out=hT[:, k, t * TT:(t + 1) * TT], in_=xT[k * 128:(k + 1) * 128, t * TT:(t + 1) * TT]),
                   writes=[("hT", k, t)], dma="x%d" % t)
    sc.add("dve", I("memset", ones_bf[:], 1.0), writes=["ones_bf"])
    sc.add("dve", I("memset", eps_t[:], EPS), writes=["eps"])

    def rmsnorm_tile(tok, gcol, xn_tile, sq, sd, rstd, stat_bank, keypfx, scr=None):
        ti = tok // TT
        scr = keypfx if scr is None else scr
        for k in range(8):
            sc.add("act", I("activation", out=sq[:, k, :], in_=hT[:, k, tok:tok + TT], func=AF.Square),
                   reads=[("hT", k, ti)], writes=[(scr, "sq", k)])
        for k in range(8):
            sc.add("pe", I("matmul", bank(stat_bank), lhsT=ones_bf[:], rhs=sq[:, k, :],
                                                 start=(k == 0), stop=(k == 7)),
                   reads=[(scr, "sq", k), "ones_bf"], writes=[("ps", stat_bank)])
        sc.add("act", I("activation", out=sd[:], in_=bank(stat_bank), func=AF.Ln, bias=eps_t[:], scale=1.0 / D),
               reads=[("ps", stat_bank), "eps"], writes=[(scr, "sd")])
        sc.add("act", I("activation", out=rstd[:], in_=sd[:], func=AF.Exp, scale=-0.5), reads=[(scr, "sd")], writes=[(scr, "rstd")])
        for k in range(8):
            sc.add("dve", I("scalar_tensor_tensor",
                out=xn_tile[:, k, :], in0=hT[:, k, tok:tok + TT], scalar=col(gcol + k), in1=rstd[:],
                op0=ALU.mult, op1=ALU.mult),
                reads=[("hT", k, ti), (scr, "rstd"), "consts"], writes=[(keypfx, "xn", k)])

    def ffn(name, gcol, wg, wu, wd, mid_hook=None):
        wg_v = wg.rearrange("(k p) f -> p k f", p=128)
        wu_v = wu.rearrange("(k p) f -> p k f", p=128)
        wd_v = wd.rearrange("(f p) o -> p f o", p=128)
        with ExitStack() as st:
            al = lambda n, s, d: st.enter_context(nc.sbuf_tensor(name + n, s, d))
            xn = al("xn", [128, 2, 8, TT], BF16)
            hid = al("hid", [128, NF, 2 * TT], BF16)
            sq = al("sq", [128, 8, TT], BF16)
            sd = al("sd", [128, TT], F32)
            rstd = al("rstd", [128, TT], F32)
            wgu = [al("wgu%d" % i, [128, 2, 8, 128], BF16) for i in range(3)]
            wdt = [al("wdt%d" % i, [128, NF, 128], BF16) for i in range(2)]
            sg = [al("sg%d" % i, [128, TT], F32) for i in range(2)]
            widx = [0]
            didx = [0]
            for half in range(2):
                if half == 1 and mid_hook is not None:
                    mid_hook()
                for sub in range(2):
                    tok = half * 1024 + sub * TT
                    rmsnorm_tile(tok, gcol, xn[:, sub, :, :], sq, sd, rstd, 6 + sub, (name, "n", sub), scr=(name, "scr"))
                for f in range(NF):
                    slot = widx[0] % 3
                    widx[0] += 1
                    wt = wgu[slot]
                    sc.add("pool", I("dma_start", out=wt[:, 0, :, :], in_=wg_v[:, :, f * 128:(f + 1) * 128]),
                           writes=[(name, "wgu", slot, 0)], dma=(name, "wgu", slot))
                    sc.add("pool", I("dma_start", out=wt[:, 1, :, :], in_=wu_v[:, :, f * 128:(f + 1) * 128]),
                           writes=[(name, "wgu", slot, 1)], dma=(name, "wgu", slot))
                    for sub in range(2):
                        gb, ub = sub, 2 + sub
                        for k in range(8):
                            sc.add("pe", I("matmul",
                                bank(gb), lhsT=wt[:, 0, k, :], rhs=xn[:, sub, k, :], start=(k == 0), stop=(k == 7)),
                                reads=[(name, "wgu", slot, 0), ((name, "n", sub), "xn", k)], writes=[("ps", gb)])
                        for k in range(8):
                            sc.add("pe", I("matmul",
                                bank(ub), lhsT=wt[:, 1, k, :], rhs=xn[:, sub, k, :], start=(k == 0), stop=(k == 7)),
                                reads=[(name, "wgu", slot, 1), ((name, "n", sub), "xn", k)], writes=[("ps", ub)])
                        sc.add("act", I("activation", out=sg[sub][:], in_=bank(gb), func=AF.Silu),
                               reads=[("ps", gb)], writes=[(name, "sg", sub)])
                        sc.add("dve", I("tensor_tensor",
                            out=hid[:, f, sub * TT:(sub + 1) * TT], in0=sg[sub][:], in1=bank(ub), op=ALU.mult),
                            reads=[(name, "sg", sub), ("ps", ub)], writes=[(name, "hid", f, sub)])
                for o in range(8):
                    slot = didx[0] % 2
                    didx[0] += 1
                    wt = wdt[slot]
                    sc.add("pool", I("dma_start", out=wt[:], in_=wd_v[:, :, o * 128:(o + 1) * 128]),
                           writes=[(name, "wdt", slot)], dma=(name, "wdt", slot))
                    for sub in range(2):
                        ob = 4 + sub
                        ti = half * 2 + sub
                        tok = ti * TT
                        for f in range(NF):
                            sc.add("pe", I("matmul",
                                bank(ob), lhsT=wt[:, f, :], rhs=hid[:, f, sub * TT:(sub + 1) * TT],
                                start=(f == 0), stop=(f == NF - 1)),
                                reads=[(name, "wdt", slot), (name, "hid", f, sub)], writes=[("ps", ob)])
                        sc.add("dve", I("scalar_tensor_tensor",
                            out=hT[:, o, tok:tok + TT], in0=bank(ob), scalar=0.5, in1=hT[:, o, tok:tok + TT],
                            op0=ALU.mult, op1=ALU.add),
                            reads=[("ps", ob), ("hT", o, ti)], writes=[("hT", o, ti)])
        sc.barrier()

    wst = ExitStack()
    win_pre = None
    if upto >= 2 and "mix" in phases and "f1" in phases:
        win_pre = wst.enter_context(nc.sbuf_tensor("m_win", [128, 8, N_IN], BF16))

    def win_hook():
        win_v = w_in.rearrange("(k p) n -> p k n", p=128)
        for k in range(8):
            sc.add("pool", I("dma_start", out=win_pre[:, k, :], in_=win_v[:, k, :]), writes=[("win", k)], dma=("win", k))

    if "f1" in phases:
        ffn("f1", C_GFFN1, w1g, w1u, w1d, mid_hook=win_hook if win_pre is not None else None)
    if upto >= 2 and "mix" in phases:
        mixer_phase(nc, sc, locals())
    with ExitStack() as pst:
        pw = ple_prefetch(nc, sc, locals(), pst) if (upto >= 4 and "ple" in phases) else None
        if upto >= 3 and "f2" in phases:
            ffn("f2", C_GFFN2, w2g, w2u, w2d)
        if pw is not None:
            ple_phase(nc, sc, locals(), pw)

    for k in range(8):
        sc.add("sp", I("dma_start", out=outT[k * 128:(k + 1) * 128, :], in_=hT[:, k, :]),
               reads=[("hT", k, t) for t in range(NTT)], dma="out")
    sc.emit(nc)
    return nc


MIX_STOP = 99


def mixer_phase(nc, sc, L):
    hT, consts, emats, ones_bf, eps_t, misc, PS = (L[k] for k in ("hT", "consts", "emats", "ones_bf", "eps_t", "misc", "PS"))
    bank, col, rmsnorm_tile = L["bank"], L["col"], L["rmsnorm_tile"]
    ONES32, EMLA, EDIFF, SHIFT = L["ONES32"], L["EMLA"], L["EDIFF"], L["SHIFT"]
    w_in, w_qup, w_kvup, w_out = L["w_in"], L["w_qup"], L["w_kvup"], L["w_out"]
    cs96_d, cs32_d, tlin_d, tabs_d, ftab_d = L["cs96_d"], L["cs32_d"], L["tlin_d"], L["tabs_d"], L["ftab_d"]
    q_mla_d, q_diff_d = L["q_mla_d"], L["q_diff_d"]
    kT_mla_l, kT_diff_l, v_mla_l, v_diff_l = L["kT_mla_l"], L["kT_diff_l"], L["v_mla_l"], L["v_diff_l"]
    kT_mla_g, kT_diff_g, v_mla_g, v_diff_g = L["kT_mla_g"], L["kT_diff_g"], L["v_mla_g"], L["v_diff_g"]
    kT_diff_r, v_diff_r = L["kT_diff_r"], L["v_diff_r"]

    groups = [[0, 1, 2, 3], [4, 5, 6, 7]]
    MLA_PAIRS = [[("v_mla", c, v_mla_l[c], v_mla_g[c]), ("kT_mla", c, kT_mla_l[c], kT_mla_g[c])] for c in range(4)]
    DIFF_PAIRS = [("kT_diff", c, kT_diff_l[c], kT_diff_g[c]) for c in range(2)] + [("v_diff", c, v_diff_l[c], v_diff_g[c]) for c in range(2)]
    ncc = [0]

    def issue_gathers(pairs, explicit_deps, extra=()):
        for (nm, c, a, g) in pairs:
            rd = []
            if explicit_deps and nm == "v_mla":
                rd = [("v_mla_l", c, tb) for tb in range(4)]
            if explicit_deps and nm == "kT_mla":
                rd = [("kT_mla_l", h, part, c) for h in range(8) for part in (0, 1)]
            sc.add("pool", I("collective_compute", "AllGather", ALU.bypass, replica_groups=groups,
                             ins=[a.ap().opt()], outs=[g.ap().opt()]),
                   reads=rd, writes=[("gath", nm, c)], cc="cc%d" % ncc[0], extra=extra)
            ncc[0] += 1

    with ExitStack() as st:
        al = lambda n, s, d: st.enter_context(nc.sbuf_tensor("m_" + n, s, d))
        xn = al("xn", [128, 8, TT], BF16)
        sq8 = al("sq8", [128, 8, TT], BF16)
        sd0 = al("sd0", [128, TT], F32)
        rstd0 = al("rstd0", [128, TT], F32)
        win_pre = L.get("win_pre")
        win = win_pre if win_pre is not None else al("win", [128, 8, N_IN], BF16)
        win_rot = al("winrot", [128, 8, 32], BF16)
        wq = al("wq", [128, 2, 768], BF16)
        wq_rot = al("wqrot", [128, 2, 8, 96], BF16)
        wkv = al("wkv", [128, 1024], BF16)
        wkv_v = al("wkvv", [128, 8, 64], BF16)
        cg = al("cg", [96, 2, T], F32)
        ck = al("ck", [32, 2, T], F32)
        qlat_n = al("qlatn", [128, 2, TT], BF16)
        kvlat_n = al("kvlatn", [128, TT], BF16)
        NS = 4
        sqf = [al("sqf%d" % i, [128, TT], BF16) for i in range(NS)]
        embf = al("embf", [128, 3, 128], BF16)
        sc.add("dve", I("tensor_copy", out=embf[:], in_=emats[:, 0:3, :]), reads=["emats"], writes=["embf"])
        ONESB, EMLAB, EDIFFB = embf[:, 0, :], embf[:, 1, :], embf[:, 2, :]
        NOB = 8
        rsf = [al("rsf%d" % i, [128, TT], F32) for i in range(NS)]
        t1 = [al("t1_%d" % i, [128, TT], F32) for i in range(NS)]
        t2 = [al("t2_%d" % i, [128, TT], F32) for i in range(NS)]
        ob = [al("ob%d" % i, [128, TT], BF16) for i in range(8)]
        vst = [al("vst%d" % i, [128, 512], BF16) for i in range(NS)]

        win_v = w_in.rearrange("(k p) n -> p k n", p=128)
        for k in range(8 if win_pre is None else 0):
            sc.add("pool", I("dma_start", out=win[:, k, :], in_=win_v[:, k, :]), writes=[("win", k)], dma=("win", k))
        for k in range(8):
            sc.add("pool", I("dma_start", out=win_rot[:, k, 0:16], in_=win_v[:, k, OFF_KROPE + 16:OFF_KROPE + 32]),
                   writes=[("winrot", k, 0)], dma="winrot")
            sc.add("pool", I("dma_start", out=win_rot[:, k, 16:32], in_=win_v[:, k, OFF_KROPE:OFF_KROPE + 16]),
                   writes=[("winrot", k, 1)], dma="winrot")
        wq_v = w_qup.rearrange("(k p) n -> p k n", p=128)
        wq_v4 = w_qup.rearrange("(k p) (h d) -> p k h d", p=128, d=96)
        sc.add("dve", I("memset", wq_rot[:], 0.0), writes=["wqrot"])
        for k in range(2):
            sc.add("pool", I("dma_start", out=wq[:, k, :], in_=wq_v[:, k, :]), writes=[("wq", k)], dma="wq")
            sc.add("pool", I("dma_start", out=wq_rot[:, k, :, 64:80], in_=wq_v4[:, k, :, 80:96]),
                   reads=["wqrot"], writes=[("wqrot", k, 0)], dma="wqrot")
            sc.add("pool", I("dma_start", out=wq_rot[:, k, :, 80:96], in_=wq_v4[:, k, :, 64:80]),
                   reads=["wqrot"], writes=[("wqrot", k, 1)], dma="wqrot")
        sc.add("pool", I("dma_start", out=wkv[:], in_=w_kvup), writes=["wkv"], dma="wkv")
        sc.add("pool", I("dma_start", out=wkv_v[:], in_=w_kvup.rearrange("p (h d) -> p h d", d=128)[:, :, 64:128]),
               writes=["wkvv"], dma="wkv")
        sc.add("sp", I("dma_start", out=cg[:], in_=cs96_d), writes=["cg_raw"], dma="cg")
        sc.add("sp", I("dma_start", out=ck[:], in_=cs32_d), writes=["ck_raw"], dma="cg")
        sc.add("dve", I("tensor_scalar", out=cg[:, 0, :], in0=cg[:, 0, :], scalar1=col(C_GMQ, 0, 96), scalar2=None, op0=ALU.mult),
               reads=["cg_raw", "consts"], writes=["cg0"])
        sc.add("dve", I("tensor_scalar", out=cg[:, 1, :], in0=cg[:, 1, :], scalar1=col(C_GMQROT, 0, 96), scalar2=None, op0=ALU.mult),
               reads=["cg_raw", "consts"], writes=["cg1"])
        sc.add("dve", I("tensor_scalar", out=ck[:, 0, :], in0=ck[:, 0, :], scalar1=col(C_GMKR, 0, 32), scalar2=None, op0=ALU.mult),
               reads=["ck_raw", "consts"], writes=["ck0"])
        sc.add("dve", I("tensor_scalar", out=ck[:, 1, :], in0=ck[:, 1, :], scalar1=col(C_GMKRROT, 0, 32), scalar2=None, op0=ALU.mult),
               reads=["ck_raw", "consts"], writes=["ck1"])
        sc.add("dve", I("tensor_tensor", out=misc[0:64, 0:1], in0=col(C_LQ1, 0, 64), in1=col(C_LK1, 0, 64), op=ALU.mult),
               reads=["consts"], writes=["lam_p1"])
        sc.add("dve", I("tensor_tensor", out=misc[0:64, 1:2], in0=col(C_LQ2, 0, 64), in1=col(C_LK2, 0, 64), op=ALU.mult),
               reads=["consts"], writes=["lam_p2"])
        sc.add("pe", I("matmul", PS[:, 0:2], lhsT=emats[0:64, 0, :], rhs=misc[0:64, 0:2], start=True, stop=True),
               reads=["lam_p1", "lam_p2", "emats"], writes=[("ps", 0)])
        sc.add("act", I("activation", out=misc[:, 2:4], in_=PS[:, 0:2], func=AF.Exp), reads=[("ps", 0)], writes=["lam_e"])
        sc.add("dve", I("tensor_tensor", out=misc[:, 4:5], in0=misc[:, 3:4], in1=misc[:, 2:3], op=ALU.subtract),
               reads=["lam_e"], writes=["lam_d"])
        sc.add("dve", I("tensor_scalar", out=misc[:, 5:6], in0=misc[:, 4:5], scalar1=-LAMBDA_INIT, scalar2=None, op0=ALU.add),
               reads=["lam_d"], writes=["neglam"])
        sc.add("dve", I("tensor_scalar", out=misc[:, 6:7], in0=col(C_GSUB), scalar1=1.0 - LAMBDA_INIT, scalar2=None, op0=ALU.mult),
               reads=["consts"], writes=["gsub"])

        rr = [0]
        pb = [0]

        obr = [0]

        def nob():
            obr[0] += 1
            return obr[0] % 8

        def nslot():
            rr[0] += 1
            return rr[0] % NS

        def nbank():
            pb[0] += 1
            return pb[0] % 6

        def norm_stats(src_bank, P, emat_ap, scale, s):
            sb = 6 + (s % 2)
            sc.add("act", I("activation", out=sqf[s][0:P, :], in_=bank(src_bank, 0, P), func=AF.Square),
                   reads=[("ps", src_bank)], writes=[("sqf", s)])
            sc.add("pe", I("matmul", bank(sb, 0, P), lhsT=emat_ap, rhs=sqf[s][0:P, :], start=True, stop=True),
                   reads=[("sqf", s), "embf"], writes=[("ps", sb)])
            sc.add("act", I("activation", out=rsf[s][0:P, :], in_=bank(sb, 0, P), func=AF.Ln, bias=eps_t[0:P, :], scale=scale),
                   reads=[("ps", sb), "eps"], writes=[("rsf", s)])
            sc.add("act", I("activation", out=rsf[s][0:P, :], in_=rsf[s][0:P, :], func=AF.Exp, scale=-0.5), reads=[("rsf", s)], writes=[("rsf", s)])

        def sq_part(src_bank, P, s):
            sc.add("act", I("activation", out=sqf[s][0:P, :], in_=bank(src_bank, 0, P), func=AF.Square),
                   reads=[("ps", src_bank)], writes=[("sqf", s)])

        def stat_part(P, emat_ap, scale, s):
            sb = 6 + (s % 2)
            sc.add("pe", I("matmul", bank(sb, 0, P), lhsT=emat_ap, rhs=sqf[s][0:P, :], start=True, stop=True),
                   reads=[("sqf", s), "embf"], writes=[("ps", sb)])
            sc.add("act", I("activation", out=rsf[s][0:P, :], in_=bank(sb, 0, P), func=AF.Ln, bias=eps_t[0:P, :], scale=scale),
                   reads=[("ps", sb), "eps"], writes=[("rsf", s)])
            sc.add("act", I("activation", out=rsf[s][0:P, :], in_=rsf[s][0:P, :], func=AF.Exp, scale=-0.5), reads=[("rsf", s)], writes=[("rsf", s)])

        def run_chains(chains):
            prev = None
            for ch in chains:
                ch[0]()
                if prev is not None:
                    prev[1]()
                prev = ch
            if prev is not None:
                prev[1]()

        xk = lambda k: (("mx",), "xn", k)

        def proj_in(b, c0, M):
            for k in range(8):
                sc.add("pe", I("matmul", bank(b, 0, M), lhsT=win[:, k, c0:c0 + M], rhs=xn[:, k, :], start=(k == 0), stop=(k == 7)),
                       reads=[("win", k), xk(k)], writes=[("ps", b)])

        def mk_A(ti):
            st_ = {}

            def A():
                st_["b"] = (nbank(), nbank())
                st_["s"] = (nslot(), nslot())
                for c in range(2):
                    proj_in(st_["b"][c], OFF_QLAT + c * 128, 128)
                    sq_part(st_["b"][c], 128, st_["s"][c])

            def B():
                (b0, b1), (s0, s1) = st_["b"], st_["s"]
                sc.add("pe", I("matmul", bank(6), lhsT=ONESB, rhs=sqf[s0][:], start=True, stop=False), reads=[("sqf", s0), "embf"], writes=[("ps", 6)])
                sc.add("pe", I("matmul", bank(6), lhsT=ONESB, rhs=sqf[s1][:], start=False, stop=True), reads=[("sqf", s1), "embf"], writes=[("ps", 6)])
                sc.add("act", I("activation", out=rsf[s0][:], in_=bank(6), func=AF.Ln, bias=eps_t[:], scale=1.0 / 256),
                       reads=[("ps", 6), "eps"], writes=[("rsf", s0)])
                sc.add("act", I("activation", out=rsf[s0][:], in_=rsf[s0][:], func=AF.Exp, scale=-0.5), reads=[("rsf", s0)], writes=[("rsf", s0)])
                for c, b in ((0, b0), (1, b1)):
                    sc.add("dve", I("scalar_tensor_tensor", out=qlat_n[:, c, :], in0=bank(b), scalar=col(C_GQLAT + c), in1=rsf[s0][:],
                                    op0=ALU.mult, op1=ALU.mult),
                           reads=[("ps", b), ("rsf", s0), "consts"], writes=[("qlatn", c)])
            return A, B

        def mk_B(ti):
            st_ = {}

            def A():
                st_["b"], st_["s"] = nbank(), nslot()
                proj_in(st_["b"], OFF_KVLAT, 128)
                sq_part(st_["b"], 128, st_["s"])

            def B():
                b, s = st_["b"], st_["s"]
                stat_part(128, ONESB, 1.0 / 128, s)
                sc.add("dve", I("scalar_tensor_tensor", out=kvlat_n[:], in0=bank(b), scalar=col(C_GKVLAT), in1=rsf[s][:], op0=ALU.mult, op1=ALU.mult),
                       reads=[("ps", b), ("rsf", s), "consts"], writes=["kvlatn"])
            return A, B

        def rope_tail(P, bq, br, s, tab, tsl, k0, k1, eng="pool"):
            sc.add("dve", I("tensor_tensor", out=t1[s][0:P, :], in0=bank(bq, 0, P), in1=tab[:, 0, tsl], op=ALU.mult),
                   reads=[("ps", bq), k0], writes=[("t1", s)])
            sc.add("dve", I("tensor_tensor", out=t2[s][0:P, :], in0=bank(br, 0, P), in1=tab[:, 1, tsl], op=ALU.mult),
                   reads=[("ps", br), k1], writes=[("t2", s)])
            sc.add(eng, I("tensor_tensor", out=t1[s][0:P, :], in0=t1[s][0:P, :], in1=t2[s][0:P, :], op=ALU.add),
                   reads=[("t1", s), ("t2", s)], writes=[("t1", s)])
            o_ = nob()
            sc.add(eng, I("tensor_tensor", out=ob[o_][0:P, :], in0=t1[s][0:P, :], in1=rsf[s][0:P, :], op=ALU.mult),
                   reads=[("t1", s), ("rsf", s)], writes=[("ob", o_)])
            return o_

        def mk_C(ti, h):
            st_ = {}
            tsl = slice(ti * TT, (ti + 1) * TT)

            def A():
                bq, br, s = nbank(), nbank(), nslot()
                st_.update(bq=bq, br=br, s=s)
                for c in range(2):
                    sc.add("pe", I("matmul", bank(bq, 0, 96), lhsT=wq[:, c, h * 96:(h + 1) * 96], rhs=qlat_n[:, c, :], start=(c == 0), stop=(c == 1)),
                           reads=[("wq", c), ("qlatn", c)], writes=[("ps", bq)])
                for c in range(2):
                    sc.add("pe", I("matmul", bank(br, 0, 96), lhsT=wq_rot[:, c, h, :], rhs=qlat_n[:, c, :], start=(c == 0), stop=(c == 1)),
                           reads=[("wqrot", c, 0), ("wqrot", c, 1), ("qlatn", c)], writes=[("ps", br)])
                sq_part(bq, 96, s)

            def B():
                bq, br, s = st_["bq"], st_["br"], st_["s"]
                stat_part(96, EMLAB[0:96, 0:96], 1.0, s)
                o_ = rope_tail(96, bq, br, s, cg, tsl, "cg0", "cg1")
                sc.add("sp", I("dma_start", out=q_mla_d[h * 96:(h + 1) * 96, tsl], in_=ob[o_][0:96, :]),
                       reads=[("ob", o_)], writes=[("q_mla_d", h, ti)], dma=("st", o_))
            return A, B

        def mk_D(ti, h):
            st_ = {}
            tsl = slice(ti * TT, (ti + 1) * TT)

            def A():
                bk, s = nbank(), nslot()
                st_.update(bk=bk, s=s)
                sc.add("pe", I("matmul", bank(bk, 0, 64), lhsT=wkv[:, h * 128:h * 128 + 64], rhs=kvlat_n[:], start=True, stop=True),
                       reads=["wkv", "kvlatn"], writes=[("ps", bk)])
                sq_part(bk, 64, s)

            def B():
                bk, s = st_["bk"], st_["s"]
                stat_part(64, ONESB[0:64, 0:64], 1.0 / 64, s)
                o_ = nob()
                sc.add("dve", I("scalar_tensor_tensor", out=ob[o_][0:64, :], in0=bank(bk, 0, 64), scalar=col(C_GMKN, 0, 64), in1=rsf[s][0:64, :],
                                op0=ALU.mult, op1=ALU.mult),
                       reads=[("ps", bk), ("rsf", s), "consts"], writes=[("ob", o_)])
                sc.add("sp", I("dma_start", out=kT_mla_l[ti].ap()[h * 96:h * 96 + 64, :], in_=ob[o_][0:64, :]),
                       reads=[("ob", o_)], writes=[("kT_mla_l", h, 0, ti)], dma=("st", o_))
            return A, B

        def mk_KR(ti):
            st_ = {}
            tsl = slice(ti * TT, (ti + 1) * TT)

            def A():
                bq, br, s = nbank(), nbank(), nslot()
                st_.update(bq=bq, br=br, s=s)
                for k in range(8):
                    sc.add("pe", I("matmul", bank(bq, 0, 32), lhsT=win[:, k, OFF_KROPE:OFF_KROPE + 32], rhs=xn[:, k, :], start=(k == 0), stop=(k == 7)),
                           reads=[("win", k), xk(k)], writes=[("ps", bq)])
                for k in range(8):
                    sc.add("pe", I("matmul", bank(br, 0, 32), lhsT=win_rot[:, k, :], rhs=xn[:, k, :], start=(k == 0), stop=(k == 7)),
                           reads=[("winrot", k, 0), ("winrot", k, 1), xk(k)], writes=[("ps", br)])
                sq_part(bq, 32, s)

            def B():
                bq, br, s = st_["bq"], st_["br"], st_["s"]
                stat_part(32, ONESB[0:32, 0:32], 1.0 / 32, s)
                o_ = rope_tail(32, bq, br, s, ck, tsl, "ck0", "ck1", eng="dve")
                for h in range(8):
                    sc.add("sp", I("dma_start", out=kT_mla_l[ti].ap()[h * 96 + 64:h * 96 + 96, :], in_=ob[o_][0:32, :]),
                           reads=[("ob", o_)], writes=[("kT_mla_l", h, 1, ti)], dma=("st", o_))
            return A, B

        def mk_FG(ti, off, gc, dname, h):
            st_ = {}
            tsl = slice(ti * TT, (ti + 1) * TT)
            dst_ap = (q_diff_d[h * 128:(h + 1) * 128, tsl] if dname == "q_diff_d"
                      else kT_diff_l[h // 2].ap()[(h % 2) * 128:(h % 2 + 1) * 128, tsl])

            def A():
                b, s = nbank(), nslot()
                st_.update(b=b, s=s)
                proj_in(b, off + h * 128, 128)
                sq_part(b, 128, s)

            def B():
                b, s = st_["b"], st_["s"]
                stat_part(128, EDIFFB, 1.0, s)
                o_ = nob()
                sc.add("dve", I("scalar_tensor_tensor", out=ob[o_][:], in0=bank(b), scalar=col(gc), in1=rsf[s][:], op0=ALU.mult, op1=ALU.mult),
                       reads=[("ps", b), ("rsf", s), "consts"], writes=[("ob", o_)])
                sc.add("sp", I("dma_start", out=dst_ap, in_=ob[o_][:]), reads=[("ob", o_)], writes=[(dname, h, ti)], dma=("st", o_))
            return A, B

        def mk_V(ti, tb, kind):
            st_ = {}
            tok = ti * TT
            row = (tok + tb * 128) % 1024
            dst = (v_mla_l[ti].ap()[tb * 128:(tb + 1) * 128, :] if kind == "mla"
                   else v_diff_l[(tok + tb * 128) // 1024].ap()[row:row + 128, :])

            def A():
                bv = nbank()
                st_.update(bv=bv)
                if kind == "mla":
                    sc.add("pe", I("matmul", bank(bv), lhsT=kvlat_n[:, tb * 128:(tb + 1) * 128], rhs=wkv_v[:].rearrange("p h d -> p (h d)"),
                                   start=True, stop=True),
                           reads=["kvlatn", "wkvv"], writes=[("ps", bv)])
                else:
                    for k in range(8):
                        sc.add("pe", I("matmul", bank(bv), lhsT=xn[:, k, tb * 128:(tb + 1) * 128], rhs=win[:, k, OFF_VD:OFF_VD + 512],
                                       start=(k == 0), stop=(k == 7)),
                               reads=[("win", k), xk(k)], writes=[("ps", bv)])

            def B():
                bv, s = st_["bv"], nslot()
                sc.add("act", I("activation", out=vst[s][:], in_=bank(bv), func=AF.Copy), reads=[("ps", bv)], writes=[("vst", s)])
                sc.add("sp", I("dma_start", out=dst, in_=vst[s][:]), reads=[("vst", s)],
                       writes=[("v_mla_l" if kind == "mla" else "v_diff_l", ti, tb)], dma=("stv", s))
            return A, B

        for want in ("BDE", "ACFGH"):
            for ti in range(NTT):
                rmsnorm_tile(ti * TT, C_GMIX, xn, sq8, sd0, rstd0, 7, ("mx",))
                if want == "BDE":
                    chains = [mk_B(ti), mk_KR(ti)] + [mk_D(ti, h) for h in range(8)] + [mk_V(ti, tb, "mla") for tb in range(4)]
                else:
                    chains = ([mk_A(ti)] + [mk_FG(ti, OFF_QD, C_GDQ, "q_diff_d", h) for h in range(4)]
                              + [mk_FG(ti, OFF_KD, C_GDK, "kT_diff_l", h) for h in range(4)]
                              + [mk_C(ti, h) for h in range(8)] + [mk_V(ti, tb, "diff") for tb in range(4)])
                run_chains(chains)
                if want == "BDE":
                    issue_gathers(MLA_PAIRS[ti], True)
    sc.barrier()
    L["wst"].close()
    if MIX_STOP <= 1:
        return

    if MIX_STOP <= 1.5:
        return

    def krot(e, sg, c):
        rho = (_pid(e) % 4 + sg) % 4
        k4 = kT_diff_g[c].ap().rearrange("(rk p) t -> rk p t", rk=4)
        return e.dma_start(out=kT_diff_r.ap()[sg * 512 + c * 256:sg * 512 + (c + 1) * 256, :], in_=k4[bass.ds(rho, 1), :, :])

    def vrot(e, sg, c):
        rho = (_pid(e) % 4 + sg) % 4
        v4 = v_diff_g[c].ap().rearrange("(rk t) c -> rk t c", rk=4)
        return e.dma_start(out=v_diff_r.ap()[sg * T + c * 1024:sg * T + (c + 1) * 1024, :], in_=v4[bass.ds(rho, 1), :, :])

    def record_rot():
        for sg in range(4):
            for c in range(2):
                sc.add("sp", lambda e, sg=sg, c=c: krot(e, sg, c), reads=[("gath", "kT_diff", c)], writes=["kT_diff_r"], dma="rot")
                sc.add("sp", lambda e, sg=sg, c=c: vrot(e, sg, c), reads=[("gath", "v_diff", c)], writes=["v_diff_r"], dma="rot")

    if MIX_STOP <= 2:
        record_rot()
        sc.barrier()
        return

    with ExitStack() as st0:
      mixT = st0.enter_context(nc.sbuf_tensor("a_mixT", [128, 8, T], BF16))
      with ExitStack() as st:
        al = lambda n, s, d: st.enter_context(nc.sbuf_tensor("a_" + n, s, d))
        kbuf = [al("kbuf%d" % i, [128, S], BF16) for i in range(2)]
        vbuf = [al("vbuf%d" % i, [128, NKB, 128], BF16) for i in range(2)]
        qbuf = [al("qbuf0", [128, T], BF16)] * 2
        NPT = 3
        pT_ = [al("pT%d" % i, [128, 2 * TT], BF16) for i in range(NPT)]
        tmp = [al("tmp%d" % i, [128, 2 * TT], F32) for i in range(2)]
        tlin = al("tlin", [128, TT], F32)
        tabs = al("tabs", [128, 896], F32)
        ftab = al("ftab", [128, NFT], F32)
        osb = [al("osb%d" % i, [128, TT], F32) for i in range(2)]
        rec = [al("rec%d" % i, [128, TT], F32) for i in range(2)]
        ea = [al("ea%d" % i, [128, TT], F32) for i in range(2)]
        ostage = [al("ost0", [64, TT], BF16)] * 2

        sc.add("sp", I("dma_start", out=tlin[:], in_=tlin_d), writes=["tlin"], dma="atab")
        sc.add("sp", I("dma_start", out=tabs[:], in_=tabs_d), writes=["tabs"], dma="atab")
        sc.add("sp", I("dma_start", out=ftab[:], in_=ftab_d), writes=["ftab"], dma="atab")
        for i in range(2):
            sc.add("dve", I("memset", vbuf[i][:, :, 64:128], 1.0), writes=[("vbuf", i, "ones")])

        def load_mla(h, i):
            for c in range(NTT):
                for r in range(4):
                    j0 = c * 16 + r * 4
                    sc.add("sp", I("dma_start", out=kbuf[i][0:96, j0 * 128:(j0 + 4) * 128],
                                   in_=kT_mla_g[c].ap()[r * 768 + h * 96:r * 768 + (h + 1) * 96, :]),
                           reads=[("gath", "kT_mla", c)], writes=[("kbuf", i, c)], dma=("kb", i, c))
                    vsrc = v_mla_g[c].ap()[r * TT:(r + 1) * TT, h * 64:(h + 1) * 64].rearrange("(kb p) d -> p kb d", p=128)
                    sc.add("sp", I("dma_start", out=vbuf[i][:, j0:j0 + 4, 0:64], in_=vsrc),
                           reads=[("vbuf", i, "ones"), ("gath", "v_mla", c)], writes=[("vbuf", i, c)], dma=("vb", i, c))

        def load_mla_q(h):
            sc.add("sp", I("dma_start", out=qbuf[0][0:96, :], in_=q_mla_d[h * 96:(h + 1) * 96, :]),
                   writes=[("qbuf", 0)], dma=("qb", 0))

        steps = [(h, qt, g) for h in range(8) for qt in range(NTT) for g in range(NKB // 2)]

        def mla_qk(si):
            h, qt, g = steps[si]
            i = h % 2
            sb = (si % 3) * 2
            for j in range(2):
                kb = g * 2 + j
                sc.add("pe", I("matmul", bank(sb + j), lhsT=kbuf[i][0:96, kb * 128:(kb + 1) * 128],
                                                           rhs=qbuf[i][0:96, qt * TT:(qt + 1) * TT], start=True, stop=True),
                       reads=[("kbuf", i, kb // 16), ("qbuf", 0)], writes=[("ps", sb + j)])

        def mla_exp_av(si):
            h, qt, g = steps[si]
            i = h % 2
            sb = (si % 3) * 2
            ps_ = si % NPT
            it = h * NTT + qt
            obk = 6 + (it % 2)
            sc.add("act", I("activation", out=pT_[ps_][:], in_=bank(sb, n=2), func=AF.Exp, scale=MLA_SCALE),
                   reads=[("ps", sb), ("ps", sb + 1)], writes=[("pT", ps_)])
            for j in range(2):
                kb = g * 2 + j
                sc.add("pe", I("matmul", bank(obk), lhsT=vbuf[i][:, kb, :], rhs=pT_[ps_][:, j * TT:(j + 1) * TT],
                                                           start=(kb == 0), stop=(kb == NKB - 1)),
                       reads=[("vbuf", i, kb // 16), ("pT", ps_)], writes=[("ps", obk)])
            if g == NKB // 2 - 1:
                e2 = it % 2
                db = obk
                sc.add("act", I("activation", out=osb[e2][:], in_=bank(obk), func=AF.Copy),
                       reads=[("ps", obk)], writes=[("osb", e2)])
                sc.add("pe", I("matmul", bank(db, 0, 64), lhsT=SHIFT[:, 0:64], rhs=osb[e2][:], start=True, stop=True),
                       reads=[("osb", e2), "emats"], writes=[("ps", db)])
                sc.add("dve", I("reciprocal", out=rec[e2][0:64, :], in_=bank(db, 0, 64)), reads=[("ps", db)], writes=[("rec", e2)])
                sc.add("dve", I("tensor_tensor", out=ostage[e2][:], in0=osb[e2][0:64, :], in1=rec[e2][0:64, :], op=ALU.mult),
                       reads=[("osb", e2), ("rec", e2)], writes=[("ost", 0)])
                p0 = (h % 2) * 64
                sc.add("sp", I("dma_start", out=mixT[p0:p0 + 64, h // 2, qt * TT:(qt + 1) * TT], in_=ostage[e2][:]),
                       reads=[("ost", 0)], writes=[("mixT", h // 2, qt, h % 2)], dma=("ostd", 0))

        PF = 2
        load_mla_q(0)
        load_mla(0, 0)
        for si in range(len(steps) if MIX_STOP >= 3 else 0):
            h, qt, g = steps[si]
            if qt == 0 and g == 0 and h + 1 < 8:
                load_mla(h + 1, (h + 1) % 2)
            if qt == 0 and g == 0 and 1 <= h <= 4:
                nb = (h + 1) % 2
                marker = sc.add("dve", I("memset", misc[:, 7:8], 0.0),
                                reads=[("kbuf", nb, c) for c in range(NTT)] + [("vbuf", nb, c) for c in range(NTT)], writes=["marker"])
                issue_gathers([DIFF_PAIRS[h - 1]], False, extra=[marker])
            if qt == 0 and g == 0 and h == 6:
                record_rot()
            if si == 0:
                for p in range(min(PF, len(steps))):
                    mla_qk(p)
            if si + PF < len(steps):
                if steps[si + PF][0] != steps[si + PF - 1][0]:
                    load_mla_q(steps[si + PF][0])
                mla_qk(si + PF)
            mla_exp_av(si)

        def load_diff(h, i):
            for sg in range(4):
                sc.add("sp", I("dma_start", out=kbuf[i][:, sg * T:(sg + 1) * T],
                               in_=kT_diff_r.ap()[sg * 512 + h * 128:sg * 512 + (h + 1) * 128, :]),
                       reads=["kT_diff_r"], writes=[("kbuf", i)] + [("kbuf", i, c) for c in range(NTT)], dma=("kbd", i))
            vsrc = v_diff_r.ap().rearrange("(kb p) (h d) -> p kb h d", p=128, d=128)
            for q4 in range(4):
                sc.add("sp", I("dma_start", out=vbuf[i][:, q4 * 16:(q4 + 1) * 16, :], in_=vsrc[:, q4 * 16:(q4 + 1) * 16, h, :]),
                       reads=["v_diff_r"], writes=[("vbuf", i)] + [("vbuf", i, c) for c in range(NTT)], dma=("vbd", i))

        def load_diff_q(h):
            sc.add("sp", I("dma_start", out=qbuf[0][:], in_=q_diff_d[h * 128:(h + 1) * 128, :]),
                   writes=[("qbuf", 0)], dma=("qb", 0))

        dsteps = []
        for h in range(4):
            for qt in range(NTT):
                js = [j for j in range(NKB) if _tile_needed(h, qt, j)]
                for n, j in enumerate(js):
                    dsteps.append((h, qt, j, n == 0, n == len(js) - 1))

        def diff_qk(si):
            h, qt, j, first, last = dsteps[si]
            i = h % 2
            for m in range(2):
                sb = (2 * si + m) % 4
                sc.add("pe", I("matmul", bank(sb), lhsT=kbuf[i][m * 64:(m + 1) * 64, j * 128:(j + 1) * 128],
                                                     rhs=qbuf[i][m * 64:(m + 1) * 64, qt * TT:(qt + 1) * TT], start=True, stop=True),
                       reads=[("kbuf", i), ("qbuf", 0)], writes=[("ps", sb)])

        def diff_rest(si):
            h, qt, j, first, last = dsteps[si]
            i = h % 2
            sl = SLOPES[h]
            sg, b = j // 16, j % 16
            if sg == 0 and 4 * qt <= b < 4 * qt + 4:
                off = 384 - 128 * (b - 4 * qt)
                in0, scal, abias, rd = tabs[:, off:off + TT], 8.0 * sl, 0.0, ["tabs"]
            elif sg == 0:
                sgn = 1.0 if b < 4 * qt else -1.0
                in0, scal, abias, rd = tlin[:], -8.0 * sl * sgn, -sl * sgn * float(TT * qt - 128 * b), ["tlin"]
            else:
                c_sig = h * 3 + (sg - 1)
                c_cb = 12 + ((h * 3 + (sg - 1)) * NTT + qt) * 16 + b
                in0, scal, abias, rd = tlin[:], ftab[:, c_sig:c_sig + 1], ftab[:, c_cb:c_cb + 1], ["tlin", "ftab"]
            for m in range(2):
                sb = (2 * si + m) % 4
                sl4 = (2 * si + m) % 4
                tv = tmp[sl4 // 2][:, (sl4 % 2) * TT:(sl4 % 2 + 1) * TT]
                pv = pT_[sl4 // 2][:, (sl4 % 2) * TT:(sl4 % 2 + 1) * TT]
                sc.add("dve", I("scalar_tensor_tensor", out=tv, in0=in0, scalar=scal, in1=bank(sb),
                                op0=ALU.mult, op1=ALU.add),
                       reads=rd + [("ps", sb)], writes=[("tmp", sl4)])
                sc.add("act", I("activation", out=pv, in_=tv, func=AF.Exp, bias=abias, scale=0.125),
                       reads=[("tmp", sl4), "ftab"], writes=[("pTd", sl4)])
            if si + PF < len(dsteps):
                if dsteps[si + PF][0] != dsteps[si + PF - 1][0]:
                    load_diff_q(dsteps[si + PF][0])
                diff_qk(si + PF)
            for m in range(2):
                sl4 = (2 * si + m) % 4
                pv = pT_[sl4 // 2][:, (sl4 % 2) * TT:(sl4 % 2 + 1) * TT]
                sc.add("pe", I("matmul", bank(4 + 2 * m), lhsT=vbuf[i][:, j, :], rhs=pv, start=first, stop=last),
                       reads=[("vbuf", i), ("pTd", sl4)], writes=[("ps", 4 + 2 * m)])
                sc.add("pe", I("matmul", bank(5 + 2 * m), lhsT=ones_bf[:], rhs=pv, start=first, stop=last),
                       reads=["ones_bf", ("pTd", sl4)], writes=[("ps", 5 + 2 * m)])
            if last:
                for m in range(2):
                    sc.add("act", I("activation", out=rec[m][:], in_=bank(5 + 2 * m), func=AF.Ln), reads=[("ps", 5 + 2 * m)], writes=[("rec", m)])
                    sc.add("act", I("activation", out=rec[m][:], in_=rec[m][:], func=AF.Exp, scale=-1.0), reads=[("rec", m)], writes=[("rec", m)])
                    sc.add("dve", I("tensor_tensor", out=ea[m][:], in0=bank(4 + 2 * m), in1=rec[m][:], op=ALU.mult),
                           reads=[("ps", 4 + 2 * m), ("rec", m)], writes=[("ea", m)])
                sc.add("dve", I("scalar_tensor_tensor", out=osb[0][:], in0=ea[1][:], scalar=misc[:, 5:6], in1=ea[0][:],
                                                               op0=ALU.mult, op1=ALU.add),
                       reads=[("ea", 0), ("ea", 1), "neglam"], writes=[("osb", 0)])
                sc.add("act", I("activation", out=osb[1][:], in_=osb[0][:], func=AF.Square), reads=[("osb", 0)], writes=[("osb", 1)])
                sc.add("pe", I("matmul", bank(5), lhsT=ONES32, rhs=osb[1][:], start=True, stop=True),
                       reads=[("osb", 1), "emats"], writes=[("ps", 5)])
                sc.add("act", I("activation", out=rec[0][:], in_=bank(5), func=AF.Ln, bias=eps_t[:], scale=1.0 / 128),
                       reads=[("ps", 5), "eps"], writes=[("rec", 0)])
                sc.add("act", I("activation", out=rec[1][:], in_=rec[0][:], func=AF.Exp, scale=-0.5), reads=[("rec", 0)], writes=[("rec", 1)])
                sc.add("dve", I("scalar_tensor_tensor", out=mixT[:, 4 + h, qt * TT:(qt + 1) * TT], in0=osb[0][:], scalar=misc[:, 6:7],
                                                               in1=rec[1][:], op0=ALU.mult, op1=ALU.mult),
                       reads=[("osb", 0), ("rec", 1), "gsub"], writes=[("mixT", 4 + h, qt, 0), ("mixT", 4 + h, qt, 1)])

        load_diff(0, 0)
        load_diff_q(0)
        for si in range(len(dsteps) if MIX_STOP >= 4 else 0):
            h, qt, j, first, last = dsteps[si]
            if qt == 0 and first and h + 1 < 4:
                load_diff(h + 1, (h + 1) % 2)
            if si == 0:
                for p in range(min(PF, len(dsteps))):
                    diff_qk(p)
            diff_rest(si)

      sc.barrier()
      with ExitStack() as st:
        wout = st.enter_context(nc.sbuf_tensor("a_wout", [128, 8, D], BF16))
        wout_v = w_out.rearrange("(k p) n -> p k n", p=128)
        for k in range(8):
            sc.add("pool", I("dma_start", out=wout[:, k, :], in_=wout_v[:, k, :]), writes=[("wout", k)], dma=("wout", k))
        for ti in range(NTT):
            for o in range(8):
                b = (ti * 8 + o) % 4
                for c in range(8):
                    sc.add("pe", I("matmul", bank(b), lhsT=wout[:, c, o * 128:(o + 1) * 128],
                                                                rhs=mixT[:, c, ti * TT:(ti + 1) * TT], start=(c == 0), stop=(c == 7)),
                           reads=[("wout", c), ("mixT", c, ti, 0), ("mixT", c, ti, 1)], writes=[("ps", b)])
                sc.add("dve", I("tensor_tensor", out=hT[:, o, ti * TT:(ti + 1) * TT], in0=bank(b),
                                                               in1=hT[:, o, ti * TT:(ti + 1) * TT], op=ALU.add),
                       reads=[("ps", b), ("hT", o, ti)], writes=[("hT", o, ti)])
    sc.barrier()


def ple_prefetch(nc, sc, L, st):
    w_pg, w_pp, pT = L["w_pg"], L["w_pp"], L["pT"]
    wpg = st.enter_context(nc.sbuf_tensor("p_wpg", [128, 8, D], BF16))
    wpp = st.enter_context(nc.sbuf_tensor("p_wpp", [128, 2, D], BF16))
    pbf = st.enter_context(nc.sbuf_tensor("p_pbf", [128, 2, T], BF16))
    wpg_v = w_pg.rearrange("(k p) n -> p k n", p=128)
    for k in range(8):
        sc.add("pool", I("dma_start", out=wpg[:, k, :], in_=wpg_v[:, k, :]), writes=[("wpg", k)], dma="wpg")
    wpp_v = w_pp.rearrange("(k p) n -> p k n", p=128)
    pT_v = pT.rearrange("(k p) t -> p k t", p=128)
    for k in range(2):
        sc.add("pool", I("dma_start", out=wpp[:, k, :], in_=wpp_v[:, k, :]), writes=[("wpp", k)], dma="wpp")
        for q4 in range(4):
            sc.add("pool", I("dma_start", out=pbf[:, k, q4 * TT:(q4 + 1) * TT], in_=pT_v[:, k, q4 * TT:(q4 + 1) * TT]),
                   writes=[("pbf", k)], dma="pbf")
    return wpg, wpp, pbf


def ple_phase(nc, sc, L, pw):
    hT, consts, emats, ones_bf, eps_t, PS = (L[k] for k in ("hT", "consts", "emats", "ones_bf", "eps_t", "PS"))
    bank, col, rmsnorm_tile = L["bank"], L["col"], L["rmsnorm_tile"]
    ONES32 = L["ONES32"]
    wpg, wpp, pbf = pw
    with ExitStack() as st:
        al = lambda n, s, d: st.enter_context(nc.sbuf_tensor("p_" + n, s, d))
        xn = [al("xn%d" % i, [128, 8, TT], BF16) for i in range(2)]
        sq8 = al("sq8", [128, 8, TT], BF16)
        sd0 = al("sd0", [128, TT], F32)
        rstd0 = al("rstd0", [128, TT], F32)
        gate = [al("gate0", [128, 8, TT], F32)] * 2
        esb = [al("esb%d" % i, [128, 8, TT], F32) for i in range(2)]
        esq = [al("esq0", [128, 8, TT], F32)] * 2
        sd1 = [al("sd1_%d" % i, [128, TT], F32) for i in range(2)]
        rs1 = [al("rs1_%d" % i, [128, TT], F32) for i in range(2)]
        for ti in range(NTT):
            tok = ti * TT
            z = ti % 2
            for o in range(8):
                b = 3 + (o % 3)
                for k in range(2):
                    sc.add("pe", I("matmul", bank(b), lhsT=wpp[:, k, o * 128:(o + 1) * 128], rhs=pbf[:, k, tok:tok + TT],
                                   start=(k == 0), stop=(k == 1)),
                           reads=[("wpp", k), ("pbf", k)], writes=[("ps", b)])
                sc.add("dve", I("tensor_scalar", out=esb[z][:, o, :], in0=bank(b), scalar1=col(C_GPLEOUT + o), scalar2=None, op0=ALU.mult),
                       reads=[("ps", b), "consts"], writes=[("esb", z, o)])
                sc.add("act", I("activation", out=esq[z][:, o, :], in_=bank(b), func=AF.Square), reads=[("ps", b)], writes=[("esq", 0, o)])
            for o in range(8):
                sc.add("pe", I("matmul", bank(6), lhsT=ONES32, rhs=esq[z][:, o, :], start=(o == 0), stop=(o == 7)),
                       reads=[("esq", 0, o), "emats"], writes=[("ps", 6)])
            sc.add("act", I("activation", out=sd1[z][:], in_=bank(6), func=AF.Ln, bias=eps_t[:], scale=1.0 / D),
                   reads=[("ps", 6), "eps"], writes=[("sd1", z)])
            sc.add("act", I("activation", out=rs1[z][:], in_=sd1[z][:], func=AF.Exp, scale=-0.5), reads=[("sd1", z)], writes=[("rs1", z)])
            rmsnorm_tile(tok, C_GPLEIN, xn[z], sq8, sd0, rstd0, 7, ("pl", z), scr=("pl", "scr"))
            for o in range(8):
                b = o % 3
                for k in range(8):
                    sc.add("pe", I("matmul", bank(b), lhsT=wpg[:, k, o * 128:(o + 1) * 128], rhs=xn[z][:, k, :],
                                   start=(k == 0), stop=(k == 7)),
                           reads=[("wpg", k), (("pl", z), "xn", k)], writes=[("ps", b)])
                sc.add("act", I("activation", out=gate[z][:, o, :], in_=bank(b), func=AF.Sigmoid, bias=col(C_BPLE + o)),
                       reads=[("ps", b), "consts"], writes=[("gate", 0, o)])
                sc.add("pool", I("tensor_tensor", out=esb[z][:, o, :], in0=esb[z][:, o, :], in1=rs1[z][:], op=ALU.mult),
                       reads=[("esb", z, o), ("rs1", z)], writes=[("esb", z, o)])
                sc.add("dve", I("tensor_tensor", out=esb[z][:, o, :], in0=esb[z][:, o, :], in1=gate[z][:, o, :], op=ALU.mult),
                       reads=[("esb", z, o), ("gate", 0, o)], writes=[("esb", z, o)])
                sc.add("dve", I("tensor_tensor", out=hT[:, o, tok:tok + TT], in0=esb[z][:, o, :], in1=hT[:, o, tok:tok + TT], op=ALU.add),
                       reads=[("esb", z, o), ("hT", o, ti)], writes=[("hT", o, ti)])
    sc.barrier()


def _const_tables(core):
    r = core % 4
    pos = (np.arange(T, dtype=np.float32) + np.float32(r * T)).astype(np.float32)
    inv = (np.float32(10000.0) ** (-np.arange(0, 32, 2, dtype=np.float32) / np.float32(32))).astype(np.float32)
    ang = pos[:, None] * inv[None, :]
    ang = np.concatenate([ang, ang], axis=-1)
    cos = np.cos(ang).astype(np.float32).T
    sin = np.sin(ang).astype(np.float32).T
    sin_signed = sin.copy()
    sin_signed[0:16] *= -1.0
    cs32 = np.stack([cos, sin_signed], axis=1)
    cs96 = np.zeros((96, 2, T), np.float32)
    cs96[0:64, 0, :] = 1.0
    cs96[64:96] = cs32
    ip = np.arange(TT, dtype=np.float32)[None, :]
    jp = np.arange(128, dtype=np.float32)[:, None]
    tlin = (ip - jp).astype(np.float32)
    xx = np.arange(896, dtype=np.float32)[None, :]
    tabs = (-np.abs(xx - 384.0 - jp)).astype(np.float32)
    ftab = np.zeros((128, NFT), np.float32)
    for h in range(4):
        for sg in (1, 2, 3):
            rho = (r + sg) % 4
            sgn = 1.0 if r > rho else -1.0
            ftab[:, h * 3 + (sg - 1)] = -8.0 * SLOPES[h] * sgn
            for qt in range(NTT):
                for b in range(16):
                    x0 = float(T * (r - rho) + TT * qt - 128 * b)
                    ftab[:, 12 + ((h * 3 + (sg - 1)) * NTT + qt) * 16 + b] = -SLOPES[h] * sgn * x0
    return cs96, cs32, tlin, tabs, ftab


def _emats():
    e = np.zeros((128, 4, 128), np.float32)
    e[:, 0, :] = 1.0
    e[0:64, 1, 0:64] = 1.0 / 64
    e[64:96, 1, 64:96] = 1.0 / 32
    e[0:64, 2, 0:64] = 1.0 / 64
    e[64:128, 2, 64:128] = 1.0 / 64
    for i in range(64):
        e[64 + i, 3, i] = 1.0
    return e


def _consts(inp):
    c = np.zeros((128, NCONST), np.float32)

    def chunks(v):
        return np.asarray(v, np.float32).reshape(8, 128).T

    c[:, C_GFFN1:C_GFFN1 + 8] = chunks(inp["g_ffn1"][0])
    c[:, C_GMIX:C_GMIX + 8] = chunks(inp["g_mix"][0])
    c[:, C_GFFN2:C_GFFN2 + 8] = chunks(inp["g_ffn2"][0])
    c[:, C_GPLEIN:C_GPLEIN + 8] = chunks(inp["g_ple_in"][0])
    c[:, C_GPLEOUT:C_GPLEOUT + 8] = chunks(inp["g_ple_out"][0])
    c[:, C_BPLE:C_BPLE + 8] = chunks(inp["b_ple_gate"][0])
    c[:, C_GQLAT:C_GQLAT + 2] = np.asarray(inp["g_q_lat"][0], np.float32).reshape(2, 128).T
    c[:, C_GKVLAT] = np.asarray(inp["g_kv_lat"][0], np.float32)
    gq = np.asarray(inp["g_mla_q"][0], np.float32)
    gk = np.asarray(inp["g_mla_k"][0], np.float32)
    perm = np.concatenate([np.arange(16, 32), np.arange(0, 16)])
    c[0:96, C_GMQ] = gq
    c[64:96, C_GMQROT] = gq[64 + perm]
    c[0:64, C_GMKN] = gk[0:64]
    c[0:32, C_GMKR] = gk[64:96]
    c[0:32, C_GMKRROT] = gk[64 + perm]
    c[:, C_GDQ] = np.tile(np.asarray(inp["g_diff_q"][0], np.float32), 2)
    c[:, C_GDK] = np.tile(np.asarray(inp["g_diff_k"][0], np.float32), 2)
    c[:, C_GSUB] = np.asarray(inp["g_diff_sub"][0], np.float32)
    c[0:64, C_LQ1] = np.asarray(inp["lambda_q1"][0], np.float32)
    c[0:64, C_LK1] = np.asarray(inp["lambda_k1"][0], np.float32)
    c[0:64, C_LQ2] = np.asarray(inp["lambda_q2"][0], np.float32)
    c[0:64, C_LK2] = np.asarray(inp["lambda_k2"][0], np.float32)
    return c


_PROG = {}


def _get_prog(upto=99):
    if upto not in _PROG:
        _PROG[upto] = build_program(upto=upto)
    return _PROG[upto]


def make_in_maps(inp):
    x = np.asarray(inp["x"], np.float32)
    p = np.asarray(inp["p"], np.float32)[0]
    shared = {
        "w1g": np.ascontiguousarray(inp["w_ffn1_gate"][0], np.float32),
        "w1u": np.ascontiguousarray(inp["w_ffn1_up"][0], np.float32),
        "w1d": np.ascontiguousarray(inp["w_ffn1_down"][0], np.float32),
        "w2g": np.ascontiguousarray(inp["w_ffn2_gate"][0], np.float32),
        "w2u": np.ascontiguousarray(inp["w_ffn2_up"][0], np.float32),
        "w2d": np.ascontiguousarray(inp["w_ffn2_down"][0], np.float32),
        "w_in": np.ascontiguousarray(inp["w_in"][0], np.float32),
        "w_qup": np.ascontiguousarray(inp["w_q_up"][0], np.float32),
        "w_kvup": np.ascontiguousarray(inp["w_kv_up"][0], np.float32),
        "w_out": np.ascontiguousarray(inp["w_out"][0], np.float32),
        "w_pg": np.ascontiguousarray(inp["w_ple_gate"][0], np.float32),
        "w_pp": np.ascontiguousarray(inp["w_ple_proj"][0], np.float32),
        "consts": _consts(inp),
        "emats": _emats(),
    }
    maps = []
    for c in range(NCORES):
        b, r = c // 4, c % 4
        cs96, cs32, tlin, tabs, ftab = _const_tables(c)
        m = dict(shared)
        m["xT"] = np.ascontiguousarray(x[b, r * T:(r + 1) * T, :].T)
        m["pT"] = np.ascontiguousarray(p[b, r * T:(r + 1) * T, :].T)
        m["cs96"], m["cs32"], m["tlin"], m["tabs"], m["ftab"] = cs96, cs32, tlin, tabs, ftab
        maps.append(m)
    return maps


def kernel(**inputs):
    nc = _get_prog()
    maps = make_in_maps(inputs)
    res = run_bass_kernel_spmd(nc, maps, core_ids=list(range(NCORES)))
    out = np.empty((2, S, D), np.float32)
    for c in range(NCORES):
        b, r = c // 4, c % 4
        out[b, r * T:(r + 1) * T, :] = np.asarray(res.results[c]["outT"], np.float32).T
    return out
```

```python
import math
from contextlib import ExitStack

import numpy as np
import concourse.bass as bass
import concourse.mybir as mybir
from concourse.bass_utils import run_bass_kernel_spmd

F32 = mybir.dt.float32
BF16 = mybir.dt.bfloat16
AF = mybir.ActivationFunctionType
ALU = mybir.AluOpType

NCORES = 8
D = 1024
DFF = 2816
NF = DFF // 128
S = 8192
T = 2048
TT = 512
NTT = T // TT
NKB = S // 128
EPS = 1e-6
N_IN = 1952
LAMBDA_INIT = 0.8 - 0.6 * math.exp(0.0)
SLOPES = [2.0 ** (-8.0 * (i + 1) / 4) for i in range(4)]
MLA_SCALE = 96 ** -0.5

NFT = 12 + 12 * NTT * 16
SKIP_TH = 150.0


def _tile_needed(h, qt, j):
    sg, b = j // 16, j % 16
    for r in range(4):
        rho = (r + sg) % 4
        lo = T * (r - rho) + TT * qt - 128 * b - 127
        hi = T * (r - rho) + TT * qt - 128 * b + 511
        dmin = 0 if lo <= 0 <= hi else min(abs(lo), abs(hi))
        if SLOPES[h] * dmin <= SKIP_TH:
            return True
    return False


OFF_QLAT, OFF_KVLAT, OFF_KROPE, OFF_QD, OFF_KD, OFF_VD = 0, 256, 384, 416, 928, 1440

C_GFFN1, C_GMIX, C_GFFN2, C_GPLEIN, C_GPLEOUT, C_BPLE = 0, 8, 16, 24, 32, 40
C_GQLAT = 48
C_GKVLAT = 50
C_GMQ = 51
C_GMQROT = 52
C_GMKN = 53
C_GMKR = 54
C_GMKRROT = 55
C_GDQ = 56
C_GDK = 57
C_GSUB = 58
C_LQ1, C_LK1, C_LQ2, C_LK2 = 59, 60, 61, 62
NCONST = 64


_PID = {}


def _pid(e):
    if id(e) not in _PID:
        _PID[id(e)] = e.partition_id()
    return _PID[id(e)]


def I(method, *a, **kw):
    return lambda e: getattr(e, method)(*a, **kw)


class Sched:
    ENGS = ("sp", "act", "pool", "dve", "pe")

    def __init__(self):
        self.ops = {e: [] for e in self.ENGS}
        self.last_w = {}
        self.readers = {}
        self.sem_cnt = {}
        self.sem_amt = {}
        self.barrier_refs = []

    def add(self, eng, fn, reads=(), writes=(), dma=None, cc=None, extra=()):
        writes = list(writes) + [k for k in reads if isinstance(k, tuple) and k[0] == "ps" and k not in writes]
        deps = set(self.barrier_refs) | set(extra)
        for k in reads:
            w = self.last_w.get(k)
            if w is not None:
                deps.add(w)
        for k in writes:
            w = self.last_w.get(k)
            if w is not None:
                deps.add(w)
            for r in self.readers.get(k, ()):
                deps.add(r)
        deps = set(("sem", d[1], self.sem_cnt[d[1]]) if d[0] == "sem" else d for d in deps)
        idx = len(self.ops[eng])
        key = dma if dma is not None else cc
        if key is not None:
            self.sem_amt[key] = 16 if dma is not None else 1
            self.sem_cnt[key] = self.sem_cnt.get(key, 0) + 1
            ref = ("sem", key, self.sem_cnt[key])
        else:
            ref = ("op", eng, idx)
        self.ops[eng].append(dict(fn=fn, deps=deps, key=key, ref=ref))
        for k in reads:
            self.readers.setdefault(k, []).append(ref)
        for k in writes:
            self.last_w[k] = ref
            self.readers[k] = []
        return ref

    def barrier(self):
        refs = []
        for e in self.ENGS:
            if self.ops[e]:
                o = self.ops[e][-1]
                refs.append(("op", e, len(self.ops[e]) - 1) if o["key"] is None else o["ref"])
        for key, cnt in self.sem_cnt.items():
            refs.append(("sem", key, cnt))
        self.barrier_refs = refs
        self.last_w = {}
        self.readers = {}

    def emit(self, nc):
        need = {e: set() for e in self.ENGS}
        for e in self.ENGS:
            for o in self.ops[e]:
                nd = set()
                for d in o["deps"]:
                    if d[0] == "op":
                        if d[1] == "pe" and e == "pe":
                            continue
                        if d[1] == "sp":
                            continue
                        need[d[1]].add(d[2])
                    nd.add(d)
                o["deps"] = nd
        cnt = {}
        for e in self.ENGS:
            c = 0
            for i, o in enumerate(self.ops[e]):
                if o["key"] is None and i in need[e]:
                    c += 1
                    o["inc"] = True
                else:
                    o["inc"] = False
                cnt[(e, i)] = c
        with ExitStack() as st:
            esem = {e: st.enter_context(nc.semaphore("s_" + e)) for e in ("act", "pool", "dve", "pe")}
            ksem = {}
            for i, k in enumerate(self.sem_cnt):
                ksem[k] = st.enter_context(nc.semaphore("k%d" % i))
            block = st.enter_context(nc.Block())

            def run(ename, eng):
                waited = {}
                for o in self.ops[ename]:
                    for d in o["deps"]:
                        if d[0] == "op":
                            if d[1] == "pe" and ename == "pe":
                                continue
                            if d[1] == "sp":
                                continue
                            sem, val, wk = esem[d[1]], cnt[(d[1], d[2])], ("e", d[1])
                        else:
                            sem, val, wk = ksem[d[1]], d[2] * self.sem_amt[d[1]], ("k", d[1])
                        if waited.get(wk, 0) >= val:
                            continue
                        waited[wk] = val
                        eng.wait_ge(sem, val)
                    inst = o["fn"](eng)
                    if o["key"] is not None:
                        if self.sem_amt[o["key"]] == 16:
                            inst.then_inc(ksem[o["key"]], 16)
                        else:
                            inst.then_inc(ksem[o["key"]])
                    elif o["inc"]:
                        inst.then_inc(esem[ename], 1)
                for o in self.ops[ename]:
                    pass
                for k, c in self.sem_cnt.items():
                    if k in self.issuer and self.issuer[k] == ename:
                        eng.wait_ge(ksem[k], c * self.sem_amt[k])

            self.issuer = {}
            for e in self.ENGS:
                for o in self.ops[e]:
                    if o["key"] is not None:
                        self.issuer[o["key"]] = e

            @block.sync
            def _(eng):
                run("sp", eng)

            @block.scalar
            def _(eng):
                run("act", eng)

            @block.gpsimd
            def _(eng):
                run("pool", eng)

            @block.vector
            def _(eng):
                run("dve", eng)

            @block.tensor
            def _(eng):
                run("pe", eng)


def build_program(upto=99, debug=False, phases=("f1", "mix", "f2", "ple")):
    nc = bass.Bass("TRN2", target_bir_lowering=False)
    sc = Sched()

    def din(name, shape, dt=F32):
        return nc.dram_tensor(name, list(shape), dt, kind="ExternalInput").ap()

    xT = din("xT", [D, T])
    pT = din("pT", [256, T])
    w1g, w1u, w1d = din("w1g", [D, DFF]), din("w1u", [D, DFF]), din("w1d", [DFF, D])
    w2g, w2u, w2d = din("w2g", [D, DFF]), din("w2u", [D, DFF]), din("w2d", [DFF, D])
    w_in = din("w_in", [D, N_IN])
    w_qup = din("w_qup", [256, 768])
    w_kvup = din("w_kvup", [128, 1024])
    w_out = din("w_out", [D, D])
    w_pg = din("w_pg", [D, D])
    w_pp = din("w_pp", [256, D])
    consts_d = din("consts", [128, NCONST])
    cs96_d = din("cs96", [96, 2, T])
    cs32_d = din("cs32", [32, 2, T])
    tlin_d = din("tlin", [128, TT])
    tabs_d = din("tabs", [128, 896])
    ftab_d = din("ftab", [128, NFT])
    emats_d = din("emats", [128, 4, 128])
    outT = nc.dram_tensor("outT", [D, T], F32, kind="ExternalOutput").ap()

    q_mla_d = nc.dram_tensor("q_mla_d", [8 * 96, T], BF16).ap()
    q_diff_d = nc.dram_tensor("q_diff_d", [4 * 128, T], BF16).ap()
    kT_mla_l = [nc.dram_tensor("kT_mla_l%d" % c, [768, TT], BF16) for c in range(4)]
    kT_mla_g = [nc.dram_tensor("kT_mla_g%d" % c, [4 * 768, TT], BF16) for c in range(4)]
    kT_diff_l = [nc.dram_tensor("kT_diff_l%d" % c, [256, T], BF16) for c in range(2)]
    kT_diff_g = [nc.dram_tensor("kT_diff_g%d" % c, [4 * 256, T], BF16) for c in range(2)]
    v_mla_l = [nc.dram_tensor("v_mla_l%d" % c, [TT, 512], BF16) for c in range(4)]
    v_mla_g = [nc.dram_tensor("v_mla_g%d" % c, [4 * TT, 512], BF16) for c in range(4)]
    v_diff_l = [nc.dram_tensor("v_diff_l%d" % c, [1024, 512], BF16) for c in range(2)]
    v_diff_g = [nc.dram_tensor("v_diff_g%d" % c, [4 * 1024, 512], BF16) for c in range(2)]
    kT_diff_r = nc.dram_tensor("kT_diff_r", [4 * 4 * 128, T], BF16)
    v_diff_r = nc.dram_tensor("v_diff_r", [S, 512], BF16)

    hT = nc.alloc_sbuf_tensor("hT", [128, 8, T], F32)
    consts = nc.alloc_sbuf_tensor("consts_sb", [128, NCONST], F32)
    emats = nc.alloc_sbuf_tensor("emats_sb", [128, 4, 128], F32)
    ones_bf = nc.alloc_sbuf_tensor("ones_bf", [128, 128], BF16)
    eps_t = nc.alloc_sbuf_tensor("eps_t", [128, 1], F32)
    misc = nc.alloc_sbuf_tensor("misc", [128, 8], F32)
    PS = nc.alloc_psum_tensor("ps", [128, 8 * 512], F32)

    def bank(b, p0=0, p1=128, n=1):
        return PS[p0:p1, b * 512:(b + n) * 512]

    ONES32 = emats[:, 0, :]
    EMLA = emats[:, 1, :]
    EDIFF = emats[:, 2, :]
    SHIFT = emats[:, 3, :]

    def col(c, p0=0, p1=128):
        return consts[p0:p1, c:c + 1]

    sc.add("sp", I("dma_start", out=consts[:], in_=consts_d), writes=["consts"], dma="c0")
    sc.add("sp", I("dma_start", out=emats[:], in_=emats_d), writes=["emats"], dma="c0")
    for t in range(NTT):
        for k in range(8):
            sc.add("sp", I("dma_start", out=hT[:, k, t * TT:(t + 1) * TT], in_=xT[k * 128:(k + 1) * 128, t * TT:(t + 1) * TT]),
                   writes=[("hT", k, t)], dma="x%d" % t)
    sc.add("dve", I("memset", ones_bf[:], 1.0), writes=["ones_bf"])
    sc.add("dve", I("memset", eps_t[:], EPS), writes=["eps"])

    def rmsnorm_tile(tok, gcol, xn_tile, sq, sd, rstd, stat_bank, keypfx, scr=None):
        ti = tok // TT
        scr = keypfx if scr is None else scr
        for k in range(8):
            sc.add("act", I("activation", out=sq[:, k, :], in_=hT[:, k, tok:tok + TT], func=AF.Square),
                   reads=[("hT", k, ti)], writes=[(scr, "sq", k)])
        for k in range(8):
            sc.add("pe", I("matmul", bank(stat_bank), lhsT=ones_bf[:], rhs=sq[:, k, :],
                                                 start=(k == 0), stop=(k == 7)),
                   reads=[(scr, "sq", k), "ones_bf"], writes=[("ps", stat_bank)])
        sc.add("act", I("activation", out=sd[:], in_=bank(stat_bank), func=AF.Ln, bias=eps_t[:], scale=1.0 / D),
               reads=[("ps", stat_bank), "eps"], writes=[(scr, "sd")])
        sc.add("act", I("activation", out=rstd[:], in_=sd[:], func=AF.Exp, scale=-0.5), reads=[(scr, "sd")], writes=[(scr, "rstd")])
        for k in range(8):
            sc.add("dve", I("scalar_tensor_tensor",
                out=xn_tile[:, k, :], in0=hT[:, k, tok:tok + TT], scalar=col(gcol + k), in1=rstd[:],
                op0=ALU.mult, op1=ALU.mult),
                reads=[("hT", k, ti), (scr, "rstd"), "consts"], writes=[(keypfx, "xn", k)])

    def ffn(name, gcol, wg, wu, wd, mid_hook=None):
        wg_v = wg.rearrange("(k p) f -> p k f", p=128)
        wu_v = wu.rearrange("(k p) f -> p k f", p=128)
        wd_v = wd.rearrange("(f p) o -> p f o", p=128)
        with ExitStack() as st:
            al = lambda n, s, d: st.enter_context(nc.sbuf_tensor(name + n, s, d))
            xn = al("xn", [128, 2, 8, TT], BF16)
            hid = al("hid", [128, NF, 2 * TT], BF16)
            sq = al("sq", [128, 8, TT], BF16)
            sd = al("sd", [128, TT], F32)
            rstd = al("rstd", [128, TT], F32)
            wgu = [al("wgu%d" % i, [128, 2, 8, 128], BF16) for i in range(3)]
            wdt = [al("wdt%d" % i, [128, NF, 128], BF16) for i in range(2)]
            sg = [al("sg%d" % i, [128, TT], F32) for i in range(2)]
            widx = [0]
            didx = [0]
            for half in range(2):
                if half == 1 and mid_hook is not None:
                    mid_hook()
                for sub in range(2):
                    tok = half * 1024 + sub * TT
                    rmsnorm_tile(tok, gcol, xn[:, sub, :, :], sq, sd, rstd, 6 + sub, (name, "n", sub), scr=(name, "scr"))
                for f in range(NF):
                    slot = widx[0] % 3
                    widx[0] += 1
                    wt = wgu[slot]
                    sc.add("pool", I("dma_start", out=wt[:, 0, :, :], in_=wg_v[:, :, f * 128:(f + 1) * 128]),
                           writes=[(name, "wgu", slot, 0)], dma=(name, "wgu", slot))
                    sc.add("pool", I("dma_start", out=wt[:, 1, :, :], in_=wu_v[:, :, f * 128:(f + 1) * 128]),
                           writes=[(name, "wgu", slot, 1)], dma=(name, "wgu", slot))
                    for sub in range(2):
                        gb, ub = sub, 2 + sub
                        for k in range(8):
                            sc.add("pe", I("matmul",
                                bank(gb), lhsT=wt[:, 0, k, :], rhs=xn[:, sub, k, :], start=(k == 0), stop=(k == 7)),
                                reads=[(name, "wgu", slot, 0), ((name, "n", sub), "xn", k)], writes=[("ps", gb)])
                        for k in range(8):
                            sc.add("pe", I("matmul",
                                bank(ub), lhsT=wt[:, 1, k, :], rhs=xn[:, sub, k, :], start=(k == 0), stop=(k == 7)),
                                reads=[(name, "wgu", slot, 1), ((name, "n", sub), "xn", k)], writes=[("ps", ub)])
                        sc.add("act", I("activation", out=sg[sub][:], in_=bank(gb), func=AF.Silu),
                               reads=[("ps", gb)], writes=[(name, "sg", sub)])
                        sc.add("dve", I("tensor_tensor",
                            out=hid[:, f, sub * TT:(sub + 1) * TT], in0=sg[sub][:], in1=bank(ub), op=ALU.mult),
                            reads=[(name, "sg", sub), ("ps", ub)], writes=[(name, "hid", f, sub)])
                for o in range(8):
                    slot = didx[0] % 2
                    didx[0] += 1
                    wt = wdt[slot]
                    sc.add("pool", I("dma_start", out=wt[:], in_=wd_v[:, :, o * 128:(o + 1) * 128]),
                           writes=[(name, "wdt", slot)], dma=(name, "wdt", slot))
                    for sub in range(2):
                        ob = 4 + sub
                        ti = half * 2 + sub
                        tok = ti * TT
                        for f in range(NF):
                            sc.add("pe", I("matmul",
                                bank(ob), lhsT=wt[:, f, :], rhs=hid[:, f, sub * TT:(sub + 1) * TT],
                                start=(f == 0), stop=(f == NF - 1)),
                                reads=[(name, "wdt", slot), (name, "hid", f, sub)], writes=[("ps", ob)])
                        sc.add("dve", I("scalar_tensor_tensor",
                            out=hT[:, o, tok:tok + TT], in0=bank(ob), scalar=0.5, in1=hT[:, o, tok:tok + TT],
                            op0=ALU.mult, op1=ALU.add),
                            reads=[("ps", ob), ("hT", o, ti)], writes=[("hT", o, ti)])
        sc.barrier()

    wst = ExitStack()
    win_pre = None
    if upto >= 2 and "mix" in phases and "f1" in phases:
        win_pre = wst.enter_context(nc.sbuf_tensor("m_win", [128, 8, N_IN], BF16))

    def win_hook():
        win_v = w_in.rearrange("(k p) n -> p k n", p=128)
        for k in range(8):
            sc.add("pool", I("dma_start", out=win_pre[:, k, :], in_=win_v[:, k, :]), writes=[("win", k)], dma=("win", k))

    if "f1" in phases:
        ffn("f1", C_GFFN1, w1g, w1u, w1d, mid_hook=win_hook if win_pre is not None else None)
    if upto >= 2 and "mix" in phases:
        mixer_phase(nc, sc, locals())
    with ExitStack() as pst:
        pw = ple_prefetch(nc, sc, locals(), pst) if (upto >= 4 and "ple" in phases) else None
        if upto >= 3 and "f2" in phases:
            ffn("f2", C_GFFN2, w2g, w2u, w2d)
        if pw is not None:
            ple_phase(nc, sc, locals(), pw)

    for k in range(8 if pw is None else 0):
        sc.add("sp", I("dma_start", out=outT[k * 128:(k + 1) * 128, :], in_=hT[:, k, :]),
               reads=[("hT", k, t) for t in range(NTT)], dma="out")
    sc.emit(nc)
    return nc


MIX_STOP = 99


def mixer_phase(nc, sc, L):
    hT, consts, emats, ones_bf, eps_t, misc, PS = (L[k] for k in ("hT", "consts", "emats", "ones_bf", "eps_t", "misc", "PS"))
    bank, col, rmsnorm_tile = L["bank"], L["col"], L["rmsnorm_tile"]
    ONES32, EMLA, EDIFF, SHIFT = L["ONES32"], L["EMLA"], L["EDIFF"], L["SHIFT"]
    w_in, w_qup, w_kvup, w_out = L["w_in"], L["w_qup"], L["w_kvup"], L["w_out"]
    cs96_d, cs32_d, tlin_d, tabs_d, ftab_d = L["cs96_d"], L["cs32_d"], L["tlin_d"], L["tabs_d"], L["ftab_d"]
    q_mla_d, q_diff_d = L["q_mla_d"], L["q_diff_d"]
    kT_mla_l, kT_diff_l, v_mla_l, v_diff_l = L["kT_mla_l"], L["kT_diff_l"], L["v_mla_l"], L["v_diff_l"]
    kT_mla_g, kT_diff_g, v_mla_g, v_diff_g = L["kT_mla_g"], L["kT_diff_g"], L["v_mla_g"], L["v_diff_g"]
    kT_diff_r, v_diff_r = L["kT_diff_r"], L["v_diff_r"]

    groups = [[0, 1, 2, 3], [4, 5, 6, 7]]
    MLA_PAIRS = [[("v_mla", c, v_mla_l[c], v_mla_g[c]), ("kT_mla", c, kT_mla_l[c], kT_mla_g[c])] for c in range(4)]
    DIFF_PAIRS = [("kT_diff", c, kT_diff_l[c], kT_diff_g[c]) for c in range(2)] + [("v_diff", c, v_diff_l[c], v_diff_g[c]) for c in range(2)]
    ncc = [0]

    def issue_gathers(pairs, explicit_deps, extra=()):
        for (nm, c, a, g) in pairs:
            rd = []
            if explicit_deps and nm == "v_mla":
                rd = [("v_mla_l", c, tb) for tb in range(4)]
            if explicit_deps and nm == "kT_mla":
                rd = [("kT_mla_l", h, part, c) for h in range(8) for part in (0, 1)]
            sc.add("pool", I("collective_compute", "AllGather", ALU.bypass, replica_groups=groups,
                             ins=[a.ap().opt()], outs=[g.ap().opt()]),
                   reads=rd, writes=[("gath", nm, c)], cc="cc%d" % ncc[0], extra=extra)
            ncc[0] += 1

    with ExitStack() as st:
        al = lambda n, s, d: st.enter_context(nc.sbuf_tensor("m_" + n, s, d))
        xn = al("xn", [128, 8, TT], BF16)
        sq8 = al("sq8", [128, 8, TT], BF16)
        sd0 = al("sd0", [128, TT], F32)
        rstd0 = al("rstd0", [128, TT], F32)
        win_pre = L.get("win_pre")
        win = win_pre if win_pre is not None else al("win", [128, 8, N_IN], BF16)
        win_rot = al("winrot", [128, 8, 32], BF16)
        wq = al("wq", [128, 2, 768], BF16)
        wq_rot = al("wqrot", [128, 2, 8, 96], BF16)
        wkv = al("wkv", [128, 1024], BF16)
        wkv_v = al("wkvv", [128, 8, 64], BF16)
        cg = al("cg", [96, 2, T], F32)
        ck = al("ck", [32, 2, T], F32)
        qlat_n = al("qlatn", [128, 2, TT], BF16)
        kvlat_n = al("kvlatn", [128, TT], BF16)
        NS = 4
        sqf = [al("sqf%d" % i, [128, TT], BF16) for i in range(NS)]
        embf = al("embf", [128, 3, 128], BF16)
        sc.add("dve", I("tensor_copy", out=embf[:], in_=emats[:, 0:3, :]), reads=["emats"], writes=["embf"])
        ONESB, EMLAB, EDIFFB = embf[:, 0, :], embf[:, 1, :], embf[:, 2, :]
        NOB = 8
        rsf = [al("rsf%d" % i, [128, TT], F32) for i in range(NS)]
        t1 = [al("t1_%d" % i, [128, TT], F32) for i in range(NS)]
        t2 = [al("t2_%d" % i, [128, TT], F32) for i in range(NS)]
        ob = [al("ob%d" % i, [128, TT], BF16) for i in range(8)]
        vst = [al("vst%d" % i, [128, 512], BF16) for i in range(NS)]

        win_v = w_in.rearrange("(k p) n -> p k n", p=128)
        for k in range(8 if win_pre is None else 0):
            sc.add("pool", I("dma_start", out=win[:, k, :], in_=win_v[:, k, :]), writes=[("win", k)], dma=("win", k))
        for k in range(8):
            sc.add("pool", I("dma_start", out=win_rot[:, k, 0:16], in_=win_v[:, k, OFF_KROPE + 16:OFF_KROPE + 32]),
                   writes=[("winrot", k, 0)], dma="winrot")
            sc.add("pool", I("dma_start", out=win_rot[:, k, 16:32], in_=win_v[:, k, OFF_KROPE:OFF_KROPE + 16]),
                   writes=[("winrot", k, 1)], dma="winrot")
        wq_v = w_qup.rearrange("(k p) n -> p k n", p=128)
        wq_v4 = w_qup.rearrange("(k p) (h d) -> p k h d", p=128, d=96)
        sc.add("dve", I("memset", wq_rot[:], 0.0), writes=["wqrot"])
        for k in range(2):
            sc.add("pool", I("dma_start", out=wq[:, k, :], in_=wq_v[:, k, :]), writes=[("wq", k)], dma="wq")
            sc.add("pool", I("dma_start", out=wq_rot[:, k, :, 64:80], in_=wq_v4[:, k, :, 80:96]),
                   reads=["wqrot"], writes=[("wqrot", k, 0)], dma="wqrot")
            sc.add("pool", I("dma_start", out=wq_rot[:, k, :, 80:96], in_=wq_v4[:, k, :, 64:80]),
                   reads=["wqrot"], writes=[("wqrot", k, 1)], dma="wqrot")
        sc.add("pool", I("dma_start", out=wkv[:], in_=w_kvup), writes=["wkv"], dma="wkv")
        sc.add("pool", I("dma_start", out=wkv_v[:], in_=w_kvup.rearrange("p (h d) -> p h d", d=128)[:, :, 64:128]),
               writes=["wkvv"], dma="wkv")
        sc.add("sp", I("dma_start", out=cg[:], in_=cs96_d), writes=["cg_raw"], dma="cg")
        sc.add("sp", I("dma_start", out=ck[:], in_=cs32_d), writes=["ck_raw"], dma="cg")
        sc.add("dve", I("tensor_scalar", out=cg[:, 0, :], in0=cg[:, 0, :], scalar1=col(C_GMQ, 0, 96), scalar2=None, op0=ALU.mult),
               reads=["cg_raw", "consts"], writes=["cg0"])
        sc.add("dve", I("tensor_scalar", out=cg[:, 1, :], in0=cg[:, 1, :], scalar1=col(C_GMQROT, 0, 96), scalar2=None, op0=ALU.mult),
               reads=["cg_raw", "consts"], writes=["cg1"])
        sc.add("dve", I("tensor_scalar", out=ck[:, 0, :], in0=ck[:, 0, :], scalar1=col(C_GMKR, 0, 32), scalar2=None, op0=ALU.mult),
               reads=["ck_raw", "consts"], writes=["ck0"])
        sc.add("dve", I("tensor_scalar", out=ck[:, 1, :], in0=ck[:, 1, :], scalar1=col(C_GMKRROT, 0, 32), scalar2=None, op0=ALU.mult),
               reads=["ck_raw", "consts"], writes=["ck1"])
        sc.add("dve", I("tensor_tensor", out=misc[0:64, 0:1], in0=col(C_LQ1, 0, 64), in1=col(C_LK1, 0, 64), op=ALU.mult),
               reads=["consts"], writes=["lam_p1"])
        sc.add("dve", I("tensor_tensor", out=misc[0:64, 1:2], in0=col(C_LQ2, 0, 64), in1=col(C_LK2, 0, 64), op=ALU.mult),
               reads=["consts"], writes=["lam_p2"])
        sc.add("pe", I("matmul", PS[:, 0:2], lhsT=emats[0:64, 0, :], rhs=misc[0:64, 0:2], start=True, stop=True),
               reads=["lam_p1", "lam_p2", "emats"], writes=[("ps", 0)])
        sc.add("act", I("activation", out=misc[:, 2:4], in_=PS[:, 0:2], func=AF.Exp), reads=[("ps", 0)], writes=["lam_e"])
        sc.add("dve", I("tensor_tensor", out=misc[:, 4:5], in0=misc[:, 3:4], in1=misc[:, 2:3], op=ALU.subtract),
               reads=["lam_e"], writes=["lam_d"])
        sc.add("dve", I("tensor_scalar", out=misc[:, 5:6], in0=misc[:, 4:5], scalar1=-LAMBDA_INIT, scalar2=None, op0=ALU.add),
               reads=["lam_d"], writes=["neglam"])
        sc.add("dve", I("tensor_scalar", out=misc[:, 6:7], in0=col(C_GSUB), scalar1=1.0 - LAMBDA_INIT, scalar2=None, op0=ALU.mult),
               reads=["consts"], writes=["gsub"])

        rr = [0]
        pb = [0]

        obr = [0]

        def nob():
            obr[0] += 1
            return obr[0] % 8

        def nslot():
            rr[0] += 1
            return rr[0] % NS

        def nbank():
            pb[0] += 1
            return pb[0] % 6

        def norm_stats(src_bank, P, emat_ap, scale, s):
            sb = 6 + (s % 2)
            sc.add("act", I("activation", out=sqf[s][0:P, :], in_=bank(src_bank, 0, P), func=AF.Square),
                   reads=[("ps", src_bank)], writes=[("sqf", s)])
            sc.add("pe", I("matmul", bank(sb, 0, P), lhsT=emat_ap, rhs=sqf[s][0:P, :], start=True, stop=True),
                   reads=[("sqf", s), "embf"], writes=[("ps", sb)])
            sc.add("act", I("activation", out=rsf[s][0:P, :], in_=bank(sb, 0, P), func=AF.Ln, bias=eps_t[0:P, :], scale=scale),
                   reads=[("ps", sb), "eps"], writes=[("rsf", s)])
            sc.add("act", I("activation", out=rsf[s][0:P, :], in_=rsf[s][0:P, :], func=AF.Exp, scale=-0.5), reads=[("rsf", s)], writes=[("rsf", s)])

        def sq_part(src_bank, P, s):
            sc.add("act", I("activation", out=sqf[s][0:P, :], in_=bank(src_bank, 0, P), func=AF.Square),
                   reads=[("ps", src_bank)], writes=[("sqf", s)])

        def stat_part(P, emat_ap, scale, s):
            sb = 6 + (s % 2)
            sc.add("pe", I("matmul", bank(sb, 0, P), lhsT=emat_ap, rhs=sqf[s][0:P, :], start=True, stop=True),
                   reads=[("sqf", s), "embf"], writes=[("ps", sb)])
            sc.add("act", I("activation", out=rsf[s][0:P, :], in_=bank(sb, 0, P), func=AF.Ln, bias=eps_t[0:P, :], scale=scale),
                   reads=[("ps", sb), "eps"], writes=[("rsf", s)])
            sc.add("act", I("activation", out=rsf[s][0:P, :], in_=rsf[s][0:P, :], func=AF.Exp, scale=-0.5), reads=[("rsf", s)], writes=[("rsf", s)])

        def run_chains(chains):
            prev = None
            for ch in chains:
                ch[0]()
                if prev is not None:
                    prev[1]()
                prev = ch
            if prev is not None:
                prev[1]()

        xk = lambda k: (("mx",), "xn", k)

        def proj_in(b, c0, M):
            for k in range(8):
                sc.add("pe", I("matmul", bank(b, 0, M), lhsT=win[:, k, c0:c0 + M], rhs=xn[:, k, :], start=(k == 0), stop=(k == 7)),
                       reads=[("win", k), xk(k)], writes=[("ps", b)])

        def mk_A(ti):
            st_ = {}

            def A():
                st_["b"] = (nbank(), nbank())
                st_["s"] = (nslot(), nslot())
                for c in range(2):
                    proj_in(st_["b"][c], OFF_QLAT + c * 128, 128)
                    sq_part(st_["b"][c], 128, st_["s"][c])

            def B():
                (b0, b1), (s0, s1) = st_["b"], st_["s"]
                sc.add("pe", I("matmul", bank(6), lhsT=ONESB, rhs=sqf[s0][:], start=True, stop=False), reads=[("sqf", s0), "embf"], writes=[("ps", 6)])
                sc.add("pe", I("matmul", bank(6), lhsT=ONESB, rhs=sqf[s1][:], start=False, stop=True), reads=[("sqf", s1), "embf"], writes=[("ps", 6)])
                sc.add("act", I("activation", out=rsf[s0][:], in_=bank(6), func=AF.Ln, bias=eps_t[:], scale=1.0 / 256),
                       reads=[("ps", 6), "eps"], writes=[("rsf", s0)])
                sc.add("act", I("activation", out=rsf[s0][:], in_=rsf[s0][:], func=AF.Exp, scale=-0.5), reads=[("rsf", s0)], writes=[("rsf", s0)])
                for c, b in ((0, b0), (1, b1)):
                    sc.add("dve", I("scalar_tensor_tensor", out=qlat_n[:, c, :], in0=bank(b), scalar=col(C_GQLAT + c), in1=rsf[s0][:],
                                    op0=ALU.mult, op1=ALU.mult),
                           reads=[("ps", b), ("rsf", s0), "consts"], writes=[("qlatn", c)])
            return A, B

        def mk_B(ti):
            st_ = {}

            def A():
                st_["b"], st_["s"] = nbank(), nslot()
                proj_in(st_["b"], OFF_KVLAT, 128)
                sq_part(st_["b"], 128, st_["s"])

            def B():
                b, s = st_["b"], st_["s"]
                stat_part(128, ONESB, 1.0 / 128, s)
                sc.add("dve", I("scalar_tensor_tensor", out=kvlat_n[:], in0=bank(b), scalar=col(C_GKVLAT), in1=rsf[s][:], op0=ALU.mult, op1=ALU.mult),
                       reads=[("ps", b), ("rsf", s), "consts"], writes=["kvlatn"])
            return A, B

        def rope_tail(P, bq, br, s, tab, tsl, k0, k1, eng="pool"):
            sc.add("dve", I("tensor_tensor", out=t1[s][0:P, :], in0=bank(bq, 0, P), in1=tab[:, 0, tsl], op=ALU.mult),
                   reads=[("ps", bq), k0], writes=[("t1", s)])
            sc.add("dve", I("tensor_tensor", out=t2[s][0:P, :], in0=bank(br, 0, P), in1=tab[:, 1, tsl], op=ALU.mult),
                   reads=[("ps", br), k1], writes=[("t2", s)])
            sc.add(eng, I("tensor_tensor", out=t1[s][0:P, :], in0=t1[s][0:P, :], in1=t2[s][0:P, :], op=ALU.add),
                   reads=[("t1", s), ("t2", s)], writes=[("t1", s)])
            o_ = nob()
            sc.add(eng, I("tensor_tensor", out=ob[o_][0:P, :], in0=t1[s][0:P, :], in1=rsf[s][0:P, :], op=ALU.mult),
                   reads=[("t1", s), ("rsf", s)], writes=[("ob", o_)])
            return o_

        def mk_C(ti, h):
            st_ = {}
            tsl = slice(ti * TT, (ti + 1) * TT)

            def A():
                bq, br, s = nbank(), nbank(), nslot()
                st_.update(bq=bq, br=br, s=s)
                for c in range(2):
                    sc.add("pe", I("matmul", bank(bq, 0, 96), lhsT=wq[:, c, h * 96:(h + 1) * 96], rhs=qlat_n[:, c, :], start=(c == 0), stop=(c == 1)),
                           reads=[("wq", c), ("qlatn", c)], writes=[("ps", bq)])
                for c in range(2):
                    sc.add("pe", I("matmul", bank(br, 0, 96), lhsT=wq_rot[:, c, h, :], rhs=qlat_n[:, c, :], start=(c == 0), stop=(c == 1)),
                           reads=[("wqrot", c, 0), ("wqrot", c, 1), ("qlatn", c)], writes=[("ps", br)])
                sq_part(bq, 96, s)

            def B():
                bq, br, s = st_["bq"], st_["br"], st_["s"]
                stat_part(96, EMLAB[0:96, 0:96], 1.0, s)
                o_ = rope_tail(96, bq, br, s, cg, tsl, "cg0", "cg1")
                sc.add("sp", I("dma_start", out=q_mla_d[h * 96:(h + 1) * 96, tsl], in_=ob[o_][0:96, :]),
                       reads=[("ob", o_)], writes=[("q_mla_d", h, ti)], dma=("st", o_))
            return A, B

        def mk_D(ti, h):
            st_ = {}
            tsl = slice(ti * TT, (ti + 1) * TT)

            def A():
                bk, s = nbank(), nslot()
                st_.update(bk=bk, s=s)
                sc.add("pe", I("matmul", bank(bk, 0, 64), lhsT=wkv[:, h * 128:h * 128 + 64], rhs=kvlat_n[:], start=True, stop=True),
                       reads=["wkv", "kvlatn"], writes=[("ps", bk)])
                sq_part(bk, 64, s)

            def B():
                bk, s = st_["bk"], st_["s"]
                stat_part(64, ONESB[0:64, 0:64], 1.0 / 64, s)
                o_ = nob()
                sc.add("dve", I("scalar_tensor_tensor", out=ob[o_][0:64, :], in0=bank(bk, 0, 64), scalar=col(C_GMKN, 0, 64), in1=rsf[s][0:64, :],
                                op0=ALU.mult, op1=ALU.mult),
                       reads=[("ps", bk), ("rsf", s), "consts"], writes=[("ob", o_)])
                sc.add("sp", I("dma_start", out=kT_mla_l[ti].ap()[h * 96:h * 96 + 64, :], in_=ob[o_][0:64, :]),
                       reads=[("ob", o_)], writes=[("kT_mla_l", h, 0, ti)], dma=("st", o_))
            return A, B

        def mk_KR(ti):
            st_ = {}
            tsl = slice(ti * TT, (ti + 1) * TT)

            def A():
                bq, br, s = nbank(), nbank(), nslot()
                st_.update(bq=bq, br=br, s=s)
                for k in range(8):
                    sc.add("pe", I("matmul", bank(bq, 0, 32), lhsT=win[:, k, OFF_KROPE:OFF_KROPE + 32], rhs=xn[:, k, :], start=(k == 0), stop=(k == 7)),
                           reads=[("win", k), xk(k)], writes=[("ps", bq)])
                for k in range(8):
                    sc.add("pe", I("matmul", bank(br, 0, 32), lhsT=win_rot[:, k, :], rhs=xn[:, k, :], start=(k == 0), stop=(k == 7)),
                           reads=[("winrot", k, 0), ("winrot", k, 1), xk(k)], writes=[("ps", br)])
                sq_part(bq, 32, s)

            def B():
                bq, br, s = st_["bq"], st_["br"], st_["s"]
                stat_part(32, ONESB[0:32, 0:32], 1.0 / 32, s)
                o_ = rope_tail(32, bq, br, s, ck, tsl, "ck0", "ck1", eng="dve")
                for h in range(8):
                    sc.add("sp", I("dma_start", out=kT_mla_l[ti].ap()[h * 96 + 64:h * 96 + 96, :], in_=ob[o_][0:32, :]),
                           reads=[("ob", o_)], writes=[("kT_mla_l", h, 1, ti)], dma=("st", o_))
            return A, B

        def mk_FG(ti, off, gc, dname, h):
            st_ = {}
            tsl = slice(ti * TT, (ti + 1) * TT)
            dst_ap = (q_diff_d[h * 128:(h + 1) * 128, tsl] if dname == "q_diff_d"
                      else kT_diff_l[h // 2].ap()[(h % 2) * 128:(h % 2 + 1) * 128, tsl])

            def A():
                b, s = nbank(), nslot()
                st_.update(b=b, s=s)
                proj_in(b, off + h * 128, 128)
                sq_part(b, 128, s)

            def B():
                b, s = st_["b"], st_["s"]
                stat_part(128, EDIFFB, 1.0, s)
                o_ = nob()
                sc.add("dve", I("scalar_tensor_tensor", out=ob[o_][:], in0=bank(b), scalar=col(gc), in1=rsf[s][:], op0=ALU.mult, op1=ALU.mult),
                       reads=[("ps", b), ("rsf", s), "consts"], writes=[("ob", o_)])
                sc.add("sp", I("dma_start", out=dst_ap, in_=ob[o_][:]), reads=[("ob", o_)], writes=[(dname, h, ti)], dma=("st", o_))
            return A, B

        def mk_V(ti, tb, kind):
            st_ = {}
            tok = ti * TT
            row = (tok + tb * 128) % 1024
            dst = (v_mla_l[ti].ap()[tb * 128:(tb + 1) * 128, :] if kind == "mla"
                   else v_diff_l[(tok + tb * 128) // 1024].ap()[row:row + 128, :])

            def A():
                bv = nbank()
                st_.update(bv=bv)
                if kind == "mla":
                    sc.add("pe", I("matmul", bank(bv), lhsT=kvlat_n[:, tb * 128:(tb + 1) * 128], rhs=wkv_v[:].rearrange("p h d -> p (h d)"),
                                   start=True, stop=True),
                           reads=["kvlatn", "wkvv"], writes=[("ps", bv)])
                else:
                    for k in range(8):
                        sc.add("pe", I("matmul", bank(bv), lhsT=xn[:, k, tb * 128:(tb + 1) * 128], rhs=win[:, k, OFF_VD:OFF_VD + 512],
                                       start=(k == 0), stop=(k == 7)),
                               reads=[("win", k), xk(k)], writes=[("ps", bv)])

            def B():
                bv, s = st_["bv"], nslot()
                sc.add("act", I("activation", out=vst[s][:], in_=bank(bv), func=AF.Copy), reads=[("ps", bv)], writes=[("vst", s)])
                sc.add("sp", I("dma_start", out=dst, in_=vst[s][:]), reads=[("vst", s)],
                       writes=[("v_mla_l" if kind == "mla" else "v_diff_l", ti, tb)], dma=("stv", s))
            return A, B

        for want in ("BDE", "ACFGH"):
            for ti in range(NTT):
                rmsnorm_tile(ti * TT, C_GMIX, xn, sq8, sd0, rstd0, 7, ("mx",))
                if want == "BDE":
                    chains = [mk_B(ti), mk_KR(ti)] + [mk_D(ti, h) for h in range(8)] + [mk_V(ti, tb, "mla") for tb in range(4)]
                else:
                    chains = ([mk_A(ti)] + [mk_FG(ti, OFF_QD, C_GDQ, "q_diff_d", h) for h in range(4)]
                              + [mk_FG(ti, OFF_KD, C_GDK, "kT_diff_l", h) for h in range(4)]
                              + [mk_C(ti, h) for h in range(8)] + [mk_V(ti, tb, "diff") for tb in range(4)])
                run_chains(chains)
                if want == "BDE":
                    issue_gathers(MLA_PAIRS[ti], True)
    sc.barrier()
    L["wst"].close()
    if MIX_STOP <= 1:
        return

    if MIX_STOP <= 1.5:
        return

    def krot(e, sg, c):
        rho = (_pid(e) % 4 + sg) % 4
        k4 = kT_diff_g[c].ap().rearrange("(rk p) t -> rk p t", rk=4)
        return e.dma_start(out=kT_diff_r.ap()[sg * 512 + c * 256:sg * 512 + (c + 1) * 256, :], in_=k4[bass.ds(rho, 1), :, :])

    def vrot(e, sg, c):
        rho = (_pid(e) % 4 + sg) % 4
        v4 = v_diff_g[c].ap().rearrange("(rk t) c -> rk t c", rk=4)
        return e.dma_start(out=v_diff_r.ap()[sg * T + c * 1024:sg * T + (c + 1) * 1024, :], in_=v4[bass.ds(rho, 1), :, :])

    def record_rot():
        for sg in range(4):
            for c in range(2):
                sc.add("sp", lambda e, sg=sg, c=c: krot(e, sg, c), reads=[("gath", "kT_diff", c)], writes=["kT_diff_r"], dma="rot")
                sc.add("sp", lambda e, sg=sg, c=c: vrot(e, sg, c), reads=[("gath", "v_diff", c)], writes=["v_diff_r"], dma="rot")

    if MIX_STOP <= 2:
        record_rot()
        sc.barrier()
        return

    with ExitStack() as st0:
      mixT = st0.enter_context(nc.sbuf_tensor("a_mixT", [128, 8, T], BF16))
      with ExitStack() as st:
        al = lambda n, s, d: st.enter_context(nc.sbuf_tensor("a_" + n, s, d))
        kbuf = [al("kbuf%d" % i, [128, S], BF16) for i in range(2)]
        vbuf = [al("vbuf%d" % i, [128, NKB, 128], BF16) for i in range(2)]
        qbuf = [al("qbuf0", [128, T], BF16)] * 2
        NPT = 3
        pT_ = [al("pT%d" % i, [128, 2 * TT], BF16) for i in range(NPT)]
        tmp = [al("tmp%d" % i, [128, 2 * TT], F32) for i in range(2)]
        tlin = al("tlin", [128, TT], F32)
        tabs = al("tabs", [128, 896], F32)
        ftab = al("ftab", [128, NFT], F32)
        osb = [al("osb%d" % i, [128, TT], F32) for i in range(2)]
        rec = [al("rec%d" % i, [128, TT], F32) for i in range(2)]
        ea = [al("ea%d" % i, [128, TT], F32) for i in range(2)]
        ostage = [al("ost0", [64, TT], BF16)] * 2

        sc.add("sp", I("dma_start", out=tlin[:], in_=tlin_d), writes=["tlin"], dma="atab")
        sc.add("sp", I("dma_start", out=tabs[:], in_=tabs_d), writes=["tabs"], dma="atab")
        sc.add("sp", I("dma_start", out=ftab[:], in_=ftab_d), writes=["ftab"], dma="atab")
        for i in range(2):
            sc.add("dve", I("memset", vbuf[i][:, :, 64:128], 1.0), writes=[("vbuf", i, "ones")])

        def load_mla(h, i):
            for c in range(NTT):
                for r in range(4):
                    j0 = c * 16 + r * 4
                    sc.add("sp", I("dma_start", out=kbuf[i][0:96, j0 * 128:(j0 + 4) * 128],
                                   in_=kT_mla_g[c].ap()[r * 768 + h * 96:r * 768 + (h + 1) * 96, :]),
                           reads=[("gath", "kT_mla", c)], writes=[("kbuf", i, c)], dma=("kb", i, c))
                    vsrc = v_mla_g[c].ap()[r * TT:(r + 1) * TT, h * 64:(h + 1) * 64].rearrange("(kb p) d -> p kb d", p=128)
                    sc.add("sp", I("dma_start", out=vbuf[i][:, j0:j0 + 4, 0:64], in_=vsrc),
                           reads=[("vbuf", i, "ones"), ("gath", "v_mla", c)], writes=[("vbuf", i, c)], dma=("vb", i, c))

        def load_mla_q(h):
            sc.add("sp", I("dma_start", out=qbuf[0][0:96, :], in_=q_mla_d[h * 96:(h + 1) * 96, :]),
                   writes=[("qbuf", 0)], dma=("qb", 0))

        steps = [(h, qt, g) for h in range(8) for qt in range(NTT) for g in range(NKB // 2)]

        def mla_qk(si):
            h, qt, g = steps[si]
            i = h % 2
            sb = (si % 3) * 2
            for j in range(2):
                kb = g * 2 + j
                sc.add("pe", I("matmul", bank(sb + j), lhsT=kbuf[i][0:96, kb * 128:(kb + 1) * 128],
                                                           rhs=qbuf[i][0:96, qt * TT:(qt + 1) * TT], start=True, stop=True),
                       reads=[("kbuf", i, kb // 16), ("qbuf", 0)], writes=[("ps", sb + j)])

        def mla_exp_av(si):
            h, qt, g = steps[si]
            i = h % 2
            sb = (si % 3) * 2
            ps_ = si % NPT
            it = h * NTT + qt
            obk = 6 + (it % 2)
            sc.add("act", I("activation", out=pT_[ps_][:], in_=bank(sb, n=2), func=AF.Exp, scale=MLA_SCALE),
                   reads=[("ps", sb), ("ps", sb + 1)], writes=[("pT", ps_)])
            for j in range(2):
                kb = g * 2 + j
                sc.add("pe", I("matmul", bank(obk), lhsT=vbuf[i][:, kb, :], rhs=pT_[ps_][:, j * TT:(j + 1) * TT],
                                                           start=(kb == 0), stop=(kb == NKB - 1)),
                       reads=[("vbuf", i, kb // 16), ("pT", ps_)], writes=[("ps", obk)])
            if g == NKB // 2 - 1:
                e2 = it % 2
                db = obk
                sc.add("act", I("activation", out=osb[e2][:], in_=bank(obk), func=AF.Copy),
                       reads=[("ps", obk)], writes=[("osb", e2)])
                sc.add("pe", I("matmul", bank(db, 0, 64), lhsT=SHIFT[:, 0:64], rhs=osb[e2][:], start=True, stop=True),
                       reads=[("osb", e2), "emats"], writes=[("ps", db)])
                sc.add("dve", I("reciprocal", out=rec[e2][0:64, :], in_=bank(db, 0, 64)), reads=[("ps", db)], writes=[("rec", e2)])
                sc.add("dve", I("tensor_tensor", out=ostage[e2][:], in0=osb[e2][0:64, :], in1=rec[e2][0:64, :], op=ALU.mult),
                       reads=[("osb", e2), ("rec", e2)], writes=[("ost", 0)])
                p0 = (h % 2) * 64
                sc.add("sp", I("dma_start", out=mixT[p0:p0 + 64, h // 2, qt * TT:(qt + 1) * TT], in_=ostage[e2][:]),
                       reads=[("ost", 0)], writes=[("mixT", h // 2, qt, h % 2)], dma=("ostd", 0))

        PF = 2
        load_mla_q(0)
        load_mla(0, 0)
        for si in range(len(steps) if MIX_STOP >= 3 else 0):
            h, qt, g = steps[si]
            if qt == 0 and g == 0 and h + 1 < 8:
                load_mla(h + 1, (h + 1) % 2)
            if qt == 0 and g == 0 and 1 <= h <= 4:
                nb = (h + 1) % 2
                marker = sc.add("dve", I("memset", misc[:, 7:8], 0.0),
                                reads=[("kbuf", nb, c) for c in range(NTT)] + [("vbuf", nb, c) for c in range(NTT)], writes=["marker"])
                issue_gathers([DIFF_PAIRS[h - 1]], False, extra=[marker])
            if qt == 0 and g == 0 and h == 6:
                record_rot()
            if si == 0:
                for p in range(min(PF, len(steps))):
                    mla_qk(p)
            if si + PF < len(steps):
                if steps[si + PF][0] != steps[si + PF - 1][0]:
                    load_mla_q(steps[si + PF][0])
                mla_qk(si + PF)
            mla_exp_av(si)

        def load_diff(h, i):
            for sg in range(4):
                sc.add("sp", I("dma_start", out=kbuf[i][:, sg * T:(sg + 1) * T],
                               in_=kT_diff_r.ap()[sg * 512 + h * 128:sg * 512 + (h + 1) * 128, :]),
                       reads=["kT_diff_r"], writes=[("kbuf", i)] + [("kbuf", i, c) for c in range(NTT)], dma=("kbd", i))
            vsrc = v_diff_r.ap().rearrange("(kb p) (h d) -> p kb h d", p=128, d=128)
            for q4 in range(4):
                sc.add("sp", I("dma_start", out=vbuf[i][:, q4 * 16:(q4 + 1) * 16, :], in_=vsrc[:, q4 * 16:(q4 + 1) * 16, h, :]),
                       reads=["v_diff_r"], writes=[("vbuf", i)] + [("vbuf", i, c) for c in range(NTT)], dma=("vbd", i))

        def load_diff_q(h):
            sc.add("sp", I("dma_start", out=qbuf[0][:], in_=q_diff_d[h * 128:(h + 1) * 128, :]),
                   writes=[("qbuf", 0)], dma=("qb", 0))

        dsteps = []
        for h in range(4):
            for qt in range(NTT):
                js = [j for j in range(NKB) if _tile_needed(h, qt, j)]
                for n, j in enumerate(js):
                    dsteps.append((h, qt, j, n == 0, n == len(js) - 1))

        def diff_qk(si):
            h, qt, j, first, last = dsteps[si]
            i = h % 2
            for m in range(2):
                sb = (2 * si + m) % 4
                sc.add("pe", I("matmul", bank(sb), lhsT=kbuf[i][m * 64:(m + 1) * 64, j * 128:(j + 1) * 128],
                                                     rhs=qbuf[i][m * 64:(m + 1) * 64, qt * TT:(qt + 1) * TT], start=True, stop=True),
                       reads=[("kbuf", i), ("qbuf", 0)], writes=[("ps", sb)])

        def diff_rest(si):
            h, qt, j, first, last = dsteps[si]
            i = h % 2
            sl = SLOPES[h]
            sg, b = j // 16, j % 16
            if sg == 0 and 4 * qt <= b < 4 * qt + 4:
                off = 384 - 128 * (b - 4 * qt)
                in0, scal, abias, rd = tabs[:, off:off + TT], 8.0 * sl, 0.0, ["tabs"]
            elif sg == 0:
                sgn = 1.0 if b < 4 * qt else -1.0
                in0, scal, abias, rd = tlin[:], -8.0 * sl * sgn, -sl * sgn * float(TT * qt - 128 * b), ["tlin"]
            else:
                c_sig = h * 3 + (sg - 1)
                c_cb = 12 + ((h * 3 + (sg - 1)) * NTT + qt) * 16 + b
                in0, scal, abias, rd = tlin[:], ftab[:, c_sig:c_sig + 1], ftab[:, c_cb:c_cb + 1], ["tlin", "ftab"]
            for m in range(2):
                sb = (2 * si + m) % 4
                sl4 = (2 * si + m) % 4
                tv = tmp[sl4 // 2][:, (sl4 % 2) * TT:(sl4 % 2 + 1) * TT]
                pv = pT_[sl4 // 2][:, (sl4 % 2) * TT:(sl4 % 2 + 1) * TT]
                sc.add("dve", I("scalar_tensor_tensor", out=tv, in0=in0, scalar=scal, in1=bank(sb),
                                op0=ALU.mult, op1=ALU.add),
                       reads=rd + [("ps", sb)], writes=[("tmp", sl4)])
                sc.add("act", I("activation", out=pv, in_=tv, func=AF.Exp, bias=abias, scale=0.125),
                       reads=[("tmp", sl4), "ftab"], writes=[("pTd", sl4)])
            if si + PF < len(dsteps):
                if dsteps[si + PF][0] != dsteps[si + PF - 1][0]:
                    load_diff_q(dsteps[si + PF][0])
                diff_qk(si + PF)
            for m in range(2):
                sl4 = (2 * si + m) % 4
                pv = pT_[sl4 // 2][:, (sl4 % 2) * TT:(sl4 % 2 + 1) * TT]
                sc.add("pe", I("matmul", bank(4 + 2 * m), lhsT=vbuf[i][:, j, :], rhs=pv, start=first, stop=last),
                       reads=[("vbuf", i), ("pTd", sl4)], writes=[("ps", 4 + 2 * m)])
                sc.add("pe", I("matmul", bank(5 + 2 * m), lhsT=ones_bf[:], rhs=pv, start=first, stop=last),
                       reads=["ones_bf", ("pTd", sl4)], writes=[("ps", 5 + 2 * m)])
            if last:
                for m in range(2):
                    sc.add("act", I("activation", out=rec[m][:], in_=bank(5 + 2 * m), func=AF.Ln), reads=[("ps", 5 + 2 * m)], writes=[("rec", m)])
                    sc.add("act", I("activation", out=rec[m][:], in_=rec[m][:], func=AF.Exp, scale=-1.0), reads=[("rec", m)], writes=[("rec", m)])
                    sc.add("dve", I("tensor_tensor", out=ea[m][:], in0=bank(4 + 2 * m), in1=rec[m][:], op=ALU.mult),
                           reads=[("ps", 4 + 2 * m), ("rec", m)], writes=[("ea", m)])
                sc.add("dve", I("scalar_tensor_tensor", out=osb[0][:], in0=ea[1][:], scalar=misc[:, 5:6], in1=ea[0][:],
                                                               op0=ALU.mult, op1=ALU.add),
                       reads=[("ea", 0), ("ea", 1), "neglam"], writes=[("osb", 0)])
                sc.add("act", I("activation", out=osb[1][:], in_=osb[0][:], func=AF.Square), reads=[("osb", 0)], writes=[("osb", 1)])
                sc.add("pe", I("matmul", bank(5), lhsT=ONES32, rhs=osb[1][:], start=True, stop=True),
                       reads=[("osb", 1), "emats"], writes=[("ps", 5)])
                sc.add("act", I("activation", out=rec[0][:], in_=bank(5), func=AF.Ln, bias=eps_t[:], scale=1.0 / 128),
                       reads=[("ps", 5), "eps"], writes=[("rec", 0)])
                sc.add("act", I("activation", out=rec[1][:], in_=rec[0][:], func=AF.Exp, scale=-0.5), reads=[("rec", 0)], writes=[("rec", 1)])
                sc.add("dve", I("scalar_tensor_tensor", out=mixT[:, 4 + h, qt * TT:(qt + 1) * TT], in0=osb[0][:], scalar=misc[:, 6:7],
                                                               in1=rec[1][:], op0=ALU.mult, op1=ALU.mult),
                       reads=[("osb", 0), ("rec", 1), "gsub"], writes=[("mixT", 4 + h, qt, 0), ("mixT", 4 + h, qt, 1)])

        load_diff(0, 0)
        load_diff_q(0)
        for si in range(len(dsteps) if MIX_STOP >= 4 else 0):
            h, qt, j, first, last = dsteps[si]
            if qt == 0 and first and h + 1 < 4:
                load_diff(h + 1, (h + 1) % 2)
            if si == 0:
                for p in range(min(PF, len(dsteps))):
                    diff_qk(p)
            diff_rest(si)

      sc.barrier()
      with ExitStack() as st:
        wout = st.enter_context(nc.sbuf_tensor("a_wout", [128, 8, D], BF16))
        wout_v = w_out.rearrange("(k p) n -> p k n", p=128)
        for k in range(8):
            sc.add("pool", I("dma_start", out=wout[:, k, :], in_=wout_v[:, k, :]), writes=[("wout", k)], dma=("wout", k))
        for ti in range(NTT):
            for o in range(8):
                b = (ti * 8 + o) % 4
                for c in range(8):
                    sc.add("pe", I("matmul", bank(b), lhsT=wout[:, c, o * 128:(o + 1) * 128],
                                                                rhs=mixT[:, c, ti * TT:(ti + 1) * TT], start=(c == 0), stop=(c == 7)),
                           reads=[("wout", c), ("mixT", c, ti, 0), ("mixT", c, ti, 1)], writes=[("ps", b)])
                sc.add("dve", I("tensor_tensor", out=hT[:, o, ti * TT:(ti + 1) * TT], in0=bank(b),
                                                               in1=hT[:, o, ti * TT:(ti + 1) * TT], op=ALU.add),
                       reads=[("ps", b), ("hT", o, ti)], writes=[("hT", o, ti)])
    sc.barrier()


def ple_prefetch(nc, sc, L, st):
    w_pg, w_pp, pT = L["w_pg"], L["w_pp"], L["pT"]
    wpg = st.enter_context(nc.sbuf_tensor("p_wpg", [128, 8, D], BF16))
    wpp = st.enter_context(nc.sbuf_tensor("p_wpp", [128, 2, D], BF16))
    pbf = st.enter_context(nc.sbuf_tensor("p_pbf", [128, 2, T], BF16))
    wpg_v = w_pg.rearrange("(k p) n -> p k n", p=128)
    for k in range(8):
        sc.add("pool", I("dma_start", out=wpg[:, k, :], in_=wpg_v[:, k, :]), writes=[("wpg", k)], dma="wpg")
    wpp_v = w_pp.rearrange("(k p) n -> p k n", p=128)
    pT_v = pT.rearrange("(k p) t -> p k t", p=128)
    for k in range(2):
        sc.add("pool", I("dma_start", out=wpp[:, k, :], in_=wpp_v[:, k, :]), writes=[("wpp", k)], dma="wpp")
        for q4 in range(4):
            sc.add("pool", I("dma_start", out=pbf[:, k, q4 * TT:(q4 + 1) * TT], in_=pT_v[:, k, q4 * TT:(q4 + 1) * TT]),
                   writes=[("pbf", k)], dma="pbf")
    return wpg, wpp, pbf


def ple_phase(nc, sc, L, pw):
    hT, consts, emats, ones_bf, eps_t, PS = (L[k] for k in ("hT", "consts", "emats", "ones_bf", "eps_t", "PS"))
    bank, col, rmsnorm_tile = L["bank"], L["col"], L["rmsnorm_tile"]
    ONES32 = L["ONES32"]
    wpg, wpp, pbf = pw
    with ExitStack() as st:
        al = lambda n, s, d: st.enter_context(nc.sbuf_tensor("p_" + n, s, d))
        xn = [al("xn%d" % i, [128, 8, TT], BF16) for i in range(2)]
        sq8 = al("sq8", [128, 8, TT], BF16)
        sd0 = al("sd0", [128, TT], F32)
        rstd0 = al("rstd0", [128, TT], F32)
        gate = [al("gate0", [128, 8, TT], F32)] * 2
        esb = [al("esb%d" % i, [128, 8, TT], F32) for i in range(2)]
        esq = [al("esq0", [128, 8, TT], F32)] * 2
        sd1 = [al("sd1_%d" % i, [128, TT], F32) for i in range(2)]
        rs1 = [al("rs1_%d" % i, [128, TT], F32) for i in range(2)]
        for ti in range(NTT):
            tok = ti * TT
            z = ti % 2
            for o in range(8):
                b = 3 + (o % 3)
                for k in range(2):
                    sc.add("pe", I("matmul", bank(b), lhsT=wpp[:, k, o * 128:(o + 1) * 128], rhs=pbf[:, k, tok:tok + TT],
                                   start=(k == 0), stop=(k == 1)),
                           reads=[("wpp", k), ("pbf", k)], writes=[("ps", b)])
                sc.add("dve", I("tensor_scalar", out=esb[z][:, o, :], in0=bank(b), scalar1=col(C_GPLEOUT + o), scalar2=None, op0=ALU.mult),
                       reads=[("ps", b), "consts"], writes=[("esb", z, o)])
                sc.add("act", I("activation", out=esq[z][:, o, :], in_=bank(b), func=AF.Square), reads=[("ps", b)], writes=[("esq", 0, o)])
            for o in range(8):
                sc.add("pe", I("matmul", bank(6), lhsT=ONES32, rhs=esq[z][:, o, :], start=(o == 0), stop=(o == 7)),
                       reads=[("esq", 0, o), "emats"], writes=[("ps", 6)])
            sc.add("act", I("activation", out=sd1[z][:], in_=bank(6), func=AF.Ln, bias=eps_t[:], scale=1.0 / D),
                   reads=[("ps", 6), "eps"], writes=[("sd1", z)])
            sc.add("act", I("activation", out=rs1[z][:], in_=sd1[z][:], func=AF.Exp, scale=-0.5), reads=[("sd1", z)], writes=[("rs1", z)])
            rmsnorm_tile(tok, C_GPLEIN, xn[z], sq8, sd0, rstd0, 7, ("pl", z), scr=("pl", "scr"))
            for o in range(8):
                b = o % 3
                for k in range(8):
                    sc.add("pe", I("matmul", bank(b), lhsT=wpg[:, k, o * 128:(o + 1) * 128], rhs=xn[z][:, k, :],
                                   start=(k == 0), stop=(k == 7)),
                           reads=[("wpg", k), (("pl", z), "xn", k)], writes=[("ps", b)])
                sc.add("act", I("activation", out=gate[z][:, o, :], in_=bank(b), func=AF.Sigmoid, bias=col(C_BPLE + o)),
                       reads=[("ps", b), "consts"], writes=[("gate", 0, o)])
                sc.add("pool", I("tensor_tensor", out=esb[z][:, o, :], in0=esb[z][:, o, :], in1=rs1[z][:], op=ALU.mult),
                       reads=[("esb", z, o), ("rs1", z)], writes=[("esb", z, o)])
                sc.add("dve", I("tensor_tensor", out=esb[z][:, o, :], in0=esb[z][:, o, :], in1=gate[z][:, o, :], op=ALU.mult),
                       reads=[("esb", z, o), ("gate", 0, o)], writes=[("esb", z, o)])
                sc.add("dve", I("tensor_tensor", out=hT[:, o, tok:tok + TT], in0=esb[z][:, o, :], in1=hT[:, o, tok:tok + TT], op=ALU.add),
                       reads=[("esb", z, o), ("hT", o, ti)], writes=[("hT", o, ti)])
                sc.add("sp", I("dma_start", out=L["outT"][o * 128:(o + 1) * 128, tok:tok + TT], in_=hT[:, o, tok:tok + TT]),
                       reads=[("hT", o, ti)], dma="out")
    sc.barrier()


def _const_tables(core):
    r = core % 4
    pos = (np.arange(T, dtype=np.float32) + np.float32(r * T)).astype(np.float32)
    inv = (np.float32(10000.0) ** (-np.arange(0, 32, 2, dtype=np.float32) / np.float32(32))).astype(np.float32)
    ang = pos[:, None] * inv[None, :]
    ang = np.concatenate([ang, ang], axis=-1)
    cos = np.cos(ang).astype(np.float32).T
    sin = np.sin(ang).astype(np.float32).T
    sin_signed = sin.copy()
    sin_signed[0:16] *= -1.0
    cs32 = np.stack([cos, sin_signed], axis=1)
    cs96 = np.zeros((96, 2, T), np.float32)
    cs96[0:64, 0, :] = 1.0
    cs96[64:96] = cs32
    ip = np.arange(TT, dtype=np.float32)[None, :]
    jp = np.arange(128, dtype=np.float32)[:, None]
    tlin = (ip - jp).astype(np.float32)
    xx = np.arange(896, dtype=np.float32)[None, :]
    tabs = (-np.abs(xx - 384.0 - jp)).astype(np.float32)
    ftab = np.zeros((128, NFT), np.float32)
    for h in range(4):
        for sg in (1, 2, 3):
            rho = (r + sg) % 4
            sgn = 1.0 if r > rho else -1.0
            ftab[:, h * 3 + (sg - 1)] = -8.0 * SLOPES[h] * sgn
            for qt in range(NTT):
                for b in range(16):
                    x0 = float(T * (r - rho) + TT * qt - 128 * b)
                    ftab[:, 12 + ((h * 3 + (sg - 1)) * NTT + qt) * 16 + b] = -SLOPES[h] * sgn * x0
    return cs96, cs32, tlin, tabs, ftab


def _emats():
    e = np.zeros((128, 4, 128), np.float32)
    e[:, 0, :] = 1.0
    e[0:64, 1, 0:64] = 1.0 / 64
    e[64:96, 1, 64:96] = 1.0 / 32
    e[0:64, 2, 0:64] = 1.0 / 64
    e[64:128, 2, 64:128] = 1.0 / 64
    for i in range(64):
        e[64 + i, 3, i] = 1.0
    return e


def _consts(inp):
    c = np.zeros((128, NCONST), np.float32)

    def chunks(v):
        return np.asarray(v, np.float32).reshape(8, 128).T

    c[:, C_GFFN1:C_GFFN1 + 8] = chunks(inp["g_ffn1"][0])
    c[:, C_GMIX:C_GMIX + 8] = chunks(inp["g_mix"][0])
    c[:, C_GFFN2:C_GFFN2 + 8] = chunks(inp["g_ffn2"][0])
    c[:, C_GPLEIN:C_GPLEIN + 8] = chunks(inp["g_ple_in"][0])
    c[:, C_GPLEOUT:C_GPLEOUT + 8] = chunks(inp["g_ple_out"][0])
    c[:, C_BPLE:C_BPLE + 8] = chunks(inp["b_ple_gate"][0])
    c[:, C_GQLAT:C_GQLAT + 2] = np.asarray(inp["g_q_lat"][0], np.float32).reshape(2, 128).T
    c[:, C_GKVLAT] = np.asarray(inp["g_kv_lat"][0], np.float32)
    gq = np.asarray(inp["g_mla_q"][0], np.float32)
    gk = np.asarray(inp["g_mla_k"][0], np.float32)
    perm = np.concatenate([np.arange(16, 32), np.arange(0, 16)])
    c[0:96, C_GMQ] = gq
    c[64:96, C_GMQROT] = gq[64 + perm]
    c[0:64, C_GMKN] = gk[0:64]
    c[0:32, C_GMKR] = gk[64:96]
    c[0:32, C_GMKRROT] = gk[64 + perm]
    c[:, C_GDQ] = np.tile(np.asarray(inp["g_diff_q"][0], np.float32), 2)
    c[:, C_GDK] = np.tile(np.asarray(inp["g_diff_k"][0], np.float32), 2)
    c[:, C_GSUB] = np.asarray(inp["g_diff_sub"][0], np.float32)
    c[0:64, C_LQ1] = np.asarray(inp["lambda_q1"][0], np.float32)
    c[0:64, C_LK1] = np.asarray(inp["lambda_k1"][0], np.float32)
    c[0:64, C_LQ2] = np.asarray(inp["lambda_q2"][0], np.float32)
    c[0:64, C_LK2] = np.asarray(inp["lambda_k2"][0], np.float32)
    return c


_PROG = {}


def _get_prog(upto=99):
    if upto not in _PROG:
        _PROG[upto] = build_program(upto=upto)
    return _PROG[upto]


def make_in_maps(inp):
    x = np.asarray(inp["x"], np.float32)
    p = np.asarray(inp["p"], np.float32)[0]
    shared = {
        "w1g": np.ascontiguousarray(inp["w_ffn1_gate"][0], np.float32),
        "w1u": np.ascontiguousarray(inp["w_ffn1_up"][0], np.float32),
        "w1d": np.ascontiguousarray(inp["w_ffn1_down"][0], np.float32),
        "w2g": np.ascontiguousarray(inp["w_ffn2_gate"][0], np.float32),
        "w2u": np.ascontiguousarray(inp["w_ffn2_up"][0], np.float32),
        "w2d": np.ascontiguousarray(inp["w_ffn2_down"][0], np.float32),
        "w_in": np.ascontiguousarray(inp["w_in"][0], np.float32),
        "w_qup": np.ascontiguousarray(inp["w_q_up"][0], np.float32),
        "w_kvup": np.ascontiguousarray(inp["w_kv_up"][0], np.float32),
        "w_out": np.ascontiguousarray(inp["w_out"][0], np.float32),
        "w_pg": np.ascontiguousarray(inp["w_ple_gate"][0], np.float32),
        "w_pp": np.ascontiguousarray(inp["w_ple_proj"][0], np.float32),
        "consts": _consts(inp),
        "emats": _emats(),
    }
    maps = []
    for c in range(NCORES):
        b, r = c // 4, c % 4
        cs96, cs32, tlin, tabs, ftab = _const_tables(c)
        m = dict(shared)
        m["xT"] = np.ascontiguousarray(x[b, r * T:(r + 1) * T, :].T)
        m["pT"] = np.ascontiguousarray(p[b, r * T:(r + 1) * T, :].T)
        m["cs96"], m["cs32"], m["tlin"], m["tabs"], m["ftab"] = cs96, cs32, tlin, tabs, ftab
        maps.append(m)
    return maps


def kernel(**inputs):
    nc = _get_prog()
    maps = make_in_maps(inputs)
    res = run_bass_kernel_spmd(nc, maps, core_ids=list(range(NCORES)))
    out = np.empty((2, S, D), np.float32)
    for c in range(NCORES):
        b, r = c // 4, c % 4
        out[b, r * T:(r + 1) * T, :] = np.asarray(res.results[c]["outT"], np.float32).T
    return out
```
